# Optimizing a Trainium2 kernel written in Bass

```python
import jax, jax.numpy as jnp
from jax import lax
import numpy as np

D_MODEL = 2048
BATCH = 4
SEQ = 2048
DEPTH = 2

POOL_WIDTH = D_MODEL // 2
POOL_GROUPS = 4
POOL_GROUP_DIM = POOL_WIDTH // POOL_GROUPS
POOL_WINDOWS = (2, 4, 8, 16)
RWKV_WIDTH = D_MODEL // 2
RWKV_HEAD_DIM = 64
RWKV_HEADS = RWKV_WIDTH // RWKV_HEAD_DIM
DECAY_LORA = 64
AAA_LORA = 64
MV_LORA = 32
RMS_EPS = 1e-6
LNX_EPS = 64e-5
N_SHIFT_COLS = 4 * RWKV_WIDTH + DECAY_LORA + AAA_LORA
N_IN = 2 * POOL_WIDTH + N_SHIFT_COLS + 2 * D_MODEL

kernel_name = "hybrid_pool_rwkv7_adaln_sandwich"


def rmsnorm(x, g):
    xf = x.astype(jnp.float32)
    y = xf * lax.rsqrt(jnp.mean(xf * xf, axis=-1, keepdims=True) + RMS_EPS)
    return (y * g).astype(x.dtype)


def token_shift(z, mu):
    prev = jnp.pad(z, ((0, 0), (1, 0), (0, 0)))[:, :-1]
    return z + (prev - z) * mu


def causal_multiscale_pool(u):
    b, s, _ = u.shape
    ug = u.astype(jnp.float32).reshape(b, s, POOL_GROUPS, POOL_GROUP_DIM)
    cs = jnp.pad(jnp.cumsum(ug, axis=1), ((0, 0), (1, 0), (0, 0), (0, 0)))
    hi = jnp.arange(1, s + 1)
    outs = []
    for g, w in enumerate(POOL_WINDOWS):
        lo = jnp.maximum(hi - w, 0)
        cnt = (hi - lo).astype(jnp.float32)[None, :, None]
        mean = (cs[:, hi, g] - cs[:, lo, g]) / cnt
        outs.append(mean - ug[:, :, g])
    return jnp.stack(outs, axis=2).astype(u.dtype)


def wkv7(r, decay, k, v, a, b):
    bsz, _, h, n = r.shape

    def step(state, inp):
        r_t, w_t, k_t, v_t, a_t, b_t = inp
        sa = jnp.einsum('bhvk,bhk->bhv', state, a_t)
        state = (state * w_t[:, :, None, :] + sa[..., None] * b_t[:, :, None, :]
                 + v_t[..., None] * k_t[:, :, None, :])
        y = jnp.einsum('bhvk,bhk->bhv', state, r_t)
        return state, y

    xs = tuple(jnp.moveaxis(t.astype(jnp.float32), 1, 0) for t in (r, decay, k, v, a, b))
    s0 = jnp.zeros((bsz, h, n, n), jnp.float32)
    _, y = lax.scan(step, s0, xs)
    return jnp.moveaxis(y, 0, 1)


def setup_inputs(seed: int = 0) -> dict:
    key = jax.random.key(seed)
    ks = jax.random.split(key, 32)
    f32 = jnp.float32
    nrm = lambda k, shape, s: jax.random.normal(k, shape, f32) * s
    L, Lm = DEPTH, DEPTH - 1
    H, N = RWKV_HEADS, RWKV_HEAD_DIM
    return {
        "x": nrm(ks[0], (BATCH, SEQ, D_MODEL), 1.0),
        "c": nrm(ks[1], (BATCH, D_MODEL), 1.0),
        "w_ada": nrm(ks[2], (L, D_MODEL, 3 * D_MODEL), D_MODEL ** -0.5),
        "b_ada": nrm(ks[3], (L, 3 * D_MODEL), 0.02),
        "g_pre": 1.0 + nrm(ks[4], (L, D_MODEL), 0.05),
        "w_in": nrm(ks[5], (L, D_MODEL, N_IN), D_MODEL ** -0.5),
        "w_pool": nrm(ks[6], (L, POOL_GROUPS, POOL_GROUP_DIM, POOL_GROUP_DIM), POOL_GROUP_DIM ** -0.5),
        "pool_scale": jax.random.uniform(ks[7], (L, POOL_WIDTH), f32, 0.5, 1.5),
        "mu_shift": jax.random.uniform(ks[8], (L, N_SHIFT_COLS), f32, 0.0, 1.0),
        "w_decay_up": nrm(ks[9], (L, DECAY_LORA, RWKV_WIDTH), DECAY_LORA ** -0.5),
        "w0": jax.random.uniform(ks[10], (L, RWKV_WIDTH), f32, -3.0, 1.0),
        "w_aaa_up": nrm(ks[11], (L, AAA_LORA, RWKV_WIDTH), AAA_LORA ** -0.5),
        "a0": nrm(ks[12], (L, RWKV_WIDTH), 0.5),
        "w_mv_down": nrm(ks[13], (Lm, D_MODEL, MV_LORA), D_MODEL ** -0.5),
        "mu_mv": jax.random.uniform(ks[14], (Lm, MV_LORA), f32, 0.0, 1.0),
        "w_mv_up": nrm(ks[15], (Lm, MV_LORA, RWKV_WIDTH), MV_LORA ** -0.5),
        "mv0": nrm(ks[16], (Lm, RWKV_WIDTH), 0.5),
        "k_k": 0.85 + nrm(ks[17], (L, RWKV_WIDTH), 0.05),
        "k_a": 1.0 + nrm(ks[18], (L, RWKV_WIDTH), 0.05),
        "r_k": nrm(ks[19], (L, H, N), 0.1),
        "lnx_g": 1.0 + nrm(ks[20], (L, RWKV_WIDTH), 0.05),
        "lnx_b": nrm(ks[21], (L, RWKV_WIDTH), 0.02),
        "w_br_a": nrm(ks[22], (L, POOL_WIDTH, D_MODEL), POOL_WIDTH ** -0.5),
        "w_br_b": nrm(ks[23], (L, RWKV_WIDTH, D_MODEL), RWKV_WIDTH ** -0.5),
        "w_out": nrm(ks[24], (L, D_MODEL, D_MODEL), D_MODEL ** -0.5),
        "g_post": 1.0 + nrm(ks[25], (L, D_MODEL), 0.05),
    }


def reference(x, c, w_ada, b_ada, g_pre, w_in, w_pool, pool_scale, mu_shift,
              w_decay_up, w0, w_aaa_up, a0, w_mv_down, mu_mv, w_mv_up, mv0,
              k_k, k_a, r_k, lnx_g, lnx_b, w_br_a, w_br_b, w_out, g_post):
    bsz, s, _ = x.shape
    H, N = RWKV_HEADS, RWKV_HEAD_DIM
    cond = jax.nn.silu(c)
    split_in = [POOL_WIDTH, 2 * POOL_WIDTH, 2 * POOL_WIDTH + N_SHIFT_COLS,
                2 * POOL_WIDTH + N_SHIFT_COLS + D_MODEL]
    split_shift = [RWKV_WIDTH, 2 * RWKV_WIDTH, 3 * RWKV_WIDTH, 4 * RWKV_WIDTH,
                   4 * RWKV_WIDTH + DECAY_LORA]
    v_first = None
    for l in range(DEPTH):
        mod = cond @ w_ada[l] + b_ada[l]
        shift, scale, gate = jnp.split(mod, 3, axis=-1)
        h = rmsnorm(x, g_pre[l]) * (1.0 + scale[:, None]) + shift[:, None]

        proj = h @ w_in[l]
        u_a, z_a, shifted, gate_a, gate_b = jnp.split(proj, split_in, axis=-1)

        pooled = causal_multiscale_pool(u_a)
        y_a = jnp.einsum('bsgc,gcd->bsgd', pooled, w_pool[l]).reshape(bsz, s, POOL_WIDTH)
        y_a = y_a * pool_scale[l] * jax.nn.silu(z_a)

        xs = token_shift(shifted, mu_shift[l])
        r, k, v, z_b, w_lo, a_lo = jnp.split(xs, split_shift, axis=-1)
        w_log = -jax.nn.softplus(-(w0[l] + jnp.tanh(w_lo.astype(jnp.float32)) @ w_decay_up[l])) - 0.5
        decay = jnp.exp(-jnp.exp(w_log))
        a = jax.nn.sigmoid(a0[l] + a_lo @ w_aaa_up[l])
        if l == 0:
            v_first = v
        else:
            mv = token_shift(h @ w_mv_down[l - 1], mu_mv[l - 1])
            v = v + (v_first - v) * jax.nn.sigmoid(mv0[l - 1] + mv @ w_mv_up[l - 1])
        kk = (k * k_k[l]).reshape(bsz, s, H, N).astype(jnp.float32)
        kk = kk / jnp.maximum(jnp.linalg.norm(kk, axis=-1, keepdims=True), 1e-12)
        k = k * (1.0 + (a - 1.0) * k_a[l])
        rh, kh, vh, ah = (t.reshape(bsz, s, H, N) for t in (r, k, v, a))
        dh = decay.reshape(bsz, s, H, N)
        o = wkv7(rh, dh, kh, vh, -kk, kk * ah)
        mu_o = jnp.mean(o, axis=-1, keepdims=True)
        var_o = jnp.mean(jnp.square(o - mu_o), axis=-1, keepdims=True)
        o = ((o - mu_o) * lax.rsqrt(var_o + LNX_EPS)).reshape(bsz, s, RWKV_WIDTH)
        o = (o * lnx_g[l] + lnx_b[l]).astype(v.dtype)
        bonus = jnp.sum(rh * kh * r_k[l], axis=-1, keepdims=True) * vh
        y_b = (o + bonus.reshape(bsz, s, RWKV_WIDTH)) * jax.nn.silu(z_b)

        merged = (jax.nn.sigmoid(gate_a) * (y_a @ w_br_a[l])
                  + jax.nn.sigmoid(gate_b) * (y_b @ w_br_b[l]))
        out = merged @ w_out[l]

        x = x + gate[:, None] * rmsnorm(out, g_post[l])
    return x
```

```python
import numpy as np
import concourse.bass as bass
import concourse.mybir as mybir
from concourse.bass_utils import run_bass_kernel_spmd

F32 = mybir.dt.float32
BF16 = mybir.dt.bfloat16
AF = mybir.ActivationFunctionType
ALU = mybir.AluOpType

ENGS = ("pe", "act", "dve", "pool", "sp")
EPOCH = 30000

D = 2048
NT = 2048
NKT = 16
NIN = 10368
NPROJ = 6272
C0 = float(np.exp(-0.5))
RMS_EPS = 1e-6
LNX_EPS = 64e-5

CG_PRE, CPSC, CMU_R, CMU_K, CMU_V, CMU_Z, CMU_WA = 0, 16, 24, 32, 40, 48, 56
CW0, CA0, CMV0, CKK, CKA, CRK, CLG, CLB, CMU_MV, CMU_A = 57, 65, 73, 81, 89, 97, 105, 113, 121, 122
NCOLS = 123


class Prog:
    def __init__(self, nc):
        self.nc = nc
        self.q = {e: [] for e in ENGS}
        self.cnt = {e: 0 for e in ENGS}
        self.seen = {e: {} for e in ENGS}
        self.vc = {}
        self.last_w = {}
        self.readers = {}
        self.sems = {}
        self.dma_cnt = {}
        self.waited = {e: set() for e in ENGS}
        self._stack = []
        self._defer = None

    def _sem(self, key):
        if key not in self.sems:
            cm = self.nc.semaphore("s_%s_%s" % (key[0], key[1]))
            h = cm.__enter__()
            self._stack.append(cm)
            self.sems[key] = h
        return self.sems[key]

    def _note(self, waits):
        for (k_, v) in waits:
            if k_ in self.waited:
                self.waited[k_].add(v)

    def _deps(self, eng, reads, writes, skipkey=None):
        toks = []
        for r in reads:
            t = self.last_w.get(r)
            if t is not None:
                toks.append(t)
        for w in writes:
            t = self.last_w.get(w)
            if t is not None:
                toks.append(t)
            toks.extend(self.readers.get(w, ()))
        need = {}
        for (k, v) in toks:
            if eng == "pe" and k == "pe":
                continue
            if skipkey is not None and k == skipkey:
                continue
            if self.seen[eng].get(k, 0) >= v:
                continue
            if need.get(k, 0) < v:
                need[k] = v
        waits = []
        items = sorted(need.items(), key=lambda kv: -len(self.vc.get((kv[0], kv[1]), ())))
        for k, v in items:
            if self.seen[eng].get(k, 0) >= v:
                continue
            waits.append((k, v))
            self.seen[eng][k] = v
            for k2, v2 in self.vc.get((k, v), {}).items():
                if self.seen[eng].get(k2, 0) < v2:
                    self.seen[eng][k2] = v2
        self._note(waits)
        return waits

    def _finish(self, tok, eng, reads, writes):
        c = dict(self.seen[eng])
        k, v = tok
        c[k] = v
        self.vc[tok] = c
        for r in reads:
            self.readers.setdefault(r, []).append(tok)
        for w in writes:
            self.last_w[w] = tok
            self.readers[w] = []

    def op(self, eng, fn, reads=(), writes=(), cost=0.3):
        if self._defer is not None:
            self._defer.append(("op", eng, fn, tuple(reads), tuple(writes), cost))
            return None
        ps_r = [r for r in reads if r[:2] in ("pb", "pT")]
        if ps_r:
            reads = [r for r in reads if r[:2] not in ("pb", "pT")]
            writes = list(writes) + ps_r
        waits = self._deps(eng, reads, writes)
        self.cnt[eng] += 1
        n = self.cnt[eng]
        tok = (eng, n)
        self.q[eng].append((waits, fn, eng, n))
        self._finish(tok, eng, reads, writes)
        return tok

    def dma(self, eng, out, in_, semname, reads=(), writes=()):
        if self._defer is not None:
            self._defer.append(("dma", eng, (out, in_, semname), tuple(reads), tuple(writes), 2.0))
            return None
        key = ("dma", semname)
        waits = self._deps(eng, reads, writes, skipkey=key)
        self.dma_cnt[key] = self.dma_cnt.get(key, 0) + 16
        tok = (key, self.dma_cnt[key])

        def fn(e, out=out, in_=in_):
            return e.dma_start(out=out, in_=in_)
        self.q[eng].append((waits, fn, key, 16))
        self._finish(tok, eng, reads, writes)
        return tok

    def merge_streams(self, lists, hop=0.05):
        eng_free = {e: 0.0 for e in ENGS}
        kw, kr = {}, {}
        idx = [0] * len(lists)

        def est(rec):
            kind, eng, _, reads, writes, cost = rec
            rd = [r for r in reads if r[:2] not in ("pb", "pT")]
            wr = list(writes) + [r for r in reads if r[:2] in ("pb", "pT")]
            t = eng_free[eng]
            for r in rd:
                t = max(t, kw.get(r, 0.0) + hop)
            for w in wr:
                t = max(t, kw.get(w, 0.0) + hop, kr.get(w, 0.0) + hop)
            return t, rd, wr

        while True:
            best, bt = None, None
            for i, lst in enumerate(lists):
                if idx[i] >= len(lst):
                    continue
                t, _, _ = est(lst[idx[i]])
                if best is None or t < bt - 1e-9:
                    best, bt = i, t
            if best is None:
                break
            rec = lists[best][idx[best]]
            idx[best] += 1
            kind, eng, payload, reads, writes, cost = rec
            t, rd, wr = est(rec)
            fin = t + cost
            eng_free[eng] = t + (0.06 if kind == "dma" else cost)
            for r in rd:
                kr[r] = max(kr.get(r, 0.0), fin)
            for w in wr:
                kw[w] = fin
                kr[w] = 0.0
            if kind == "op":
                self.op(eng, payload, reads, writes)
            else:
                self.dma(eng, payload[0], payload[1], payload[2], reads, writes)
        return max(eng_free.values())

    def dma_like(self, eng, fn, semname, reads=(), writes=()):
        key = ("dma", semname)
        waits = self._deps(eng, reads, writes, skipkey=key)
        self.dma_cnt[key] = self.dma_cnt.get(key, 0) + 16
        tok = (key, self.dma_cnt[key])
        self.q[eng].append((waits, fn, key, 16))
        self._finish(tok, eng, reads, writes)
        return tok

    def barrier(self):
        toks = []
        for e in ENGS:
            if self.cnt[e] > 0:
                toks.append((e, self.cnt[e]))
        for key, val in self.dma_cnt.items():
            toks.append((key, val))
        for e in ENGS:
            waits = []
            for (k_, v) in toks:
                if e == "pe" and k_ == "pe":
                    continue
                if self.seen[e].get(k_, 0) >= v:
                    continue
                waits.append((k_, v))
                self.seen[e][k_] = v
            if waits:
                self._note(waits)
                self.q[e].append((waits, None, None, 0))

    def emit(self):
        rank = {}
        for e in ENGS:
            rank[e] = {idx: i + 1 for i, idx in enumerate(sorted(self.waited[e]))}

        def semval(k_, v):
            if k_ in rank:
                r = rank[k_][v]
                return self._sem((k_, (r - 1) // EPOCH)), (r - 1) % EPOCH + 1
            return self._sem(k_), v
        prog = self
        self.n_inc = {e: len(rank[e]) for e in ENGS}

        def run(engobj, lst):
            for (waits, fn, key, idx) in lst:
                for (k_, v) in waits:
                    sm, val = semval(k_, v)
                    engobj.wait_ge(sm, val)
                if fn is None:
                    continue
                if key in rank:
                    ins = fn(engobj)
                    if idx in rank[key]:
                        sm, _ = semval(key, idx)
                        ins.then_inc(sm, 1)
                else:
                    fn(engobj).then_inc(prog._sem(key), 16)

        with self.nc.Block() as block:
            @block.tensor
            def _(e):
                run(e, prog.q["pe"])

            @block.scalar
            def _(e):
                run(e, prog.q["act"])

            @block.vector
            def _(e):
                run(e, prog.q["dve"])

            @block.gpsimd
            def _(e):
                run(e, prog.q["pool"])

            @block.sync
            def _(e):
                run(e, prog.q["sp"])


DEBUG_OUT = False
STOP = None


class StopBuild(Exception):
    pass


class Arena:
    def __init__(self, ap):
        self.ap = ap
        self.cap = ap.shape[1]
        self.off = 0

    def reset(self):
        self.off = 0

    def take(self, shape, dt):
        n = 1
        for d_ in shape[1:]:
            n *= d_
        units = n * (2 if dt == F32 else 1)
        units = (units + 15) // 16 * 16
        assert self.off + units <= self.cap, ("arena overflow", self.off, units, self.cap)
        v = self.ap[0:shape[0], self.off:self.off + units]
        self.off += units
        if dt == F32:
            v = v.bitcast(F32)
        v = v[:, 0:n]
        if len(shape) == 3:
            v = v.rearrange("p (a b) -> p a b", b=shape[2])
        return v


class K:
    def __init__(self, nc):
        self.nc = nc
        self.P = Prog(nc)

    def sb(self, name, shape, dt):
        return self.nc.alloc_sbuf_tensor(name, list(shape), dt).ap()

    def ps(self, name, shape, dt):
        return self.nc.alloc_psum_tensor(name, list(shape), dt).ap()

    @staticmethod
    def _n(ap):
        n = 1
        for d_ in ap.shape[1:]:
            n *= d_
        return n

    def _cd(self, out, eng="dve", mult=1.0):
        n = self._n(out)
        if eng == "act":
            return (224.0 + n) / 1200.0
        c = (64.0 + n) / 960.0 * mult
        return c * (2.0 if eng == "pool" else 1.0)

    def mm(self, out, lhsT, rhs, start, stop, r, w):
        n = max(64.0, self._n(out))
        if lhsT.dtype == F32:
            cost = (4.0 * n + 4.0 * lhsT.shape[1]) / 2400.0
        else:
            cost = n / 2400.0 + lhsT.shape[1] / 2400.0 + 0.01
        self.P.op("pe", lambda e: e.matmul(out, lhsT, rhs, start=start, stop=stop), r, w, cost=cost)

    def tr(self, out, in_, ident, r, w):
        self.P.op("pe", lambda e: e.transpose(out, in_, ident), r, w, cost=0.07)

    def act(self, out, in_, func, r, w, bias=None, scale=1.0, accum=None):
        def fn(e):
            kw = {}
            if bias is not None:
                kw["bias"] = bias
            if accum is not None:
                kw["accum_out"] = accum
            return e.activation(out=out, in_=in_, func=func, scale=scale, **kw)
        self.P.op("act", fn, r, w, cost=self._cd(out, "act"))

    def tt(self, out, a, b, op, r, w, eng="dve"):
        self.P.op(eng, lambda e: e.tensor_tensor(out=out, in0=a, in1=b, op=op), r, w, cost=self._cd(out, eng))

    def ts1(self, out, a, s, op, r, w, eng="dve"):
        self.P.op(eng, lambda e: e.tensor_single_scalar(out=out, in_=a, scalar=s, op=op), r, w, cost=self._cd(out, eng))

    def ts2(self, out, a, s1, s2, op0, op1, r, w, eng="dve"):
        self.P.op(eng, lambda e: e.tensor_scalar(out=out, in0=a, scalar1=s1, scalar2=s2, op0=op0, op1=op1), r, w,
                  cost=self._cd(out, eng))

    def stt(self, out, a, s, b, op0, op1, r, w, eng="dve"):
        self.P.op(eng, lambda e: e.scalar_tensor_tensor(out=out, in0=a, scalar=s, in1=b, op0=op0, op1=op1), r, w,
                  cost=self._cd(out, eng))

    def cp(self, out, in_, r, w, eng="dve"):
        if eng == "act":
            self.act(out, in_, AF.Copy, r, w)
        else:
            self.P.op(eng, lambda e: e.tensor_copy(out=out, in_=in_), r, w, cost=self._cd(out, eng))

    def recip(self, out, in_, r, w):
        self.P.op("dve", lambda e: e.reciprocal(out=out, in_=in_), r, w, cost=self._cd(out, "dve", 6.0))

    def memset(self, ap, val, w, eng="dve"):
        self.P.op(eng, lambda e: e.memset(ap, val), (), w, cost=self._cd(ap, eng, 0.5))

    def scan(self, out, d0, d1, r, w):
        self.P.op("dve", lambda e: e.tensor_tensor_scan(out=out, data0=d0, data1=d1, initial=0.0,
                                                        op0=ALU.mult, op1=ALU.add), r, w, cost=self._cd(out, "dve", 2.0))

    def asel(self, out, in_, pattern, cmp_op, fill, cm, r, w):
        self.P.op("pool", lambda e: e.affine_select(out=out, in_=in_, pattern=pattern, compare_op=cmp_op,
                                                     fill=fill, base=0, channel_multiplier=cm), r, w)

    def dma(self, eng, out, in_, sem, r=(), w=()):
        self.P.dma(eng, out, in_, sem, r, w)


def build_program():
    nc = bass.Bass("TRN2", target_bir_lowering=False)
    k = K(nc)
    P = k.P

    def din(name, shape, dt=F32):
        return nc.dram_tensor(name, list(shape), dt, kind="ExternalInput").ap()

    x_in = din("x", [NT, D])
    cT_in = din("cT", [128, NKT])
    w_ada = din("w_ada", [2, D, 3 * D])
    b_ada = din("b_ada", [2, 3 * D])
    w_in = din("w_in", [2, D, NIN])
    w_pool = din("w_pool", [2, 4, 256, 256])
    w_du = din("w_decay_up", [2, 64, 1024])
    w_au = din("w_aaa_up", [2, 64, 1024])
    w_mvd = din("w_mv_down", [1, D, 32])
    w_mvu = din("w_mv_up", [1, 32, 1024])
    w_bra = din("w_br_a", [2, 1024, D])
    w_brb = din("w_br_b", [2, 1024, D])
    w_out = din("w_out", [2, D, D])
    g_post = din("g_post", [2, D])
    cols_in = din("cols", [2, 128, NCOLS])
    y_out = nc.dram_tensor("y", [NT, D], F32, kind="ExternalOutput").ap()

    skind = "ExternalOutput" if DEBUG_OUT else "Internal"
    projT = nc.dram_tensor("projT", [NPROJ, NT], F32, kind=skind).ap()
    sgT = nc.dram_tensor("sgT", [2 * D, NT], BF16, kind=skind).ap()
    vfT = nc.dram_tensor("vfT", [1024, NT], F32, kind=skind).ap()
    xmid = nc.dram_tensor("xmid", [NT, D], F32, kind=skind).ap()
    ggd = nc.dram_tensor("ggd", [1, D], F32, kind="Internal").ap()
    lorad = nc.dram_tensor("lorad", [2, 1024, NT], F32, kind="Internal").ap()

    if DEBUG_OUT:
        dbg_big = nc.dram_tensor("dbg_big", [2, 128, 16, NT], BF16, kind="ExternalOutput").ap()
    big = k.sb("big", [128, 16, NT], BF16)
    hT = big
    yaT = big[:, 0:8, :]
    ybT = big[:, 8:16, :]
    arA = k.sb("arenaA", [128, 32768], BF16)
    arB = k.sb("arenaB", [128, 30208], BF16)
    A = Arena(arA)
    B = Arena(arB)
    cols = [k.sb("cols%d" % l, [128, NCOLS], F32) for l in range(2)]
    small = k.sb("small", [128, 64], F32)
    shc = k.sb("shc", [128, 16], F32)
    gsc = k.sb("gsc", [128, 16], F32)
    omk = k.sb("omk", [128, 8], F32)
    cT = k.sb("cTs", [128, NKT], F32)
    condT = k.sb("condT", [128, NKT], BF16)
    onesrow = k.sb("onesrow", [1, 128], F32)
    ident = k.sb("ident", [128, 128], BF16)
    identx = k.sb("identx", [64, 512], F32)
    m_su = k.sb("m_su", [64, 512], BF16)
    m_u = k.sb("m_u", [64, 512], BF16)
    m_sl = k.sb("m_sl", [64, 512], BF16)
    bd1 = k.sb("bd1", [128, 128], BF16)
    bdm = k.sb("bdm", [128, 128], BF16)
    smask = k.sb("smask", [128, 512], F32)
    epsr = k.sb("epsr", [128, 1], F32)
    epsl = k.sb("epsl", [128, 1], F32)
    mvs = k.sb("mvs", [32, NT], BF16)
    negc = k.sb("negc", [128, 24], F32)
    onec = k.sb("onec", [128, 1], F32)
    cnt16 = k.sb("cnt16", [128, 16], F32)
    one16 = k.sb("one16", [128, 16], F32)
    Hf = [k.sb("Hf%d" % p, [128, 128], F32) for p in range(8)]
    Hb = [k.sb("Hb%d" % p, [128, 128], BF16) for p in range(8)]

    pb = [k.ps("pb%d" % i, [128, 512], F32) for i in range(6)]
    pT = [k.ps("pT%d" % i, [128, 1024], BF16) for i in range(2)]

    identf = A.take([128, 128], F32)
    identxf = A.take([64, 512], F32)
    mtmp = A.take([64, 512], F32)
    k.memset(identf, 0.0, ["identf"])
    k.asel(identf, identf, [[-1, 128]], ALU.not_equal, 1.0, 1, ["identf"], ["identf"])
    k.cp(ident, identf, ["identf"], ["ident"])
    ix3 = identxf.rearrange("p (u j) -> p u j", j=64)
    k.memset(identxf, 0.0, ["identxf"])
    k.asel(ix3, ix3, [[0, 8], [-1, 64]], ALU.not_equal, 1.0, 1, ["identxf"], ["identxf"])
    k.cp(identx, identxf, ["identxf"], ["identx"])
    for (m, mk_, cmpop, cm, coef) in ((m_su, "m_su", ALU.is_gt, -1, 1), (m_u, "m_u", ALU.is_ge, -1, 1),
                                      (m_sl, "m_sl", ALU.is_gt, 1, -1)):
        m3 = mtmp.rearrange("p (u j) -> p u j", j=64)
        k.memset(mtmp, 1.0, ["mtmp"])
        k.asel(m3, m3, [[0, 8], [coef, 64]], cmpop, 0.0, cm, ["mtmp"], ["mtmp"])
        k.cp(m, mtmp, ["mtmp"], [mk_])
    k.memset(bd1, 0.0, ["bd1"])
    k.memset(bd1[0:64, 0:64], 1.0, ["bd1"])
    k.memset(bd1[64:128, 64:128], 1.0, ["bd1"])
    k.memset(bdm, 0.0, ["bdm"])
    k.memset(bdm[0:64, 0:64], 1.0 / 64, ["bdm"])
    k.memset(bdm[64:128, 64:128], 1.0 / 64, ["bdm"])
    k.memset(smask, 1.0, ["smask"])
    k.memset(smask.rearrange("p (c j) -> p c j", j=64)[:, :, 0:1], 0.0, ["smask"])
    k.memset(epsr, RMS_EPS, ["epsr"])
    k.memset(epsl, LNX_EPS, ["epsl"])
    k.memset(onesrow, 1.0, ["onesrow"])
    k.memset(small, 0.0, ["small"])
    k.memset(one16, 1.0, ["one16"])
    k.memset(onec, 1.0, ["onec"])
    k.scan(cnt16, one16, one16, ["one16"], ["cnt16"])

    k.dma("sp", cT, cT_in, "l_c", w=["cT"])
    for l in range(2):
        k.dma("sp", cols[l], cols_in[l], "l_cols%d" % l, w=["cols%d" % l])
    k.act(condT, cT, AF.Silu, ["cT"], ["condT"])
    P.barrier()

    evac_rr = [0]
    stage_rr = [0]
    wgrp = [None, None]

    def col(l, c):
        return cols[l][:, c:c + 1]

    def load_wgrp(l, slot, src, c0, wd):
        v = src.rearrange("(kt p) n -> p kt n", p=128)
        for q4 in range(4):
            k.dma("pool", wgrp[slot][:, q4 * 4:(q4 + 1) * 4, 0:wd], v[:, q4 * 4:(q4 + 1) * 4, c0:c0 + wd],
                  "l_wg%d" % slot, w=["wgrp%d" % slot])

    def stage_end(l, tag):
        if STOP == "%d:%s" % (l, tag):
            P.barrier()
            raise StopBuild()

    def build_layer(l):
        xsrc = x_in if l == 0 else xmid
        xdst = xmid if l == 0 else y_out
        ck = "cols%d" % l
        A.reset()
        B.reset()
        modrow = A.take([1, 3 * D], F32)
        gprow = A.take([1, D], F32)
        ggrow = A.take([1, D], F32)
        brow = A.take([1, 512], F32)
        wgrp[0] = B.take([128, 16, 512], BF16)
        wgrp[1] = B.take([128, 16, 512], BF16)
        k.dma("sp", gprow, g_post[l:l + 1, :], "l_gp", w=["gprow"])
        for cg in range(12):
            slot = cg % 2
            load_wgrp(l, slot, w_ada[l], cg * 512, 512)
            k.dma("sp", brow, b_ada[l:l + 1, cg * 512:(cg + 1) * 512], "l_bada", w=["brow"])
            for kt in range(NKT):
                k.mm(pb[0][0:1, :], condT[:, kt:kt + 1], wgrp[slot][:, kt, :], kt == 0, kt == NKT - 1,
                     ["condT", "wgrp%d" % slot], ["pb0"])
            k.tt(modrow[:, cg * 512:(cg + 1) * 512], pb[0][0:1, :], brow, ALU.add,
                 ["pb0", "brow"], ["modrow"])
        for i in range(32):
            k.mm(pb[1][:, i:i + 1], modrow[0:1, i * 128:(i + 1) * 128], onesrow[0:1, 0:1], True, True,
                 ["modrow", "onesrow"], ["pb1"])
        k.cp(shc, pb[1][:, 0:16], ["pb1"], ["shc"])
        k.stt(gsc, pb[1][:, 16:32], 1.0, cols[l][:, CG_PRE:CG_PRE + 16], ALU.add, ALU.mult, ["pb1", ck], ["gsc"])
        k.tt(ggrow, modrow[:, 2 * D:3 * D], gprow, ALU.mult, ["modrow", "gprow"], ["ggrow"])
        k.dma("sp", ggd, ggrow, "s_ggd", r=["ggrow"], w=["ggd"])
        k.ts2(omk, cols[l][:, CKA:CKA + 8], -1.0, 1.0, ALU.mult, ALU.add, [ck], ["omk"])
        P.barrier()
        A.reset()
        B.reset()
        xt = [A.take([128, D], F32) for _ in range(2)]
        junk = A.take([128, D], F32)
        xnb = [A.take([128, D], BF16) for _ in range(4)]
        for tg in range(4):
            for j in range(4):
                tt_ = tg * 4 + j
                xs = xt[tt_ % 2]
                k.dma("sp", xs, xsrc[tt_ * 128:(tt_ + 1) * 128, :], "l_xt%d" % (tt_ % 2), w=["xt%d" % (tt_ % 2)])
                k.act(junk, xs, AF.Square, ["xt%d" % (tt_ % 2)], ["junk", "small"], accum=small[:, tt_:tt_ + 1])
                k.act(small[:, 16 + tt_:17 + tt_], small[:, tt_:tt_ + 1], AF.Sqrt, ["small", "epsr"], ["small"],
                      bias=epsr, scale=1.0 / D)
                k.recip(small[:, 32 + tt_:33 + tt_], small[:, 16 + tt_:17 + tt_], ["small"], ["small"])
                k.ts1(xnb[j], xs, small[:, 32 + tt_:33 + tt_], ALU.mult, ["xt%d" % (tt_ % 2), "small"], ["xnb%d" % j])
            for ft in range(NKT):
                pt = pT[ft % 2]
                for j in range(4):
                    k.tr(pt[:, j * 128:(j + 1) * 128], xnb[j][:, ft * 128:(ft + 1) * 128], ident,
                         ["xnb%d" % j, "ident"], ["pT%d" % (ft % 2)])
                k.act(hT[:, ft, tg * 512:(tg + 1) * 512], pt[:, 0:512], AF.Identity, ["pT%d" % (ft % 2), "gsc", "shc"],
                      ["hT"], bias=shc[:, ft:ft + 1], scale=gsc[:, ft:ft + 1])
        k.memset(small[:, 0:16], 0.0, ["small"])

        stage_end(l, "S1")
        P.barrier()
        A.reset()
        B.reset()
        stage = [A.take([128, 512], F32) for _ in range(4)]
        stageb = [A.take([128, 512], BF16) for _ in range(4)]
        mvraw = A.take([32, 1 + NT], F32)
        f_dm = A.take([32, 512], F32)
        wgrp[0] = B.take([128, 16, 512], BF16)
        wgrp[1] = B.take([128, 16, 512], BF16)
        wmvd = B.take([128, 16, 32], BF16)
        if l == 1:
            k.memset(mvraw[:, 0:1], 0.0, ["mvraw"])
            k.dma("pool", wmvd, w_mvd[0].rearrange("(kt p) n -> p kt n", p=128), "l_wmvd", w=["wmvd"])
            for tb in range(4):
                for kt in range(NKT):
                    k.mm(pb[4][0:32, :], wmvd[:, kt, :], hT[:, kt, tb * 512:(tb + 1) * 512], kt == 0, kt == NKT - 1,
                         ["wmvd", "hT"], ["pb4"])
                k.cp(mvraw[:, 1 + tb * 512:1 + (tb + 1) * 512], pb[4][0:32, :], ["pb4"], ["mvraw"])
            for tb in range(4):
                k.tt(f_dm, mvraw[:, tb * 512:tb * 512 + 512], mvraw[:, 1 + tb * 512:1 + tb * 512 + 512], ALU.subtract,
                     ["mvraw"], ["f_dm"])
                k.stt(mvs[:, tb * 512:(tb + 1) * 512], f_dm, cols[l][0:32, CMU_MV:CMU_MV + 1],
                      mvraw[:, 1 + tb * 512:1 + tb * 512 + 512], ALU.mult, ALU.add, ["f_dm", ck, "mvraw"], ["mvs"])
        groups = [(g * 512, 512) for g in range(20)] + [(10240, 128)]
        for gi, (c0, wd) in enumerate(groups):
            slot = gi % 2
            load_wgrp(l, slot, w_in[l], c0, wd)
            for cb in range(wd // 128):
                cc = c0 + cb * 128
                for tb in range(4):
                    bi = evac_rr[0] % 4
                    evac_rr[0] += 1
                    bank = pb[bi]
                    for kt in range(NKT):
                        k.mm(bank, wgrp[slot][:, kt, cb * 128:(cb + 1) * 128], hT[:, kt, tb * 512:(tb + 1) * 512],
                             kt == 0, kt == NKT - 1, ["wgrp%d" % slot, "hT"], ["pb%d" % bi])
                    si = stage_rr[0] % 4
                    stage_rr[0] += 1
                    if cc >= NPROJ:
                        k.act(stageb[si], bank, AF.Sigmoid, ["pb%d" % bi], ["stageb%d" % si])
                        k.dma("sp", sgT[cc - NPROJ:cc - NPROJ + 128, tb * 512:(tb + 1) * 512], stageb[si],
                              "s_stb%d" % si, r=["stageb%d" % si])
                    else:
                        if 1024 <= cc < 2048:
                            k.act(stage[si], bank, AF.Silu, ["pb%d" % bi], ["stage%d" % si])
                        elif si % 2 == 0:
                            k.cp(stage[si], bank, ["pb%d" % bi], ["stage%d" % si], eng="dve")
                        else:
                            k.cp(stage[si], bank, ["pb%d" % bi], ["stage%d" % si], eng="act")
                        k.dma("sp", projT[cc:cc + 128, tb * 512:(tb + 1) * 512], stage[si],
                              "s_st%d" % si, r=["stage%d" % si], w=["projT"])

        stage_end(l, "S2")
        P.barrier()
        A.reset()
        B.reset()
        ubuf = [A.take([128, 16 + NT], F32) for _ in range(2)]
        sA = A.take([128, 16 + NT], F32)
        sB = A.take([128, 16 + NT], F32)
        invc = A.take([128, NT], F32)
        zab = A.take([128, NT], F32)
        plb = [B.take([128, NT], BF16) for _ in range(2)]
        wpl = B.take([128, 2, 256], BF16)
        for i in range(2):
            k.memset(ubuf[i][:, 0:16], 0.0, ["ubuf%d" % i])
        k.memset(sA[:, 0:16], 0.0, ["sA"])
        k.memset(sB[:, 0:16], 0.0, ["sB"])
        for g in range(4):
            wwin = 2 ** (g + 1)
            k.dma("pool", wpl, w_pool[l, g].rearrange("(ck p) d -> p ck d", p=128), "l_wpl", w=["wpl"])
            k.memset(invc, 1.0 / wwin, ["invc"])
            k.ts1(invc[:, 0:16], cnt16, float(wwin), ALU.min, ["cnt16"], ["invc"])
            k.recip(invc[:, 0:16], invc[:, 0:16], ["invc"], ["invc"])
            for ck_ in range(2):
                ct = 2 * g + ck_
                ub = ubuf[ck_]
                uk = "ubuf%d" % ck_
                k.dma("sp", ub[:, 16:], projT[ct * 128:(ct + 1) * 128, :], "l_ub%d" % ck_, r=["projT"], w=[uk])
                cur, curk = ub, uk
                sh = 1
                bufs = [(sA, "sA"), (sB, "sB")]
                bi2 = 0
                while sh < wwin:
                    nb, nk = bufs[bi2 % 2]
                    bi2 += 1
                    k.tt(nb[:, 16:], cur[:, 16:], cur[:, 16 - sh:16 + NT - sh], ALU.add, [curk], [nk])
                    cur, curk = nb, nk
                    sh *= 2
                k.tt(cur[:, 16:], cur[:, 16:], invc, ALU.mult, [curk, "invc"], [curk])
                k.tt(plb[ck_], cur[:, 16:], ub[:, 16:], ALU.subtract, [curk, uk], ["plb%d" % ck_])
            if g == 0:
                stage_end(l, "S3a")
            for db in range(2):
                dt_ = 2 * g + db
                k.dma("sp", zab, projT[1024 + dt_ * 128:1024 + (dt_ + 1) * 128, :], "l_zab", r=["projT"], w=["zab"])
                for tb in range(4):
                    bi = evac_rr[0] % 4
                    evac_rr[0] += 1
                    for ck_ in range(2):
                        k.mm(pb[bi], wpl[:, ck_, db * 128:(db + 1) * 128], plb[ck_][:, tb * 512:(tb + 1) * 512],
                             ck_ == 0, ck_ == 1, ["wpl", "plb%d" % ck_], ["pb%d" % bi])
                    k.stt(yaT[:, dt_, tb * 512:(tb + 1) * 512], pb[bi], col(l, CPSC + dt_), zab[:, tb * 512:(tb + 1) * 512],
                          ALU.mult, ALU.mult, ["pb%d" % bi, ck, "zab"], ["yaT", "hT"])
            if g == 0:
                stage_end(l, "S3b")

        stage_end(l, "S3")
        P.barrier()
        A.reset()
        B.reset()
        walo = A.take([64, 1 + NT], F32)
        pfd = A.take([64, 512], F32)
        pft = A.take([64, 512], F32)
        pstg = [A.take([128, 512], F32) for _ in range(4)]
        tw_w = B.take([64, NT], BF16)
        tw_a = B.take([64, NT], BF16)
        lw_w = B.take([64, 1024], BF16)
        lw_a = B.take([64, 1024], BF16)
        k.memset(walo[:, 0:1], 0.0, ["walo"])
        k.dma("pool", lw_w, w_du[l], "l_lw", w=["lw"])
        k.dma("pool", lw_a, w_au[l], "l_lw", w=["lw"])
        for which in range(2):
            k.dma("sp", walo[:, 1:], projT[6144 + which * 64:6208 + which * 64, :], "l_walo", r=["projT"], w=["walo"])
            mucol = cols[l][0:64, (CMU_WA if which == 0 else CMU_A):(CMU_WA if which == 0 else CMU_A) + 1]
            for tb in range(4):
                sl = slice(tb * 512, (tb + 1) * 512)
                k.tt(pfd, walo[:, tb * 512:tb * 512 + 512], walo[:, 1 + tb * 512:1 + tb * 512 + 512], ALU.subtract,
                     ["walo"], ["pfd"])
                k.stt(pft, pfd, mucol, walo[:, 1 + tb * 512:1 + tb * 512 + 512], ALU.mult, ALU.add,
                      ["pfd", ck, "walo"], ["pft"])
                if which == 0:
                    k.act(tw_w[:, sl], pft, AF.Tanh, ["pft"], ["tw_w"])
                else:
                    k.cp(tw_a[:, sl], pft, ["pft"], ["tw_a"])
        pi = 0
        for p in range(8):
            for which in range(2):
                lwx, twx, twk = (lw_w, tw_w, "tw_w") if which == 0 else (lw_a, tw_a, "tw_a")
                for tb in range(4):
                    bi = pi % 4
                    pi += 1
                    k.mm(pb[bi], lwx[:, p * 128:(p + 1) * 128], twx[:, tb * 512:(tb + 1) * 512], True, True, ["lw", twk], ["pb%d" % bi])
                    k.cp(pstg[bi], pb[bi], ["pb%d" % bi], ["pstg%d" % bi], eng=("act" if bi % 2 else "dve"))
                    k.dma("sp", lorad[which, p * 128:(p + 1) * 128, tb * 512:(tb + 1) * 512], pstg[bi], "s_pst%d" % bi,
                          r=["pstg%d" % bi], w=["lorad"])
        P.barrier()
        A.reset()
        B.reset()
        W = 256
        NB = NT // W
        NCK = W // 64
        WS = []
        for s_ in range(2):
            ws = {}
            for nm in ("r", "k", "v", "z"):
                ws["raw_" + nm] = A.take([128, W + 1], F32)
            for nm in ("f_r", "f_k", "f_v", "f_z", "f_d", "f_sg", "f_a", "f_t1", "f_kkn", "f_kmod", "f_ka",
                       "f_S", "f_Sp", "f_Se", "f_Wt", "f_Wi", "f_Wp", "f_Wc", "f_bonus"):
                ws[nm] = A.take([128, W], F32)
            ws["f_WC"] = A.take([128, NCK], F32)
            ws["mZ"] = A.take([64, 512], F32)
            ws["mX"] = A.take([64, 512], BF16)
            ws["mIM"] = A.take([64, 512], F32)
            for nm in ("b_x", "b_at", "b_rt", "b_kh", "b_bh", "b_v", "b_atp"):
                ws[nm] = B.take([128, W], BF16)
            ws["b_bt"] = [B.take([128, W], BF16) for _ in range(2)]
            ws["b_kt"] = [B.take([128, W], BF16) for _ in range(2)]
            ws["a_tok"] = B.take([64, NCK, 128], BF16)
            ws["PTb"] = B.take([64, 512], BF16)
            ws["kh_tok"] = B.take([64, NCK, 256], BF16)
            ws["bh_tok"] = B.take([64, NCK, 256], BF16)
            ws["Vpad"] = B.take([64, NCK, 256], BF16)
            ws["Upad"] = B.take([64, 256], BF16)
            ws["mN"] = [B.take([64, 512], F32) for _ in range(2)]
            ws["mM"] = [B.take([64, 512], F32) for _ in range(2)]
            ws["mP"] = [B.take([64, 512], F32) for _ in range(2)]
            for nm in ("mAK", "mRB", "mRK"):
                ws[nm] = B.take([64, 512], BF16)
            WS.append(ws)
        wmvu = A.take([32, 1024], BF16)
        SLOTTED = set(["raw_r", "raw_k", "raw_v", "raw_z", "f_r", "f_k", "f_v", "f_z", "f_d", "f_sg", "f_a", "f_t1",
                       "f_kkn", "f_kmod", "f_ka", "f_S", "f_Sp", "f_Se", "f_Wt", "f_Wi", "f_Wp", "f_Wc", "f_bonus",
                       "f_WC", "b_x", "b_at", "b_rt", "b_kh", "b_bh", "b_v", "b_bt", "b_kt", "a_tok", "kh_tok",
                       "bh_tok", "Vpad", "Upad", "b_atp", "mZ", "mN0", "mN1", "mM0", "mM1", "mP0", "mP1", "mIM",
                       "mAK", "mRB", "mRK", "mX", "vfT", "PTb"])

        class KS:
            def __init__(self, slot):
                self.slot = slot

            def __getattr__(self, name):
                f = getattr(k, name)
                slot = self.slot

                def mapk(a):
                    if isinstance(a, (list, tuple)) and len(a) > 0 and all(isinstance(x_, str) for x_ in a):
                        return [(x_ + "@%d" % slot) if x_ in SLOTTED else x_ for x_ in a]
                    return a

                def wrapped(*args, **kw):
                    return f(*[mapk(a_) for a_ in args], **{kk_: mapk(v_) for kk_, v_ in kw.items()})
                return wrapped

        k.ts1(negc, cols[l][:, CW0:CW0 + 24], -1.0, ALU.mult, [ck], ["negc"])
        for s_ in range(2):
            ks0 = KS(s_)
            ks0.memset(WS[s_]["Upad"], 0.0, ["Upad"])
            ks0.memset(WS[s_]["Vpad"], 0.0, ["Vpad"])
            ks0.memset(WS[s_]["kh_tok"], 0.0, ["kh_tok"])
            ks0.memset(WS[s_]["bh_tok"], 0.0, ["bh_tok"])
            for h_ in range(2):
                ks0.memset(WS[s_]["b_bt"][h_], 0.0, ["b_bt"])
                ks0.memset(WS[s_]["b_kt"][h_], 0.0, ["b_kt"])
        if l == 1:
            k.dma("pool", wmvu, w_mvu[0], "l_wmvu", w=["wmvu"])
        stage_end(l, "S4a")
        for p in range(8):
            k.memset(Hf[p], 0.0, ["Hf%d" % p])
            k.memset(Hb[p], 0.0, ["Hb%d" % p])

        def hr(h):
            return slice(h * 64, (h + 1) * 64)

        def us(u):
            return slice(u * 64, (u + 1) * 64)

        def tk(cc):
            return slice(cc * 64, (cc + 1) * 64)

        def it_gen(p, tb, slot):
            ws = WS[slot]
            kk_ = KS(slot)
            hk, hbk = "Hf%d" % p, "Hb%d" % p
            t0 = tb * W
            raw = {nm: ws["raw_" + nm] for nm in ("r", "k", "v", "z")}
            f_r, f_k, f_v, f_z, f_d, f_sg, f_a, f_t1 = (ws[n_] for n_ in ("f_r", "f_k", "f_v", "f_z", "f_d", "f_sg", "f_a", "f_t1"))
            f_kkn, f_kmod, f_ka, f_S, f_Sp, f_Se = (ws[n_] for n_ in ("f_kkn", "f_kmod", "f_ka", "f_S", "f_Sp", "f_Se"))
            f_Wt, f_Wi, f_Wp, f_Wc, f_bonus, f_WC = (ws[n_] for n_ in ("f_Wt", "f_Wi", "f_Wp", "f_Wc", "f_bonus", "f_WC"))
            f_vf, f_g, f_kk, f_y = f_S, f_Sp, f_Se, f_Wp
            b_x, b_at, b_rt, b_kh, b_bh, b_v = (ws[n_] for n_ in ("b_x", "b_at", "b_rt", "b_kh", "b_bh", "b_v"))
            b_bt, b_kt = ws["b_bt"], ws["b_kt"]
            a_tok, kh_tok, bh_tok, Vpad = ws["a_tok"], ws["kh_tok"], ws["bh_tok"], ws["Vpad"]
            Upad, b_atp, mZ = ws["Upad"], ws["b_atp"], ws["mZ"]
            mN, mM, mP = ws["mN"], ws["mM"], ws["mP"]
            mIM, mAK, mRB, mRK, mX = (ws[n_] for n_ in ("mIM", "mAK", "mRB", "mRK", "mX"))
            PTb = ws["PTb"]
            X0, X1, X2 = pb[3 * slot], pb[3 * slot + 1], pb[3 * slot + 2]
            K0, K1, K2 = "pb%d" % (3 * slot), "pb%d" % (3 * slot + 1), "pb%d" % (3 * slot + 2)
            ptb, ptk = pT[slot], "pT%d" % slot
            fx = {"r": f_r, "k": f_k, "v": f_v, "z": f_z}
            mu0 = {"r": CMU_R, "k": CMU_K, "v": CMU_V, "z": CMU_Z}
            for wi, nm in enumerate(("r", "k", "v", "z")):
                row0 = 2048 + wi * 1024 + p * 128
                rb, rk_ = raw[nm], "raw_" + nm
                sem = "l_raw%s%d" % (nm, slot)
                if tb == 0:
                    kk_.memset(rb[:, 0:1], 0.0, [rk_])
                    kk_.dma("sp", rb[:, 1:W + 1], projT[row0:row0 + 128, 0:W], sem, r=["projT"], w=[rk_])
                else:
                    kk_.dma("sp", rb[:, 0:W + 1], projT[row0:row0 + 128, t0 - 1:t0 + W], sem, r=["projT"], w=[rk_])
            for wi, nm in enumerate(("r", "k", "v", "z")):
                rb, rk_ = raw[nm], "raw_" + nm
                kk_.tt(f_d, rb[:, 0:W], rb[:, 1:W + 1], ALU.subtract, [rk_], ["f_d"])
                kk_.stt(fx[nm], f_d, col(l, mu0[nm] + p), rb[:, 1:W + 1], ALU.mult, ALU.add, ["f_d", ck, rk_], ["f_" + nm])
            yield "prep"
            if l == 0:
                kk_.dma("sp", vfT[p * 128:(p + 1) * 128, t0:t0 + W], f_v, "s_vf%d" % slot, r=["f_v"], w=["vfT"])
            else:
                kk_.dma("sp", f_vf, vfT[p * 128:(p + 1) * 128, t0:t0 + W], "l_vf%d" % slot, r=["vfT"], w=["f_S"])
                kk_.mm(X1[:, 0:W], wmvu[0:32, p * 128:(p + 1) * 128], mvs[0:32, t0:t0 + W], True, True, ["wmvu", "mvs"], [K1])
                kk_.act(f_g, X1[:, 0:W], AF.Exp, [K1, "negc"], ["f_Sp"], bias=negc[:, 16 + p:17 + p], scale=-1.0)
                kk_.act(f_g, f_g, AF.Ln, ["f_Sp", "onec"], ["f_Sp"], bias=onec)
                kk_.act(f_g, f_g, AF.Exp, ["f_Sp"], ["f_Sp"], scale=-1.0)
                kk_.tt(f_d, f_vf, f_v, ALU.subtract, ["f_S", "f_v"], ["f_d"])
                kk_.tt(f_d, f_d, f_g, ALU.mult, ["f_d", "f_Sp"], ["f_d"])
                kk_.tt(f_v, f_v, f_d, ALU.add, ["f_v", "f_d"], ["f_v"])
            kk_.dma("sp", f_sg, lorad[0, p * 128:(p + 1) * 128, t0:t0 + W], "l_sg%d" % slot, r=["lorad"], w=["f_sg"])
            kk_.dma("sp", f_a, lorad[1, p * 128:(p + 1) * 128, t0:t0 + W], "l_fa%d" % slot, r=["lorad"], w=["f_a"])
            kk_.act(f_sg, f_sg, AF.Exp, ["f_sg", "negc"], ["f_sg"], bias=negc[:, p:p + 1], scale=-1.0)
            kk_.act(f_a, f_a, AF.Exp, ["f_a", "negc"], ["f_a"], bias=negc[:, 8 + p:9 + p], scale=-1.0)
            kk_.act(f_sg, f_sg, AF.Ln, ["f_sg", "onec"], ["f_sg"], bias=onec)
            kk_.act(f_a, f_a, AF.Ln, ["f_a", "onec"], ["f_a"], bias=onec)
            kk_.act(f_sg, f_sg, AF.Exp, ["f_sg"], ["f_sg"], scale=-1.0)
            kk_.act(f_a, f_a, AF.Exp, ["f_a"], ["f_a"], scale=-1.0)
            yield "prep"
            kk_.ts1(f_kk, f_k, col(l, CKK + p), ALU.mult, ["f_k", ck], ["f_Se"])
            kk_.tt(b_x, f_kk, f_kk, ALU.mult, ["f_Se"], ["b_x"])
            kk_.mm(X1[:, 0:W], bd1, b_x, True, True, ["bd1", "b_x"], [K1])
            kk_.ts1(f_t1, X1[:, 0:W], 1e-24, ALU.max, [K1], ["f_t1"])
            kk_.act(f_t1, f_t1, AF.Ln, ["f_t1"], ["f_t1"])
            kk_.act(f_t1, f_t1, AF.Exp, ["f_t1"], ["f_t1"], scale=-0.5)
            kk_.tt(f_kkn, f_kk, f_t1, ALU.mult, ["f_Se", "f_t1"], ["f_kkn"])
            kk_.ts2(f_t1, f_a, col(l, CKA + p), omk[:, p:p + 1], ALU.mult, ALU.add, ["f_a", ck, "omk"], ["f_t1"])
            kk_.tt(f_kmod, f_t1, f_k, ALU.mult, ["f_t1", "f_k"], ["f_kmod"])
            kk_.tt(f_ka, f_kkn, f_a, ALU.mult, ["f_kkn", "f_a"], ["f_ka"])
            yield "prep"
            kk_.tt(f_t1, f_r, f_kmod, ALU.mult, ["f_r", "f_kmod"], ["f_t1"])
            kk_.ts1(b_x, f_t1, col(l, CRK + p), ALU.mult, ["f_t1", ck], ["b_x"])
            kk_.mm(X2[:, 0:W], bd1, b_x, True, True, ["bd1", "b_x"], [K2])
            kk_.tt(f_bonus, X2[:, 0:W], f_v, ALU.mult, [K2, "f_v"], ["f_bonus"])
            kk_.scan(f_S, smask[:, 0:W], f_sg, ["smask", "f_sg"], ["f_S"])
            kk_.tt(f_Sp, f_S, f_sg, ALU.subtract, ["f_S", "f_sg"], ["f_Sp"])
            S3 = f_S.rearrange("p (c j) -> p c j", j=64)
            kk_.tt(f_Se.rearrange("p (c j) -> p c j", j=64), S3[:, :, 63:64].to_broadcast([128, NCK, 64]), S3, ALU.subtract,
                   ["f_S"], ["f_Se"])
            kk_.act(f_Wt, f_S, AF.Exp, ["f_S"], ["f_Wt"], scale=-C0)
            kk_.act(f_Wi, f_S, AF.Exp, ["f_S"], ["f_Wi"], scale=C0)
            kk_.act(f_Wp, f_Sp, AF.Exp, ["f_Sp"], ["f_Wp"], scale=-C0)
            kk_.act(f_Wc, f_Se, AF.Exp, ["f_Se"], ["f_Wc"], scale=-C0)
            kk_.act(f_WC, S3[:, :, 63], AF.Exp, ["f_S"], ["f_WC"], scale=-C0)
            yield "prep"
            kk_.stt(b_at, f_kkn, -1.0, f_Wp, ALU.mult, ALU.mult, ["f_kkn", "f_Wp"], ["b_at"])
            for h_ in range(2):
                hs_ = slice(h_ * 64, (h_ + 1) * 64)
                kk_.tt(b_bt[h_][hs_, :], f_ka[hs_, :], f_Wi[hs_, :], ALU.mult, ["f_ka", "f_Wi"], ["b_bt"])
                kk_.tt(b_kt[h_][hs_, :], f_kmod[hs_, :], f_Wi[hs_, :], ALU.mult, ["f_kmod", "f_Wi"], ["b_kt"])
            kk_.tt(b_rt, f_r, f_Wt, ALU.mult, ["f_r", "f_Wt"], ["b_rt"])
            kk_.tt(b_kh, f_kmod, f_Wc, ALU.mult, ["f_kmod", "f_Wc"], ["b_kh"])
            kk_.tt(b_bh, f_ka, f_Wc, ALU.mult, ["f_ka", "f_Wc"], ["b_bh"])
            kk_.cp(b_v, f_v, ["f_v"], ["b_v"], eng="act")
            yield "prep"
            for ti, (src, sk, dst, dk) in enumerate(((b_at, "b_at", a_tok, "a_tok"), (b_kh, "b_kh", kh_tok, "kh_tok"),
                                                     (b_bh, "b_bh", bh_tok, "bh_tok"), (b_v, "b_v", None, "Vpad"))):
                pt = ptb[:, (ti % 2) * 512:(ti % 2) * 512 + 512]
                pk = ptk
                for c in range(NCK):
                    kk_.tr(pt[0:64, c * 128:(c + 1) * 128], src[:, c * 64:(c + 1) * 64], ident, [sk, "ident"], [pk])
                if ti == 0:
                    kk_.cp(dst.rearrange("p c n -> p (c n)"), pt[0:64, 0:NCK * 128], [pk], [dk], eng="dve")
                else:
                    dpad = Vpad if dst is None else dst
                    o4 = dpad.rearrange("p c (h x) -> p c h x", x=128)[:, :, :, 0:64]
                    i4 = pt[0:64, 0:NCK * 128].rearrange("p (c h k) -> p c h k", h=2, k=64)
                    kk_.cp(o4, i4, [pk], [dk], eng=("act" if ti % 2 else "dve"))
            yield "prep_done"
            def vb(buf):
                return buf.bitcast(BF16)[:, 0:512]

            units = [(h, cc) for h in range(2) for cc in range(4)]
            for u, (h, cc) in enumerate(units):
                kk_.mm(X0[0:64, us(u)], b_bt[h][:, tk(cc)], b_at[:, tk(cc)], True, True, ["b_bt", "b_at"], [K0])
                kk_.mm(X1[0:64, us(u)], b_at[:, tk(cc)], b_bt[h][:, tk(cc)], True, True, ["b_bt", "b_at"], [K1])
            for u, (h, cc) in enumerate(units):
                kk_.mm(X2[0:64, us(u)], b_kt[h][:, tk(cc)], b_at[:, tk(cc)], True, True, ["b_kt", "b_at"], [K2])
            kk_.tt(vb(mN[0]), X0[0:64, :], m_su, ALU.mult, [K0, "m_su"], ["mN0"])
            kk_.tt(vb(mM[0]), X1[0:64, :], m_sl, ALU.mult, [K1, "m_sl"], ["mM0"])
            yield "ph"
            for u, (h, cc) in enumerate(units):
                kk_.mm(X0[0:64, us(u)], b_bt[h][:, tk(cc)], b_rt[:, tk(cc)], True, True, ["b_bt", "b_rt"], [K0])
                kk_.mm(X1[0:64, us(u)], b_kt[h][:, tk(cc)], b_rt[:, tk(cc)], True, True, ["b_kt", "b_rt"], [K1])
            kk_.tt(mAK, X2[0:64, :], m_su, ALU.mult, [K2, "m_su"], ["mAK"])
            kk_.tt(mP[0], vb(mN[0]), identx, ALU.add, ["mN0", "identx"], ["mP0"])
            kk_.tt(mRB, X0[0:64, :], m_u, ALU.mult, [K0, "m_u"], ["mRB"])
            kk_.tt(mRK, X1[0:64, :], m_u, ALU.mult, [K1, "m_u"], ["mRK"])
            yield "ph"
            def p_step(i):
                a_, b_ = (i - 1) % 2, i % 2
                im = vb(mIM) if i == 5 else mIM
                pin = vb(mP[a_]) if i == 5 else mP[a_]
                for u in range(8):
                    kk_.mm(X2[0:64, us(u)], im[:, us(u)], pin[:, us(u)], True, True, ["mIM", "mP%d" % a_], [K2])
                if i == 5:
                    kk_.cp(PTb, X2[0:64, :], [K2], ["PTb"], eng="act")
                elif i == 4:
                    kk_.cp(vb(mP[b_]), X2[0:64, :], [K2], ["mP%d" % b_], eng="act")
                else:
                    kk_.cp(mP[b_], X2[0:64, :], [K2], ["mP%d" % b_], eng="act")

            for i in range(1, 6):
                a_, b_ = (i - 1) % 2, i % 2
                nin = vb(mN[a_]) if i in (1, 5) else mN[a_]
                min_ = vb(mM[a_]) if i in (1, 5) else mM[a_]
                for u in range(8):
                    kk_.mm(X0[0:64, us(u)], nin[:, us(u)], min_[:, us(u)], True, True,
                           ["mN%d" % a_, "mM%d" % a_], [K0])
                if i < 5:
                    for u in range(8):
                        kk_.mm(X1[0:64, us(u)], min_[:, us(u)], nin[:, us(u)], True, True,
                               ["mN%d" % a_, "mM%d" % a_], [K1])
                if i > 1:
                    p_step(i - 1)
                if i < 5:
                    mo = vb(mM[b_]) if i == 4 else mM[b_]
                    no = vb(mN[b_]) if i == 4 else mN[b_]
                    kk_.cp(mo, X0[0:64, :], [K0], ["mM%d" % b_], eng="act")
                    kk_.cp(no, X1[0:64, :], [K1], ["mN%d" % b_], eng="dve")
                kk_.tt(vb(mIM) if i == 5 else mIM, X0[0:64, :], identx, ALU.add, [K0, "identx"], ["mIM"])
                yield "ph"
            p_step(5)
            PT, PTk = PTb, "PTb"
            for u, (h, cc) in enumerate(units):
                kk_.mm(X0[0:64, us(u)], mAK[:, us(u)], Vpad[:, cc, h * 128:h * 128 + 64], True, True, ["mAK", "Vpad"], [K0])
            kk_.cp(mX, X0[0:64, :], [K0], ["mX"], eng="dve")
            for u, (h, cc) in enumerate(units):
                kk_.mm(X2[:, us(u)], a_tok[:, cc, :], PT[:, us(u)], True, True, [PTk, "a_tok"], [K2])
            for u, (h, cc) in enumerate(units):
                kk_.mm(X1[0:64, us(u)], PT[:, us(u)], mX[:, us(u)], True, True, [PTk, "mX"], [K1])
            kk_.cp(b_atp[0:64, :], X2[0:64, 0:256], [K2], ["b_atp"], eng="dve")
            kk_.cp(b_atp[64:128, :], X2[64:128, 256:512], [K2], ["b_atp"], eng="act")
            kk_.cp(mZ, X1[0:64, :], [K1], ["mZ"], eng="act")
            yield "ph"
            U3 = Upad.rearrange("p (h x) -> p h x", x=128)[:, :, 0:64]
            for cc in range(4):
                kk_.mm(X0[0:64, 0:128], b_atp[:, tk(cc)], Hb[p], True, True, ["b_atp", hbk], [K0])
                z3 = mZ.rearrange("p (h c v) -> p h c v", h=2, v=64)[:, :, cc, :]
                kk_.tt(U3, X0[0:64, 0:128].rearrange("p (h v) -> p h v", v=64), z3, ALU.add, [K0, "mZ"], ["Upad"])
                kk_.mm(X1[:, 0:64], Hb[p], b_rt[:, tk(cc)], True, False, [hbk, "b_rt"], [K1])
                for h in range(2):
                    u = h * 4 + cc
                    kk_.mm(X1[:, 0:64], Upad[:, h * 64:h * 64 + 128], mRB[:, us(u)], False, False, ["Upad", "mRB"], [K1])
                    kk_.mm(X1[:, 0:64], Vpad[:, cc, h * 64:h * 64 + 128], mRK[:, us(u)], False, h == 1,
                           ["Vpad", "mRK"], [K1])
                for h in range(2):
                    kk_.mm(X2[:, h * 64:(h + 1) * 64], bh_tok[:, cc, h * 64:h * 64 + 128], Upad[:, h * 128:h * 128 + 64], True, False,
                           ["bh_tok", "Upad"], [K2])
                    kk_.mm(X2[:, h * 64:(h + 1) * 64], kh_tok[:, cc, h * 64:h * 64 + 128], Vpad[:, cc, h * 128:h * 128 + 64], False, True,
                           ["kh_tok", "Vpad"], [K2])
                kk_.stt(Hb[p], Hf[p], f_WC[:, cc:cc + 1], X2[:, 0:128], ALU.mult, ALU.add, [hk, "f_WC", K2], [hbk])
                kk_.stt(Hf[p], Hf[p], f_WC[:, cc:cc + 1], X2[:, 0:128], ALU.mult, ALU.add, [hk, "f_WC", K2], [hk])
                kk_.cp(f_y[:, tk(cc)], X1[:, 0:64], [K1], ["f_Wp"], eng="act")
                yield "ph"
            kk_.cp(b_x, f_y, ["f_Wp"], ["b_x"], eng="dve")
            kk_.mm(X0[:, 0:W], bdm, b_x, True, True, ["bdm", "b_x"], [K0])
            kk_.tt(f_d, f_y, X0[:, 0:W], ALU.subtract, ["f_Wp", K0], ["f_d"])
            kk_.tt(b_x, f_d, f_d, ALU.mult, ["f_d"], ["b_x"])
            kk_.mm(X0[:, W:2 * W], bdm, b_x, True, True, ["bdm", "b_x"], [K0])
            kk_.act(f_t1, X0[:, W:2 * W], AF.Ln, [K0, "epsl"], ["f_t1"], bias=epsl)
            kk_.act(f_t1, f_t1, AF.Exp, ["f_t1"], ["f_t1"], scale=-0.5)
            yield "ph"
            kk_.tt(f_d, f_d, f_t1, ALU.mult, ["f_d", "f_t1"], ["f_d"])
            kk_.ts2(f_d, f_d, col(l, CLG + p), col(l, CLB + p), ALU.mult, ALU.add, ["f_d", ck], ["f_d"])
            kk_.tt(f_d, f_d, f_bonus, ALU.add, ["f_d", "f_bonus"], ["f_d"])
            kk_.act(f_t1, f_z, AF.Exp, ["f_z"], ["f_t1"], scale=-1.0)
            kk_.act(f_t1, f_t1, AF.Ln, ["f_t1", "onec"], ["f_t1"], bias=onec)
            kk_.act(f_t1, f_t1, AF.Exp, ["f_t1"], ["f_t1"], scale=-1.0)
            kk_.tt(f_d, f_d, f_z, ALU.mult, ["f_d", "f_z"], ["f_d"])
            kk_.tt(ybT[:, p, t0:t0 + W], f_d, f_t1, ALU.mult, ["f_d", "f_t1"], ["ybT%d" % p])

        def chain(st):
            for p in range(st, 8, 2):
                for tb in range(NB):
                    for tag in it_gen(p, tb, st):
                        yield tag

        lists = []
        for st in range(2):
            P._defer = []
            for _ in chain(st):
                pass
            lists.append(P._defer)
            P._defer = None
        P.merge_streams(lists)

        if DEBUG_OUT:
            P.barrier()
            k.dma("sp", dbg_big[l], big, "s_dbg", r=["yaT", "ybT"])
        stage_end(l, "S4")
        P.barrier()
        A.reset()
        B.reset()
        mrgT = A.take([128, 16, NT], BF16)
        sga = B.take([128, NT], BF16)
        sgb = B.take([128, NT], BF16)
        wa_g = B.take([128, 8, 512], BF16)
        wb_g = B.take([128, 8, 512], BF16)
        m1 = [B.take([128, 512], F32) for _ in range(2)]
        m2 = [B.take([128, 512], F32) for _ in range(2)]
        for cgp in range(4):
            va = w_bra[l].rearrange("(kt p) n -> p kt n", p=128)
            vb = w_brb[l].rearrange("(kt p) n -> p kt n", p=128)
            for q2 in range(2):
                k.dma("pool", wa_g[:, q2 * 4:(q2 + 1) * 4, :], va[:, q2 * 4:(q2 + 1) * 4, cgp * 512:(cgp + 1) * 512], "l_wag", w=["wa_g"])
                k.dma("pool", wb_g[:, q2 * 4:(q2 + 1) * 4, :], vb[:, q2 * 4:(q2 + 1) * 4, cgp * 512:(cgp + 1) * 512], "l_wbg", w=["wb_g"])
            for cb in range(4):
                c = cgp * 4 + cb
                k.dma("sp", sga, sgT[c * 128:(c + 1) * 128, :], "l_sga", r=["sgT"], w=["sga"])
                k.dma("sp", sgb, sgT[D + c * 128:D + (c + 1) * 128, :], "l_sgb", r=["sgT"], w=["sgb"])
                for tb in range(4):
                    sl = slice(tb * 512, (tb + 1) * 512)
                    for kt in range(8):
                        k.mm(pb[0], wa_g[:, kt, cb * 128:(cb + 1) * 128], yaT[:, kt, sl], kt == 0, kt == 7, ["wa_g", "yaT"], ["pb0"])
                    for kt in range(8):
                        k.mm(pb[1], wb_g[:, kt, cb * 128:(cb + 1) * 128], ybT[:, kt, sl], kt == 0, kt == 7, ["wb_g", "ybT"], ["pb1"])
                    i2 = tb % 2
                    k.tt(m1[i2], pb[0], sga[:, sl], ALU.mult, ["pb0", "sga"], ["m1_%d" % i2])
                    k.tt(m2[i2], pb[1], sgb[:, sl], ALU.mult, ["pb1", "sgb"], ["m2_%d" % i2])
                    k.tt(mrgT[:, c, sl], m1[i2], m2[i2], ALU.add, ["m1_%d" % i2, "m2_%d" % i2], ["mrgT"])

        stage_end(l, "S5a")
        P.barrier()
        B.reset()
        xt = [B.take([128, D], F32) for _ in range(2)]
        osb = B.take([128, D], F32)
        junk = B.take([128, D], F32)
        gg = B.take([128, D], F32)
        k.dma("sp", gg, ggd[0:1, :].partition_broadcast(128), "l_gg", r=["ggd"], w=["gg"])
        vo = w_out[l].rearrange("(kt p) n -> p kt n", p=128)
        for q4 in range(4):
            k.dma("pool", big[:, q4 * 4:(q4 + 1) * 4, :], vo[:, q4 * 4:(q4 + 1) * 4, :], "l_wout", w=["hT", "yaT", "ybT", "wout"])
        for tt_ in range(16):
            xs = xt[tt_ % 2]
            xk = "xt%d" % (tt_ % 2)
            k.dma("sp", xs, xsrc[tt_ * 128:(tt_ + 1) * 128, :], "l_" + xk, w=[xk])
            for n in range(4):
                for kt in range(NKT):
                    k.mm(pb[n], mrgT[:, kt, tt_ * 128:(tt_ + 1) * 128], big[:, kt, n * 512:(n + 1) * 512], kt == 0, kt == NKT - 1,
                         ["mrgT", "wout"], ["pb%d" % n])
                k.cp(osb[:, n * 512:(n + 1) * 512], pb[n], ["pb%d" % n], ["osb"], eng=("act" if n % 2 else "dve"))
            k.act(junk, osb, AF.Square, ["osb"], ["junk", "small"], accum=small[:, tt_:tt_ + 1])
            k.act(small[:, 16 + tt_:17 + tt_], small[:, tt_:tt_ + 1], AF.Sqrt, ["small", "epsr"], ["small"], bias=epsr, scale=1.0 / D)
            k.recip(small[:, 32 + tt_:33 + tt_], small[:, 16 + tt_:17 + tt_], ["small"], ["small"])
            k.stt(osb, osb, small[:, 32 + tt_:33 + tt_], gg, ALU.mult, ALU.mult, ["osb", "small", "gg"], ["osb"])
            k.tt(osb, osb, xs, ALU.add, ["osb", xk], ["osb"])
            k.dma("sp", xdst[tt_ * 128:(tt_ + 1) * 128, :], osb, "s_out", r=["osb"], w=["xmid"])
        k.memset(small[:, 0:16], 0.0, ["small"])
        P.barrier()

    try:
        for l in range(2):
            build_layer(l)
    except StopBuild:
        pass
    P.emit()
    return nc


_NC_CACHE = {}


def _pack_cols(inp, l):
    c = np.zeros((128, NCOLS), np.float32)

    def put(c0, vec):
        v = np.asarray(vec, np.float32).reshape(-1)
        n = v.shape[0] // 128
        c[:, c0:c0 + n] = v.reshape(n, 128).T
    put(CG_PRE, inp["g_pre"][l])
    put(CPSC, inp["pool_scale"][l])
    mu = np.asarray(inp["mu_shift"][l], np.float32)
    put(CMU_R, mu[0:1024])
    put(CMU_K, mu[1024:2048])
    put(CMU_V, mu[2048:3072])
    put(CMU_Z, mu[3072:4096])
    c[0:64, CMU_WA] = mu[4096:4160]
    c[0:64, CMU_A] = mu[4160:4224]
    put(CW0, inp["w0"][l])
    put(CA0, inp["a0"][l])
    if l >= 1:
        put(CMV0, inp["mv0"][l - 1])
        c[0:32, CMU_MV] = np.asarray(inp["mu_mv"][l - 1], np.float32)
    put(CKK, inp["k_k"][l])
    put(CKA, inp["k_a"][l])
    put(CRK, np.asarray(inp["r_k"][l], np.float32).reshape(-1))
    put(CLG, inp["lnx_g"][l])
    put(CLB, inp["lnx_b"][l])
    return c


def kernel(**inputs):
    inp = {k_: np.asarray(v) for k_, v in inputs.items()}
    if "nc" not in _NC_CACHE:
        _NC_CACHE["nc"] = build_program()
    nc = _NC_CACHE["nc"]
    cols = np.stack([_pack_cols(inp, 0), _pack_cols(inp, 1)], axis=0)
    shared = {
        "w_ada": np.ascontiguousarray(inp["w_ada"], np.float32),
        "b_ada": np.ascontiguousarray(inp["b_ada"], np.float32),
        "w_in": np.ascontiguousarray(inp["w_in"], np.float32),
        "w_pool": np.ascontiguousarray(inp["w_pool"], np.float32),
        "w_decay_up": np.ascontiguousarray(inp["w_decay_up"], np.float32),
        "w_aaa_up": np.ascontiguousarray(inp["w_aaa_up"], np.float32),
        "w_mv_down": np.ascontiguousarray(inp["w_mv_down"], np.float32),
        "w_mv_up": np.ascontiguousarray(inp["w_mv_up"], np.float32),
        "w_br_a": np.ascontiguousarray(inp["w_br_a"], np.float32),
        "w_br_b": np.ascontiguousarray(inp["w_br_b"], np.float32),
        "w_out": np.ascontiguousarray(inp["w_out"], np.float32),
        "g_post": np.ascontiguousarray(inp["g_post"], np.float32),
        "cols": cols,
    }
    in_maps = []
    for core in range(8):
        b = core % 4
        m = dict(shared)
        m["x"] = np.ascontiguousarray(inp["x"][b], np.float32)
        m["cT"] = np.ascontiguousarray(np.asarray(inp["c"][b], np.float32).reshape(NKT, 128).T)
        in_maps.append(m)
    res = run_bass_kernel_spmd(nc, in_maps, core_ids=list(range(8)))
    out = np.stack([np.asarray(res.results[b]["y"], np.float32) for b in range(4)], axis=0)
    return out
```

```python
import numpy as np
import concourse.bass as bass
import concourse.mybir as mybir
from concourse.bass_utils import run_bass_kernel_spmd

F32 = mybir.dt.float32
BF16 = mybir.dt.bfloat16
AF = mybir.ActivationFunctionType
ALU = mybir.AluOpType

ENGS = ("pe", "act", "dve", "pool", "sp")
EPOCH = 30000

D = 2048
NT = 2048
NKT = 16
NIN = 10368
NPROJ = 6272
C0 = float(np.exp(-0.5))
RMS_EPS = 1e-6
LNX_EPS = 64e-5

CG_PRE, CPSC, CMU_R, CMU_K, CMU_V, CMU_Z, CMU_WA = 0, 16, 24, 32, 40, 48, 56
CW0, CA0, CMV0, CKK, CKA, CRK, CLG, CLB, CMU_MV, CMU_A = 57, 65, 73, 81, 89, 97, 105, 113, 121, 122
NCOLS = 123


class Prog:
    def __init__(self, nc):
        self.nc = nc
        self.q = {e: [] for e in ENGS}
        self.cnt = {e: 0 for e in ENGS}
        self.seen = {e: {} for e in ENGS}
        self.vc = {}
        self.last_w = {}
        self.readers = {}
        self.sems = {}
        self.dma_cnt = {}
        self.waited = {e: set() for e in ENGS}
        self._stack = []
        self._defer = None

    def _sem(self, key):
        if key not in self.sems:
            cm = self.nc.semaphore("s_%s_%s" % (key[0], key[1]))
            h = cm.__enter__()
            self._stack.append(cm)
            self.sems[key] = h
        return self.sems[key]

    def _note(self, waits):
        for (k_, v) in waits:
            if k_ in self.waited:
                self.waited[k_].add(v)

    def _deps(self, eng, reads, writes, skipkey=None):
        toks = []
        for r in reads:
            t = self.last_w.get(r)
            if t is not None:
                toks.append(t)
        for w in writes:
            t = self.last_w.get(w)
            if t is not None:
                toks.append(t)
            toks.extend(self.readers.get(w, ()))
        need = {}
        for (k, v) in toks:
            if eng == "pe" and k == "pe":
                continue
            if skipkey is not None and k == skipkey:
                continue
            if self.seen[eng].get(k, 0) >= v:
                continue
            if need.get(k, 0) < v:
                need[k] = v
        waits = []
        items = sorted(need.items(), key=lambda kv: -len(self.vc.get((kv[0], kv[1]), ())))
        for k, v in items:
            if self.seen[eng].get(k, 0) >= v:
                continue
            waits.append((k, v))
            self.seen[eng][k] = v
            for k2, v2 in self.vc.get((k, v), {}).items():
                if self.seen[eng].get(k2, 0) < v2:
                    self.seen[eng][k2] = v2
        self._note(waits)
        return waits

    def _finish(self, tok, eng, reads, writes):
        c = dict(self.seen[eng])
        k, v = tok
        c[k] = v
        self.vc[tok] = c
        for r in reads:
            self.readers.setdefault(r, []).append(tok)
        for w in writes:
            self.last_w[w] = tok
            self.readers[w] = []

    def op(self, eng, fn, reads=(), writes=(), cost=0.3):
        if self._defer is not None:
            self._defer.append(("op", eng, fn, tuple(reads), tuple(writes), cost))
            return None
        ps_r = [r for r in reads if r[:2] in ("pb", "pT")]
        if ps_r:
            reads = [r for r in reads if r[:2] not in ("pb", "pT")]
            writes = list(writes) + ps_r
        waits = self._deps(eng, reads, writes)
        self.cnt[eng] += 1
        n = self.cnt[eng]
        tok = (eng, n)
        self.q[eng].append((waits, fn, eng, n))
        self._finish(tok, eng, reads, writes)
        return tok

    def dma(self, eng, out, in_, semname, reads=(), writes=()):
        if self._defer is not None:
            self._defer.append(("dma", eng, (out, in_, semname), tuple(reads), tuple(writes), 2.0))
            return None
        key = ("dma", semname)
        waits = self._deps(eng, reads, writes, skipkey=key)
        self.dma_cnt[key] = self.dma_cnt.get(key, 0) + 16
        tok = (key, self.dma_cnt[key])

        def fn(e, out=out, in_=in_):
            return e.dma_start(out=out, in_=in_)
        self.q[eng].append((waits, fn, key, 16))
        self._finish(tok, eng, reads, writes)
        return tok

    def merge_streams(self, lists, hop=0.05):
        eng_free = {e: 0.0 for e in ENGS}
        kw, kr = {}, {}
        idx = [0] * len(lists)

        def est(rec):
            kind, eng, _, reads, writes, cost = rec
            rd = [r for r in reads if r[:2] not in ("pb", "pT")]
            wr = list(writes) + [r for r in reads if r[:2] in ("pb", "pT")]
            t = eng_free[eng]
            for r in rd:
                t = max(t, kw.get(r, 0.0) + hop)
            for w in wr:
                t = max(t, kw.get(w, 0.0) + hop, kr.get(w, 0.0) + hop)
            return t, rd, wr

        while True:
            best, bt = None, None
            for i, lst in enumerate(lists):
                if idx[i] >= len(lst):
                    continue
                t, _, _ = est(lst[idx[i]])
                if best is None or t < bt - 1e-9:
                    best, bt = i, t
            if best is None:
                break
            rec = lists[best][idx[best]]
            idx[best] += 1
            kind, eng, payload, reads, writes, cost = rec
            t, rd, wr = est(rec)
            fin = t + cost
            eng_free[eng] = t + (0.06 if kind == "dma" else cost)
            for r in rd:
                kr[r] = max(kr.get(r, 0.0), fin)
            for w in wr:
                kw[w] = fin
                kr[w] = 0.0
            if kind == "op":
                self.op(eng, payload, reads, writes)
            else:
                self.dma(eng, payload[0], payload[1], payload[2], reads, writes)
        return max(eng_free.values())

    def dma_like(self, eng, fn, semname, reads=(), writes=()):
        key = ("dma", semname)
        waits = self._deps(eng, reads, writes, skipkey=key)
        self.dma_cnt[key] = self.dma_cnt.get(key, 0) + 16
        tok = (key, self.dma_cnt[key])
        self.q[eng].append((waits, fn, key, 16))
        self._finish(tok, eng, reads, writes)
        return tok

    def barrier(self):
        toks = []
        for e in ENGS:
            if self.cnt[e] > 0:
                toks.append((e, self.cnt[e]))
        for key, val in self.dma_cnt.items():
            toks.append((key, val))
        for e in ENGS:
            waits = []
            for (k_, v) in toks:
                if e == "pe" and k_ == "pe":
                    continue
                if self.seen[e].get(k_, 0) >= v:
                    continue
                waits.append((k_, v))
                self.seen[e][k_] = v
            if waits:
                self._note(waits)
                self.q[e].append((waits, None, None, 0))

    def emit(self):
        rank = {}
        for e in ENGS:
            rank[e] = {idx: i + 1 for i, idx in enumerate(sorted(self.waited[e]))}

        def semval(k_, v):
            if k_ in rank:
                r = rank[k_][v]
                return self._sem((k_, (r - 1) // EPOCH)), (r - 1) % EPOCH + 1
            return self._sem(k_), v
        prog = self
        self.n_inc = {e: len(rank[e]) for e in ENGS}

        def run(engobj, lst):
            for (waits, fn, key, idx) in lst:
                for (k_, v) in waits:
                    sm, val = semval(k_, v)
                    engobj.wait_ge(sm, val)
                if fn is None:
                    continue
                if key in rank:
                    ins = fn(engobj)
                    if idx in rank[key]:
                        sm, _ = semval(key, idx)
                        ins.then_inc(sm, 1)
                else:
                    fn(engobj).then_inc(prog._sem(key), 16)

        with self.nc.Block() as block:
            @block.tensor
            def _(e):
                run(e, prog.q["pe"])

            @block.scalar
            def _(e):
                run(e, prog.q["act"])

            @block.vector
            def _(e):
                run(e, prog.q["dve"])

            @block.gpsimd
            def _(e):
                run(e, prog.q["pool"])

            @block.sync
            def _(e):
                run(e, prog.q["sp"])


DEBUG_OUT = False
STOP = None


class StopBuild(Exception):
    pass


class Arena:
    def __init__(self, ap):
        self.ap = ap
        self.cap = ap.shape[1]
        self.off = 0

    def reset(self):
        self.off = 0

    def take(self, shape, dt):
        n = 1
        for d_ in shape[1:]:
            n *= d_
        units = n * (2 if dt == F32 else 1)
        units = (units + 15) // 16 * 16
        assert self.off + units <= self.cap, ("arena overflow", self.off, units, self.cap)
        v = self.ap[0:shape[0], self.off:self.off + units]
        self.off += units
        if dt == F32:
            v = v.bitcast(F32)
        v = v[:, 0:n]
        if len(shape) == 3:
            v = v.rearrange("p (a b) -> p a b", b=shape[2])
        return v


class K:
    def __init__(self, nc):
        self.nc = nc
        self.P = Prog(nc)

    def sb(self, name, shape, dt):
        return self.nc.alloc_sbuf_tensor(name, list(shape), dt).ap()

    def ps(self, name, shape, dt):
        return self.nc.alloc_psum_tensor(name, list(shape), dt).ap()

    @staticmethod
    def _n(ap):
        n = 1
        for d_ in ap.shape[1:]:
            n *= d_
        return n

    def _cd(self, out, eng="dve", mult=1.0):
        n = self._n(out)
        if eng == "act":
            return (224.0 + n) / 1200.0
        c = (64.0 + n) / 960.0 * mult
        return c * (2.0 if eng == "pool" else 1.0)

    def mm(self, out, lhsT, rhs, start, stop, r, w):
        self.P.op("pe", lambda e: e.matmul(out, lhsT, rhs, start=start, stop=stop), r, w,
                  cost=max(64.0, self._n(out)) / 2400.0 + 0.01)

    def tr(self, out, in_, ident, r, w):
        self.P.op("pe", lambda e: e.transpose(out, in_, ident), r, w, cost=0.07)

    def act(self, out, in_, func, r, w, bias=None, scale=1.0, accum=None):
        def fn(e):
            kw = {}
            if bias is not None:
                kw["bias"] = bias
            if accum is not None:
                kw["accum_out"] = accum
            return e.activation(out=out, in_=in_, func=func, scale=scale, **kw)
        self.P.op("act", fn, r, w, cost=self._cd(out, "act"))

    def tt(self, out, a, b, op, r, w, eng="dve"):
        self.P.op(eng, lambda e: e.tensor_tensor(out=out, in0=a, in1=b, op=op), r, w, cost=self._cd(out, eng))

    def ts1(self, out, a, s, op, r, w, eng="dve"):
        self.P.op(eng, lambda e: e.tensor_single_scalar(out=out, in_=a, scalar=s, op=op), r, w, cost=self._cd(out, eng))

    def ts2(self, out, a, s1, s2, op0, op1, r, w, eng="dve"):
        self.P.op(eng, lambda e: e.tensor_scalar(out=out, in0=a, scalar1=s1, scalar2=s2, op0=op0, op1=op1), r, w,
                  cost=self._cd(out, eng))

    def stt(self, out, a, s, b, op0, op1, r, w, eng="dve"):
        self.P.op(eng, lambda e: e.scalar_tensor_tensor(out=out, in0=a, scalar=s, in1=b, op0=op0, op1=op1), r, w,
                  cost=self._cd(out, eng))

    def cp(self, out, in_, r, w, eng="dve"):
        if eng == "act":
            self.act(out, in_, AF.Copy, r, w)
        else:
            self.P.op(eng, lambda e: e.tensor_copy(out=out, in_=in_), r, w, cost=self._cd(out, eng))

    def recip(self, out, in_, r, w):
        self.P.op("dve", lambda e: e.reciprocal(out=out, in_=in_), r, w, cost=self._cd(out, "dve", 6.0))

    def memset(self, ap, val, w, eng="dve"):
        self.P.op(eng, lambda e: e.memset(ap, val), (), w, cost=self._cd(ap, eng, 0.5))

    def scan(self, out, d0, d1, r, w):
        self.P.op("dve", lambda e: e.tensor_tensor_scan(out=out, data0=d0, data1=d1, initial=0.0,
                                                        op0=ALU.mult, op1=ALU.add), r, w, cost=self._cd(out, "dve", 2.0))

    def asel(self, out, in_, pattern, cmp_op, fill, cm, r, w):
        self.P.op("pool", lambda e: e.affine_select(out=out, in_=in_, pattern=pattern, compare_op=cmp_op,
                                                     fill=fill, base=0, channel_multiplier=cm), r, w)

    def dma(self, eng, out, in_, sem, r=(), w=()):
        self.P.dma(eng, out, in_, sem, r, w)


def build_program():
    nc = bass.Bass("TRN2", target_bir_lowering=False)
    k = K(nc)
    P = k.P

    def din(name, shape, dt=F32):
        return nc.dram_tensor(name, list(shape), dt, kind="ExternalInput").ap()

    x_in = din("x", [NT, D])
    cT_in = din("cT", [128, NKT])
    w_ada = din("w_ada", [2, D, 3 * D])
    b_ada = din("b_ada", [2, 3 * D])
    w_in = din("w_in", [2, D, NIN])
    w_pool = din("w_pool", [2, 4, 256, 256])
    w_du = din("w_decay_up", [2, 64, 1024])
    w_au = din("w_aaa_up", [2, 64, 1024])
    w_mvd = din("w_mv_down", [1, D, 32])
    w_mvu = din("w_mv_up", [1, 32, 1024])
    w_bra = din("w_br_a", [2, 1024, D])
    w_brb = din("w_br_b", [2, 1024, D])
    w_out = din("w_out", [2, D, D])
    g_post = din("g_post", [2, D])
    cols_in = din("cols", [2, 128, NCOLS])
    y_out = nc.dram_tensor("y", [NT, D], F32, kind="ExternalOutput").ap()

    skind = "ExternalOutput" if DEBUG_OUT else "Internal"
    projT = nc.dram_tensor("projT", [NPROJ, NT], F32, kind=skind).ap()
    sgT = nc.dram_tensor("sgT", [2 * D, NT], BF16, kind=skind).ap()
    vfT = nc.dram_tensor("vfT", [1024, NT], F32, kind=skind).ap()
    xmid = nc.dram_tensor("xmid", [NT, D], F32, kind=skind).ap()
    ggd = nc.dram_tensor("ggd", [1, D], F32, kind="Internal").ap()
    lorad = nc.dram_tensor("lorad", [2, 1024, NT], F32, kind="Internal").ap()

    if DEBUG_OUT:
        dbg_big = nc.dram_tensor("dbg_big", [2, 128, 16, NT], BF16, kind="ExternalOutput").ap()
    big = k.sb("big", [128, 16, NT], BF16)
    hT = big
    yaT = big[:, 0:8, :]
    ybT = big[:, 8:16, :]
    arA = k.sb("arenaA", [128, 32768], BF16)
    arB = k.sb("arenaB", [128, 30208], BF16)
    A = Arena(arA)
    B = Arena(arB)
    cols = [k.sb("cols%d" % l, [128, NCOLS], F32) for l in range(2)]
    small = k.sb("small", [128, 64], F32)
    shc = k.sb("shc", [128, 16], F32)
    gsc = k.sb("gsc", [128, 16], F32)
    omk = k.sb("omk", [128, 8], F32)
    cT = k.sb("cTs", [128, NKT], F32)
    condT = k.sb("condT", [128, NKT], BF16)
    onesrow = k.sb("onesrow", [1, 128], F32)
    ident = k.sb("ident", [128, 128], BF16)
    identx = k.sb("identx", [64, 512], F32)
    m_su = k.sb("m_su", [64, 512], BF16)
    m_u = k.sb("m_u", [64, 512], BF16)
    m_sl = k.sb("m_sl", [64, 512], BF16)
    bd1 = k.sb("bd1", [128, 128], BF16)
    bdm = k.sb("bdm", [128, 128], BF16)
    smask = k.sb("smask", [128, 512], F32)
    epsr = k.sb("epsr", [128, 1], F32)
    epsl = k.sb("epsl", [128, 1], F32)
    mvs = k.sb("mvs", [32, NT], BF16)
    negc = k.sb("negc", [128, 24], F32)
    onec = k.sb("onec", [128, 1], F32)
    cnt16 = k.sb("cnt16", [128, 16], F32)
    one16 = k.sb("one16", [128, 16], F32)
    Hf = [k.sb("Hf%d" % p, [128, 128], F32) for p in range(8)]
    Hb = [k.sb("Hb%d" % p, [128, 128], BF16) for p in range(8)]

    pb = [k.ps("pb%d" % i, [128, 512], F32) for i in range(6)]
    pT = [k.ps("pT%d" % i, [128, 1024], BF16) for i in range(2)]

    identf = A.take([128, 128], F32)
    identxf = A.take([64, 512], F32)
    mtmp = A.take([64, 512], F32)
    k.memset(identf, 0.0, ["identf"])
    k.asel(identf, identf, [[-1, 128]], ALU.not_equal, 1.0, 1, ["identf"], ["identf"])
    k.cp(ident, identf, ["identf"], ["ident"])
    ix3 = identxf.rearrange("p (u j) -> p u j", j=64)
    k.memset(identxf, 0.0, ["identxf"])
    k.asel(ix3, ix3, [[0, 8], [-1, 64]], ALU.not_equal, 1.0, 1, ["identxf"], ["identxf"])
    k.cp(identx, identxf, ["identxf"], ["identx"])
    for (m, mk_, cmpop, cm, coef) in ((m_su, "m_su", ALU.is_gt, -1, 1), (m_u, "m_u", ALU.is_ge, -1, 1),
                                      (m_sl, "m_sl", ALU.is_gt, 1, -1)):
        m3 = mtmp.rearrange("p (u j) -> p u j", j=64)
        k.memset(mtmp, 1.0, ["mtmp"])
        k.asel(m3, m3, [[0, 8], [coef, 64]], cmpop, 0.0, cm, ["mtmp"], ["mtmp"])
        k.cp(m, mtmp, ["mtmp"], [mk_])
    k.memset(bd1, 0.0, ["bd1"])
    k.memset(bd1[0:64, 0:64], 1.0, ["bd1"])
    k.memset(bd1[64:128, 64:128], 1.0, ["bd1"])
    k.memset(bdm, 0.0, ["bdm"])
    k.memset(bdm[0:64, 0:64], 1.0 / 64, ["bdm"])
    k.memset(bdm[64:128, 64:128], 1.0 / 64, ["bdm"])
    k.memset(smask, 1.0, ["smask"])
    k.memset(smask.rearrange("p (c j) -> p c j", j=64)[:, :, 0:1], 0.0, ["smask"])
    k.memset(epsr, RMS_EPS, ["epsr"])
    k.memset(epsl, LNX_EPS, ["epsl"])
    k.memset(onesrow, 1.0, ["onesrow"])
    k.memset(small, 0.0, ["small"])
    k.memset(one16, 1.0, ["one16"])
    k.memset(onec, 1.0, ["onec"])
    k.scan(cnt16, one16, one16, ["one16"], ["cnt16"])

    k.dma("sp", cT, cT_in, "l_c", w=["cT"])
    for l in range(2):
        k.dma("sp", cols[l], cols_in[l], "l_cols%d" % l, w=["cols%d" % l])
    k.act(condT, cT, AF.Silu, ["cT"], ["condT"])
    P.barrier()

    evac_rr = [0]
    stage_rr = [0]
    wgrp = [None, None]

    def col(l, c):
        return cols[l][:, c:c + 1]

    def load_wgrp(l, slot, src, c0, wd):
        v = src.rearrange("(kt p) n -> p kt n", p=128)
        for q4 in range(4):
            k.dma("pool", wgrp[slot][:, q4 * 4:(q4 + 1) * 4, 0:wd], v[:, q4 * 4:(q4 + 1) * 4, c0:c0 + wd],
                  "l_wg%d" % slot, w=["wgrp%d" % slot])

    def stage_end(l, tag):
        if STOP == "%d:%s" % (l, tag):
            P.barrier()
            raise StopBuild()

    def build_layer(l):
        xsrc = x_in if l == 0 else xmid
        xdst = xmid if l == 0 else y_out
        ck = "cols%d" % l
        A.reset()
        B.reset()
        modrow = A.take([1, 3 * D], F32)
        gprow = A.take([1, D], F32)
        ggrow = A.take([1, D], F32)
        brow = A.take([1, 512], F32)
        wgrp[0] = B.take([128, 16, 512], BF16)
        wgrp[1] = B.take([128, 16, 512], BF16)
        k.dma("sp", gprow, g_post[l:l + 1, :], "l_gp", w=["gprow"])
        for cg in range(12):
            slot = cg % 2
            load_wgrp(l, slot, w_ada[l], cg * 512, 512)
            k.dma("sp", brow, b_ada[l:l + 1, cg * 512:(cg + 1) * 512], "l_bada", w=["brow"])
            for kt in range(NKT):
                k.mm(pb[0][0:1, :], condT[:, kt:kt + 1], wgrp[slot][:, kt, :], kt == 0, kt == NKT - 1,
                     ["condT", "wgrp%d" % slot], ["pb0"])
            k.tt(modrow[:, cg * 512:(cg + 1) * 512], pb[0][0:1, :], brow, ALU.add,
                 ["pb0", "brow"], ["modrow"])
        for i in range(32):
            k.mm(pb[1][:, i:i + 1], modrow[0:1, i * 128:(i + 1) * 128], onesrow[0:1, 0:1], True, True,
                 ["modrow", "onesrow"], ["pb1"])
        k.cp(shc, pb[1][:, 0:16], ["pb1"], ["shc"])
        k.stt(gsc, pb[1][:, 16:32], 1.0, cols[l][:, CG_PRE:CG_PRE + 16], ALU.add, ALU.mult, ["pb1", ck], ["gsc"])
        k.tt(ggrow, modrow[:, 2 * D:3 * D], gprow, ALU.mult, ["modrow", "gprow"], ["ggrow"])
        k.dma("sp", ggd, ggrow, "s_ggd", r=["ggrow"], w=["ggd"])
        k.ts2(omk, cols[l][:, CKA:CKA + 8], -1.0, 1.0, ALU.mult, ALU.add, [ck], ["omk"])
        P.barrier()
        A.reset()
        B.reset()
        xt = [A.take([128, D], F32) for _ in range(2)]
        junk = A.take([128, D], F32)
        xnb = [A.take([128, D], BF16) for _ in range(4)]
        for tg in range(4):
            for j in range(4):
                tt_ = tg * 4 + j
                xs = xt[tt_ % 2]
                k.dma("sp", xs, xsrc[tt_ * 128:(tt_ + 1) * 128, :], "l_xt%d" % (tt_ % 2), w=["xt%d" % (tt_ % 2)])
                k.act(junk, xs, AF.Square, ["xt%d" % (tt_ % 2)], ["junk", "small"], accum=small[:, tt_:tt_ + 1])
                k.act(small[:, 16 + tt_:17 + tt_], small[:, tt_:tt_ + 1], AF.Sqrt, ["small", "epsr"], ["small"],
                      bias=epsr, scale=1.0 / D)
                k.recip(small[:, 32 + tt_:33 + tt_], small[:, 16 + tt_:17 + tt_], ["small"], ["small"])
                k.ts1(xnb[j], xs, small[:, 32 + tt_:33 + tt_], ALU.mult, ["xt%d" % (tt_ % 2), "small"], ["xnb%d" % j])
            for ft in range(NKT):
                pt = pT[ft % 2]
                for j in range(4):
                    k.tr(pt[:, j * 128:(j + 1) * 128], xnb[j][:, ft * 128:(ft + 1) * 128], ident,
                         ["xnb%d" % j, "ident"], ["pT%d" % (ft % 2)])
                k.act(hT[:, ft, tg * 512:(tg + 1) * 512], pt[:, 0:512], AF.Identity, ["pT%d" % (ft % 2), "gsc", "shc"],
                      ["hT"], bias=shc[:, ft:ft + 1], scale=gsc[:, ft:ft + 1])
        k.memset(small[:, 0:16], 0.0, ["small"])

        stage_end(l, "S1")
        P.barrier()
        A.reset()
        B.reset()
        stage = [A.take([128, 512], F32) for _ in range(4)]
        stageb = [A.take([128, 512], BF16) for _ in range(4)]
        mvraw = A.take([32, 1 + NT], F32)
        f_dm = A.take([32, 512], F32)
        wgrp[0] = B.take([128, 16, 512], BF16)
        wgrp[1] = B.take([128, 16, 512], BF16)
        wmvd = B.take([128, 16, 32], BF16)
        if l == 1:
            k.memset(mvraw[:, 0:1], 0.0, ["mvraw"])
            k.dma("pool", wmvd, w_mvd[0].rearrange("(kt p) n -> p kt n", p=128), "l_wmvd", w=["wmvd"])
            for tb in range(4):
                for kt in range(NKT):
                    k.mm(pb[4][0:32, :], wmvd[:, kt, :], hT[:, kt, tb * 512:(tb + 1) * 512], kt == 0, kt == NKT - 1,
                         ["wmvd", "hT"], ["pb4"])
                k.cp(mvraw[:, 1 + tb * 512:1 + (tb + 1) * 512], pb[4][0:32, :], ["pb4"], ["mvraw"])
            for tb in range(4):
                k.tt(f_dm, mvraw[:, tb * 512:tb * 512 + 512], mvraw[:, 1 + tb * 512:1 + tb * 512 + 512], ALU.subtract,
                     ["mvraw"], ["f_dm"])
                k.stt(mvs[:, tb * 512:(tb + 1) * 512], f_dm, cols[l][0:32, CMU_MV:CMU_MV + 1],
                      mvraw[:, 1 + tb * 512:1 + tb * 512 + 512], ALU.mult, ALU.add, ["f_dm", ck, "mvraw"], ["mvs"])
        groups = [(g * 512, 512) for g in range(20)] + [(10240, 128)]
        for gi, (c0, wd) in enumerate(groups):
            slot = gi % 2
            load_wgrp(l, slot, w_in[l], c0, wd)
            for cb in range(wd // 128):
                cc = c0 + cb * 128
                for tb in range(4):
                    bi = evac_rr[0] % 4
                    evac_rr[0] += 1
                    bank = pb[bi]
                    for kt in range(NKT):
                        k.mm(bank, wgrp[slot][:, kt, cb * 128:(cb + 1) * 128], hT[:, kt, tb * 512:(tb + 1) * 512],
                             kt == 0, kt == NKT - 1, ["wgrp%d" % slot, "hT"], ["pb%d" % bi])
                    si = stage_rr[0] % 4
                    stage_rr[0] += 1
                    if cc >= NPROJ:
                        k.act(stageb[si], bank, AF.Sigmoid, ["pb%d" % bi], ["stageb%d" % si])
                        k.dma("sp", sgT[cc - NPROJ:cc - NPROJ + 128, tb * 512:(tb + 1) * 512], stageb[si],
                              "s_stb%d" % si, r=["stageb%d" % si])
                    else:
                        if 1024 <= cc < 2048:
                            k.act(stage[si], bank, AF.Silu, ["pb%d" % bi], ["stage%d" % si])
                        elif si % 2 == 0:
                            k.cp(stage[si], bank, ["pb%d" % bi], ["stage%d" % si], eng="dve")
                        else:
                            k.cp(stage[si], bank, ["pb%d" % bi], ["stage%d" % si], eng="act")
                        k.dma("sp", projT[cc:cc + 128, tb * 512:(tb + 1) * 512], stage[si],
                              "s_st%d" % si, r=["stage%d" % si], w=["projT"])

        stage_end(l, "S2")
        P.barrier()
        A.reset()
        B.reset()
        ubuf = [A.take([128, 16 + NT], F32) for _ in range(2)]
        sA = A.take([128, 16 + NT], F32)
        sB = A.take([128, 16 + NT], F32)
        invc = A.take([128, NT], F32)
        zab = A.take([128, NT], F32)
        plb = [B.take([128, NT], BF16) for _ in range(2)]
        wpl = B.take([128, 2, 256], BF16)
        for i in range(2):
            k.memset(ubuf[i][:, 0:16], 0.0, ["ubuf%d" % i])
        k.memset(sA[:, 0:16], 0.0, ["sA"])
        k.memset(sB[:, 0:16], 0.0, ["sB"])
        for g in range(4):
            wwin = 2 ** (g + 1)
            k.dma("pool", wpl, w_pool[l, g].rearrange("(ck p) d -> p ck d", p=128), "l_wpl", w=["wpl"])
            k.memset(invc, 1.0 / wwin, ["invc"])
            k.ts1(invc[:, 0:16], cnt16, float(wwin), ALU.min, ["cnt16"], ["invc"])
            k.recip(invc[:, 0:16], invc[:, 0:16], ["invc"], ["invc"])
            for ck_ in range(2):
                ct = 2 * g + ck_
                ub = ubuf[ck_]
                uk = "ubuf%d" % ck_
                k.dma("sp", ub[:, 16:], projT[ct * 128:(ct + 1) * 128, :], "l_ub%d" % ck_, r=["projT"], w=[uk])
                cur, curk = ub, uk
                sh = 1
                bufs = [(sA, "sA"), (sB, "sB")]
                bi2 = 0
                while sh < wwin:
                    nb, nk = bufs[bi2 % 2]
                    bi2 += 1
                    k.tt(nb[:, 16:], cur[:, 16:], cur[:, 16 - sh:16 + NT - sh], ALU.add, [curk], [nk])
                    cur, curk = nb, nk
                    sh *= 2
                k.tt(cur[:, 16:], cur[:, 16:], invc, ALU.mult, [curk, "invc"], [curk])
                k.tt(plb[ck_], cur[:, 16:], ub[:, 16:], ALU.subtract, [curk, uk], ["plb%d" % ck_])
            if g == 0:
                stage_end(l, "S3a")
            for db in range(2):
                dt_ = 2 * g + db
                k.dma("sp", zab, projT[1024 + dt_ * 128:1024 + (dt_ + 1) * 128, :], "l_zab", r=["projT"], w=["zab"])
                for tb in range(4):
                    bi = evac_rr[0] % 4
                    evac_rr[0] += 1
                    for ck_ in range(2):
                        k.mm(pb[bi], wpl[:, ck_, db * 128:(db + 1) * 128], plb[ck_][:, tb * 512:(tb + 1) * 512],
                             ck_ == 0, ck_ == 1, ["wpl", "plb%d" % ck_], ["pb%d" % bi])
                    k.stt(yaT[:, dt_, tb * 512:(tb + 1) * 512], pb[bi], col(l, CPSC + dt_), zab[:, tb * 512:(tb + 1) * 512],
                          ALU.mult, ALU.mult, ["pb%d" % bi, ck, "zab"], ["yaT", "hT"])
            if g == 0:
                stage_end(l, "S3b")

        stage_end(l, "S3")
        P.barrier()
        A.reset()
        B.reset()
        walo = A.take([64, 1 + NT], F32)
        pfd = A.take([64, 512], F32)
        pft = A.take([64, 512], F32)
        pstg = [A.take([128, 512], F32) for _ in range(4)]
        tw_w = B.take([64, NT], BF16)
        tw_a = B.take([64, NT], BF16)
        lw_w = B.take([64, 1024], BF16)
        lw_a = B.take([64, 1024], BF16)
        k.memset(walo[:, 0:1], 0.0, ["walo"])
        k.dma("pool", lw_w, w_du[l], "l_lw", w=["lw"])
        k.dma("pool", lw_a, w_au[l], "l_lw", w=["lw"])
        for which in range(2):
            k.dma("sp", walo[:, 1:], projT[6144 + which * 64:6208 + which * 64, :], "l_walo", r=["projT"], w=["walo"])
            mucol = cols[l][0:64, (CMU_WA if which == 0 else CMU_A):(CMU_WA if which == 0 else CMU_A) + 1]
            for tb in range(4):
                sl = slice(tb * 512, (tb + 1) * 512)
                k.tt(pfd, walo[:, tb * 512:tb * 512 + 512], walo[:, 1 + tb * 512:1 + tb * 512 + 512], ALU.subtract,
                     ["walo"], ["pfd"])
                k.stt(pft, pfd, mucol, walo[:, 1 + tb * 512:1 + tb * 512 + 512], ALU.mult, ALU.add,
                      ["pfd", ck, "walo"], ["pft"])
                if which == 0:
                    k.act(tw_w[:, sl], pft, AF.Tanh, ["pft"], ["tw_w"])
                else:
                    k.cp(tw_a[:, sl], pft, ["pft"], ["tw_a"])
        pi = 0
        for p in range(8):
            for which in range(2):
                lwx, twx, twk = (lw_w, tw_w, "tw_w") if which == 0 else (lw_a, tw_a, "tw_a")
                for tb in range(4):
                    bi = pi % 4
                    pi += 1
                    k.mm(pb[bi], lwx[:, p * 128:(p + 1) * 128], twx[:, tb * 512:(tb + 1) * 512], True, True, ["lw", twk], ["pb%d" % bi])
                    k.cp(pstg[bi], pb[bi], ["pb%d" % bi], ["pstg%d" % bi], eng=("act" if bi % 2 else "dve"))
                    k.dma("sp", lorad[which, p * 128:(p + 1) * 128, tb * 512:(tb + 1) * 512], pstg[bi], "s_pst%d" % bi,
                          r=["pstg%d" % bi], w=["lorad"])
        P.barrier()
        A.reset()
        B.reset()
        W = 256
        NB = NT // W
        NCK = W // 64
        WS = []
        for s_ in range(2):
            ws = {}
            for nm in ("r", "k", "v", "z"):
                ws["raw_" + nm] = A.take([128, W + 1], F32)
            for nm in ("f_r", "f_k", "f_v", "f_z", "f_d", "f_sg", "f_a", "f_t1", "f_kkn", "f_kmod", "f_ka",
                       "f_S", "f_Sp", "f_Se", "f_Wt", "f_Wi", "f_Wp", "f_Wc", "f_bonus"):
                ws[nm] = A.take([128, W], F32)
            ws["f_WC"] = A.take([128, NCK], F32)
            ws["mZ"] = A.take([64, 512], F32)
            ws["mX"] = A.take([64, 512], BF16)
            ws["mIM"] = A.take([64, 512], F32)
            for nm in ("b_x", "b_at", "b_rt", "b_kh", "b_bh", "b_v", "b_atp"):
                ws[nm] = B.take([128, W], BF16)
            ws["b_bt"] = [B.take([128, W], BF16) for _ in range(2)]
            ws["b_kt"] = [B.take([128, W], BF16) for _ in range(2)]
            ws["a_tok"] = B.take([64, NCK, 128], BF16)
            ws["PTb"] = B.take([64, 512], BF16)
            ws["kh_tok"] = B.take([64, NCK, 256], BF16)
            ws["bh_tok"] = B.take([64, NCK, 256], BF16)
            ws["Vpad"] = B.take([64, NCK, 256], BF16)
            ws["Upad"] = B.take([64, 256], BF16)
            ws["mN"] = [B.take([64, 512], F32) for _ in range(2)]
            ws["mM"] = [B.take([64, 512], F32) for _ in range(2)]
            ws["mP"] = [B.take([64, 512], F32) for _ in range(2)]
            for nm in ("mAK", "mRB", "mRK"):
                ws[nm] = B.take([64, 512], BF16)
            WS.append(ws)
        wmvu = A.take([32, 1024], BF16)
        SLOTTED = set(["raw_r", "raw_k", "raw_v", "raw_z", "f_r", "f_k", "f_v", "f_z", "f_d", "f_sg", "f_a", "f_t1",
                       "f_kkn", "f_kmod", "f_ka", "f_S", "f_Sp", "f_Se", "f_Wt", "f_Wi", "f_Wp", "f_Wc", "f_bonus",
                       "f_WC", "b_x", "b_at", "b_rt", "b_kh", "b_bh", "b_v", "b_bt", "b_kt", "a_tok", "kh_tok",
                       "bh_tok", "Vpad", "Upad", "b_atp", "mZ", "mN0", "mN1", "mM0", "mM1", "mP0", "mP1", "mIM",
                       "mAK", "mRB", "mRK", "mX", "vfT", "PTb"])

        class KS:
            def __init__(self, slot):
                self.slot = slot

            def __getattr__(self, name):
                f = getattr(k, name)
                slot = self.slot

                def mapk(a):
                    if isinstance(a, (list, tuple)) and len(a) > 0 and all(isinstance(x_, str) for x_ in a):
                        return [(x_ + "@%d" % slot) if x_ in SLOTTED else x_ for x_ in a]
                    return a

                def wrapped(*args, **kw):
                    return f(*[mapk(a_) for a_ in args], **{kk_: mapk(v_) for kk_, v_ in kw.items()})
                return wrapped

        k.ts1(negc, cols[l][:, CW0:CW0 + 24], -1.0, ALU.mult, [ck], ["negc"])
        for s_ in range(2):
            ks0 = KS(s_)
            ks0.memset(WS[s_]["Upad"], 0.0, ["Upad"])
            ks0.memset(WS[s_]["Vpad"], 0.0, ["Vpad"])
            ks0.memset(WS[s_]["kh_tok"], 0.0, ["kh_tok"])
            ks0.memset(WS[s_]["bh_tok"], 0.0, ["bh_tok"])
            for h_ in range(2):
                ks0.memset(WS[s_]["b_bt"][h_], 0.0, ["b_bt"])
                ks0.memset(WS[s_]["b_kt"][h_], 0.0, ["b_kt"])
        if l == 1:
            k.dma("pool", wmvu, w_mvu[0], "l_wmvu", w=["wmvu"])
        stage_end(l, "S4a")
        for p in range(8):
            k.memset(Hf[p], 0.0, ["Hf%d" % p])
            k.memset(Hb[p], 0.0, ["Hb%d" % p])

        def hr(h):
            return slice(h * 64, (h + 1) * 64)

        def us(u):
            return slice(u * 64, (u + 1) * 64)

        def tk(cc):
            return slice(cc * 64, (cc + 1) * 64)

        def it_gen(p, tb, slot):
            ws = WS[slot]
            kk_ = KS(slot)
            hk, hbk = "Hf%d" % p, "Hb%d" % p
            t0 = tb * W
            raw = {nm: ws["raw_" + nm] for nm in ("r", "k", "v", "z")}
            f_r, f_k, f_v, f_z, f_d, f_sg, f_a, f_t1 = (ws[n_] for n_ in ("f_r", "f_k", "f_v", "f_z", "f_d", "f_sg", "f_a", "f_t1"))
            f_kkn, f_kmod, f_ka, f_S, f_Sp, f_Se = (ws[n_] for n_ in ("f_kkn", "f_kmod", "f_ka", "f_S", "f_Sp", "f_Se"))
            f_Wt, f_Wi, f_Wp, f_Wc, f_bonus, f_WC = (ws[n_] for n_ in ("f_Wt", "f_Wi", "f_Wp", "f_Wc", "f_bonus", "f_WC"))
            f_vf, f_g, f_kk, f_y = f_S, f_Sp, f_Se, f_Wp
            b_x, b_at, b_rt, b_kh, b_bh, b_v = (ws[n_] for n_ in ("b_x", "b_at", "b_rt", "b_kh", "b_bh", "b_v"))
            b_bt, b_kt = ws["b_bt"], ws["b_kt"]
            a_tok, kh_tok, bh_tok, Vpad = ws["a_tok"], ws["kh_tok"], ws["bh_tok"], ws["Vpad"]
            Upad, b_atp, mZ = ws["Upad"], ws["b_atp"], ws["mZ"]
            mN, mM, mP = ws["mN"], ws["mM"], ws["mP"]
            mIM, mAK, mRB, mRK, mX = (ws[n_] for n_ in ("mIM", "mAK", "mRB", "mRK", "mX"))
            PTb = ws["PTb"]
            X0, X1, X2 = pb[3 * slot], pb[3 * slot + 1], pb[3 * slot + 2]
            K0, K1, K2 = "pb%d" % (3 * slot), "pb%d" % (3 * slot + 1), "pb%d" % (3 * slot + 2)
            ptb, ptk = pT[slot], "pT%d" % slot
            fx = {"r": f_r, "k": f_k, "v": f_v, "z": f_z}
            mu0 = {"r": CMU_R, "k": CMU_K, "v": CMU_V, "z": CMU_Z}
            for wi, nm in enumerate(("r", "k", "v", "z")):
                row0 = 2048 + wi * 1024 + p * 128
                rb, rk_ = raw[nm], "raw_" + nm
                sem = "l_raw%s%d" % (nm, slot)
                if tb == 0:
                    kk_.memset(rb[:, 0:1], 0.0, [rk_])
                    kk_.dma("sp", rb[:, 1:W + 1], projT[row0:row0 + 128, 0:W], sem, r=["projT"], w=[rk_])
                else:
                    kk_.dma("sp", rb[:, 0:W + 1], projT[row0:row0 + 128, t0 - 1:t0 + W], sem, r=["projT"], w=[rk_])
            for wi, nm in enumerate(("r", "k", "v", "z")):
                rb, rk_ = raw[nm], "raw_" + nm
                kk_.tt(f_d, rb[:, 0:W], rb[:, 1:W + 1], ALU.subtract, [rk_], ["f_d"])
                kk_.stt(fx[nm], f_d, col(l, mu0[nm] + p), rb[:, 1:W + 1], ALU.mult, ALU.add, ["f_d", ck, rk_], ["f_" + nm])
            yield "prep"
            if l == 0:
                kk_.dma("sp", vfT[p * 128:(p + 1) * 128, t0:t0 + W], f_v, "s_vf%d" % slot, r=["f_v"], w=["vfT"])
            else:
                kk_.dma("sp", f_vf, vfT[p * 128:(p + 1) * 128, t0:t0 + W], "l_vf%d" % slot, r=["vfT"], w=["f_S"])
                kk_.mm(X1[:, 0:W], wmvu[0:32, p * 128:(p + 1) * 128], mvs[0:32, t0:t0 + W], True, True, ["wmvu", "mvs"], [K1])
                kk_.act(f_g, X1[:, 0:W], AF.Exp, [K1, "negc"], ["f_Sp"], bias=negc[:, 16 + p:17 + p], scale=-1.0)
                kk_.act(f_g, f_g, AF.Ln, ["f_Sp", "onec"], ["f_Sp"], bias=onec)
                kk_.act(f_g, f_g, AF.Exp, ["f_Sp"], ["f_Sp"], scale=-1.0)
                kk_.tt(f_d, f_vf, f_v, ALU.subtract, ["f_S", "f_v"], ["f_d"])
                kk_.tt(f_d, f_d, f_g, ALU.mult, ["f_d", "f_Sp"], ["f_d"])
                kk_.tt(f_v, f_v, f_d, ALU.add, ["f_v", "f_d"], ["f_v"])
            kk_.dma("sp", f_sg, lorad[0, p * 128:(p + 1) * 128, t0:t0 + W], "l_sg%d" % slot, r=["lorad"], w=["f_sg"])
            kk_.dma("sp", f_a, lorad[1, p * 128:(p + 1) * 128, t0:t0 + W], "l_fa%d" % slot, r=["lorad"], w=["f_a"])
            kk_.act(f_sg, f_sg, AF.Exp, ["f_sg", "negc"], ["f_sg"], bias=negc[:, p:p + 1], scale=-1.0)
            kk_.act(f_a, f_a, AF.Exp, ["f_a", "negc"], ["f_a"], bias=negc[:, 8 + p:9 + p], scale=-1.0)
            kk_.act(f_sg, f_sg, AF.Ln, ["f_sg", "onec"], ["f_sg"], bias=onec)
            kk_.act(f_a, f_a, AF.Ln, ["f_a", "onec"], ["f_a"], bias=onec)
            kk_.act(f_sg, f_sg, AF.Exp, ["f_sg"], ["f_sg"], scale=-1.0)
            kk_.act(f_a, f_a, AF.Exp, ["f_a"], ["f_a"], scale=-1.0)
            yield "prep"
            kk_.ts1(f_kk, f_k, col(l, CKK + p), ALU.mult, ["f_k", ck], ["f_Se"])
            kk_.tt(b_x, f_kk, f_kk, ALU.mult, ["f_Se"], ["b_x"])
            kk_.mm(X1[:, 0:W], bd1, b_x, True, True, ["bd1", "b_x"], [K1])
            kk_.ts1(f_t1, X1[:, 0:W], 1e-24, ALU.max, [K1], ["f_t1"])
            kk_.act(f_t1, f_t1, AF.Ln, ["f_t1"], ["f_t1"])
            kk_.act(f_t1, f_t1, AF.Exp, ["f_t1"], ["f_t1"], scale=-0.5)
            kk_.tt(f_kkn, f_kk, f_t1, ALU.mult, ["f_Se", "f_t1"], ["f_kkn"])
            kk_.ts2(f_t1, f_a, col(l, CKA + p), omk[:, p:p + 1], ALU.mult, ALU.add, ["f_a", ck, "omk"], ["f_t1"])
            kk_.tt(f_kmod, f_t1, f_k, ALU.mult, ["f_t1", "f_k"], ["f_kmod"])
            kk_.tt(f_ka, f_kkn, f_a, ALU.mult, ["f_kkn", "f_a"], ["f_ka"])
            yield "prep"
            kk_.tt(f_t1, f_r, f_kmod, ALU.mult, ["f_r", "f_kmod"], ["f_t1"])
            kk_.ts1(b_x, f_t1, col(l, CRK + p), ALU.mult, ["f_t1", ck], ["b_x"])
            kk_.mm(X2[:, 0:W], bd1, b_x, True, True, ["bd1", "b_x"], [K2])
            kk_.tt(f_bonus, X2[:, 0:W], f_v, ALU.mult, [K2, "f_v"], ["f_bonus"])
            kk_.scan(f_S, smask[:, 0:W], f_sg, ["smask", "f_sg"], ["f_S"])
            kk_.tt(f_Sp, f_S, f_sg, ALU.subtract, ["f_S", "f_sg"], ["f_Sp"])
            S3 = f_S.rearrange("p (c j) -> p c j", j=64)
            kk_.tt(f_Se.rearrange("p (c j) -> p c j", j=64), S3[:, :, 63:64].to_broadcast([128, NCK, 64]), S3, ALU.subtract,
                   ["f_S"], ["f_Se"])
            kk_.act(f_Wt, f_S, AF.Exp, ["f_S"], ["f_Wt"], scale=-C0)
            kk_.act(f_Wi, f_S, AF.Exp, ["f_S"], ["f_Wi"], scale=C0)
            kk_.act(f_Wp, f_Sp, AF.Exp, ["f_Sp"], ["f_Wp"], scale=-C0)
            kk_.act(f_Wc, f_Se, AF.Exp, ["f_Se"], ["f_Wc"], scale=-C0)
            kk_.act(f_WC, S3[:, :, 63], AF.Exp, ["f_S"], ["f_WC"], scale=-C0)
            yield "prep"
            kk_.stt(b_at, f_kkn, -1.0, f_Wp, ALU.mult, ALU.mult, ["f_kkn", "f_Wp"], ["b_at"])
            for h_ in range(2):
                hs_ = slice(h_ * 64, (h_ + 1) * 64)
                kk_.tt(b_bt[h_][hs_, :], f_ka[hs_, :], f_Wi[hs_, :], ALU.mult, ["f_ka", "f_Wi"], ["b_bt"])
                kk_.tt(b_kt[h_][hs_, :], f_kmod[hs_, :], f_Wi[hs_, :], ALU.mult, ["f_kmod", "f_Wi"], ["b_kt"])
            kk_.tt(b_rt, f_r, f_Wt, ALU.mult, ["f_r", "f_Wt"], ["b_rt"])
            kk_.tt(b_kh, f_kmod, f_Wc, ALU.mult, ["f_kmod", "f_Wc"], ["b_kh"])
            kk_.tt(b_bh, f_ka, f_Wc, ALU.mult, ["f_ka", "f_Wc"], ["b_bh"])
            kk_.cp(b_v, f_v, ["f_v"], ["b_v"], eng="act")
            yield "prep"
            for ti, (src, sk, dst, dk) in enumerate(((b_at, "b_at", a_tok, "a_tok"), (b_kh, "b_kh", kh_tok, "kh_tok"),
                                                     (b_bh, "b_bh", bh_tok, "bh_tok"), (b_v, "b_v", None, "Vpad"))):
                pt = ptb[:, (ti % 2) * 512:(ti % 2) * 512 + 512]
                pk = ptk
                for c in range(NCK):
                    kk_.tr(pt[0:64, c * 128:(c + 1) * 128], src[:, c * 64:(c + 1) * 64], ident, [sk, "ident"], [pk])
                if ti == 0:
                    kk_.cp(dst.rearrange("p c n -> p (c n)"), pt[0:64, 0:NCK * 128], [pk], [dk], eng="dve")
                else:
                    dpad = Vpad if dst is None else dst
                    o4 = dpad.rearrange("p c (h x) -> p c h x", x=128)[:, :, :, 0:64]
                    i4 = pt[0:64, 0:NCK * 128].rearrange("p (c h k) -> p c h k", h=2, k=64)
                    kk_.cp(o4, i4, [pk], [dk], eng=("act" if ti % 2 else "dve"))
            yield "prep_done"
            def vb(buf):
                return buf.bitcast(BF16)[:, 0:512]

            units = [(h, cc) for h in range(2) for cc in range(4)]
            for u, (h, cc) in enumerate(units):
                kk_.mm(X0[0:64, us(u)], b_bt[h][:, tk(cc)], b_at[:, tk(cc)], True, True, ["b_bt", "b_at"], [K0])
                kk_.mm(X1[0:64, us(u)], b_at[:, tk(cc)], b_bt[h][:, tk(cc)], True, True, ["b_bt", "b_at"], [K1])
            for u, (h, cc) in enumerate(units):
                kk_.mm(X2[0:64, us(u)], b_kt[h][:, tk(cc)], b_at[:, tk(cc)], True, True, ["b_kt", "b_at"], [K2])
            kk_.tt(vb(mN[0]), X0[0:64, :], m_su, ALU.mult, [K0, "m_su"], ["mN0"])
            kk_.tt(vb(mM[0]), X1[0:64, :], m_sl, ALU.mult, [K1, "m_sl"], ["mM0"])
            yield "ph"
            for u, (h, cc) in enumerate(units):
                kk_.mm(X0[0:64, us(u)], b_bt[h][:, tk(cc)], b_rt[:, tk(cc)], True, True, ["b_bt", "b_rt"], [K0])
                kk_.mm(X1[0:64, us(u)], b_kt[h][:, tk(cc)], b_rt[:, tk(cc)], True, True, ["b_kt", "b_rt"], [K1])
            kk_.tt(mAK, X2[0:64, :], m_su, ALU.mult, [K2, "m_su"], ["mAK"])
            kk_.tt(mP[0], vb(mN[0]), identx, ALU.add, ["mN0", "identx"], ["mP0"])
            kk_.tt(mRB, X0[0:64, :], m_u, ALU.mult, [K0, "m_u"], ["mRB"])
            kk_.tt(mRK, X1[0:64, :], m_u, ALU.mult, [K1, "m_u"], ["mRK"])
            yield "ph"
            def p_step(i):
                a_, b_ = (i - 1) % 2, i % 2
                im = vb(mIM) if i >= 4 else mIM
                pin = vb(mP[a_]) if i >= 4 else mP[a_]
                for u in range(8):
                    kk_.mm(X2[0:64, us(u)], im[:, us(u)], pin[:, us(u)], True, True, ["mIM", "mP%d" % a_], [K2])
                if i == 5:
                    kk_.cp(PTb, X2[0:64, :], [K2], ["PTb"], eng="act")
                elif i in (3, 4):
                    kk_.cp(vb(mP[b_]), X2[0:64, :], [K2], ["mP%d" % b_], eng="act")
                else:
                    kk_.cp(mP[b_], X2[0:64, :], [K2], ["mP%d" % b_], eng="act")

            for i in range(1, 6):
                a_, b_ = (i - 1) % 2, i % 2
                nin = vb(mN[a_]) if i in (1, 4, 5) else mN[a_]
                min_ = vb(mM[a_]) if i in (1, 4, 5) else mM[a_]
                for u in range(8):
                    kk_.mm(X0[0:64, us(u)], nin[:, us(u)], min_[:, us(u)], True, True,
                           ["mN%d" % a_, "mM%d" % a_], [K0])
                if i < 5:
                    for u in range(8):
                        kk_.mm(X1[0:64, us(u)], min_[:, us(u)], nin[:, us(u)], True, True,
                               ["mN%d" % a_, "mM%d" % a_], [K1])
                if i > 1:
                    p_step(i - 1)
                if i < 5:
                    mo = vb(mM[b_]) if i in (3, 4) else mM[b_]
                    no = vb(mN[b_]) if i in (3, 4) else mN[b_]
                    kk_.cp(mo, X0[0:64, :], [K0], ["mM%d" % b_], eng="act")
                    kk_.cp(no, X1[0:64, :], [K1], ["mN%d" % b_], eng="dve")
                kk_.tt(vb(mIM) if i >= 4 else mIM, X0[0:64, :], identx, ALU.add, [K0, "identx"], ["mIM"])
                yield "ph"
            p_step(5)
            PT, PTk = PTb, "PTb"
            for u, (h, cc) in enumerate(units):
                kk_.mm(X0[0:64, us(u)], mAK[:, us(u)], Vpad[:, cc, h * 128:h * 128 + 64], True, True, ["mAK", "Vpad"], [K0])
            kk_.cp(mX, X0[0:64, :], [K0], ["mX"], eng="dve")
            for u, (h, cc) in enumerate(units):
                kk_.mm(X2[:, us(u)], a_tok[:, cc, :], PT[:, us(u)], True, True, [PTk, "a_tok"], [K2])
            for u, (h, cc) in enumerate(units):
                kk_.mm(X1[0:64, us(u)], PT[:, us(u)], mX[:, us(u)], True, True, [PTk, "mX"], [K1])
            kk_.cp(b_atp[0:64, :], X2[0:64, 0:256], [K2], ["b_atp"], eng="dve")
            kk_.cp(b_atp[64:128, :], X2[64:128, 256:512], [K2], ["b_atp"], eng="act")
            kk_.cp(mZ, X1[0:64, :], [K1], ["mZ"], eng="act")
            yield "ph"
            U3 = Upad.rearrange("p (h x) -> p h x", x=128)[:, :, 0:64]
            for cc in range(4):
                kk_.mm(X0[0:64, 0:128], b_atp[:, tk(cc)], Hb[p], True, True, ["b_atp", hbk], [K0])
                z3 = mZ.rearrange("p (h c v) -> p h c v", h=2, v=64)[:, :, cc, :]
                kk_.tt(U3, X0[0:64, 0:128].rearrange("p (h v) -> p h v", v=64), z3, ALU.add, [K0, "mZ"], ["Upad"])
                kk_.mm(X1[:, 0:64], Hb[p], b_rt[:, tk(cc)], True, False, [hbk, "b_rt"], [K1])
                for h in range(2):
                    u = h * 4 + cc
                    kk_.mm(X1[:, 0:64], Upad[:, h * 64:h * 64 + 128], mRB[:, us(u)], False, False, ["Upad", "mRB"], [K1])
                    kk_.mm(X1[:, 0:64], Vpad[:, cc, h * 64:h * 64 + 128], mRK[:, us(u)], False, h == 1,
                           ["Vpad", "mRK"], [K1])
                for h in range(2):
                    kk_.mm(X2[:, h * 64:(h + 1) * 64], bh_tok[:, cc, h * 64:h * 64 + 128], Upad[:, h * 128:h * 128 + 64], True, False,
                           ["bh_tok", "Upad"], [K2])
                    kk_.mm(X2[:, h * 64:(h + 1) * 64], kh_tok[:, cc, h * 64:h * 64 + 128], Vpad[:, cc, h * 128:h * 128 + 64], False, True,
                           ["kh_tok", "Vpad"], [K2])
                kk_.stt(Hb[p], Hf[p], f_WC[:, cc:cc + 1], X2[:, 0:128], ALU.mult, ALU.add, [hk, "f_WC", K2], [hbk])
                kk_.stt(Hf[p], Hf[p], f_WC[:, cc:cc + 1], X2[:, 0:128], ALU.mult, ALU.add, [hk, "f_WC", K2], [hk])
                kk_.cp(f_y[:, tk(cc)], X1[:, 0:64], [K1], ["f_Wp"], eng="act")
                yield "ph"
            kk_.cp(b_x, f_y, ["f_Wp"], ["b_x"], eng="dve")
            kk_.mm(X0[:, 0:W], bdm, b_x, True, True, ["bdm", "b_x"], [K0])
            kk_.tt(f_d, f_y, X0[:, 0:W], ALU.subtract, ["f_Wp", K0], ["f_d"])
            kk_.tt(b_x, f_d, f_d, ALU.mult, ["f_d"], ["b_x"])
            kk_.mm(X0[:, W:2 * W], bdm, b_x, True, True, ["bdm", "b_x"], [K0])
            kk_.act(f_t1, X0[:, W:2 * W], AF.Ln, [K0, "epsl"], ["f_t1"], bias=epsl)
            kk_.act(f_t1, f_t1, AF.Exp, ["f_t1"], ["f_t1"], scale=-0.5)
            yield "ph"
            kk_.tt(f_d, f_d, f_t1, ALU.mult, ["f_d", "f_t1"], ["f_d"])
            kk_.ts2(f_d, f_d, col(l, CLG + p), col(l, CLB + p), ALU.mult, ALU.add, ["f_d", ck], ["f_d"])
            kk_.tt(f_d, f_d, f_bonus, ALU.add, ["f_d", "f_bonus"], ["f_d"])
            kk_.act(f_t1, f_z, AF.Exp, ["f_z"], ["f_t1"], scale=-1.0)
            kk_.act(f_t1, f_t1, AF.Ln, ["f_t1", "onec"], ["f_t1"], bias=onec)
            kk_.act(f_t1, f_t1, AF.Exp, ["f_t1"], ["f_t1"], scale=-1.0)
            kk_.tt(f_d, f_d, f_z, ALU.mult, ["f_d", "f_z"], ["f_d"])
            kk_.tt(ybT[:, p, t0:t0 + W], f_d, f_t1, ALU.mult, ["f_d", "f_t1"], ["ybT%d" % p])

        def chain(st):
            for p in range(st, 8, 2):
                for tb in range(NB):
                    for tag in it_gen(p, tb, st):
                        yield tag

        lists = []
        for st in range(2):
            P._defer = []
            for _ in chain(st):
                pass
            lists.append(P._defer)
            P._defer = None
        P.merge_streams(lists)

        if DEBUG_OUT:
            P.barrier()
            k.dma("sp", dbg_big[l], big, "s_dbg", r=["yaT", "ybT"])
        stage_end(l, "S4")
        P.barrier()
        A.reset()
        B.reset()
        mrgT = A.take([128, 16, NT], BF16)
        sga = B.take([128, NT], BF16)
        sgb = B.take([128, NT], BF16)
        wa_g = B.take([128, 8, 512], BF16)
        wb_g = B.take([128, 8, 512], BF16)
        m1 = [B.take([128, 512], F32) for _ in range(2)]
        m2 = [B.take([128, 512], F32) for _ in range(2)]
        for cgp in range(4):
            va = w_bra[l].rearrange("(kt p) n -> p kt n", p=128)
            vb = w_brb[l].rearrange("(kt p) n -> p kt n", p=128)
            for q2 in range(2):
                k.dma("pool", wa_g[:, q2 * 4:(q2 + 1) * 4, :], va[:, q2 * 4:(q2 + 1) * 4, cgp * 512:(cgp + 1) * 512], "l_wag", w=["wa_g"])
                k.dma("pool", wb_g[:, q2 * 4:(q2 + 1) * 4, :], vb[:, q2 * 4:(q2 + 1) * 4, cgp * 512:(cgp + 1) * 512], "l_wbg", w=["wb_g"])
            for cb in range(4):
                c = cgp * 4 + cb
                k.dma("sp", sga, sgT[c * 128:(c + 1) * 128, :], "l_sga", r=["sgT"], w=["sga"])
                k.dma("sp", sgb, sgT[D + c * 128:D + (c + 1) * 128, :], "l_sgb", r=["sgT"], w=["sgb"])
                for tb in range(4):
                    sl = slice(tb * 512, (tb + 1) * 512)
                    for kt in range(8):
                        k.mm(pb[0], wa_g[:, kt, cb * 128:(cb + 1) * 128], yaT[:, kt, sl], kt == 0, kt == 7, ["wa_g", "yaT"], ["pb0"])
                    for kt in range(8):
                        k.mm(pb[1], wb_g[:, kt, cb * 128:(cb + 1) * 128], ybT[:, kt, sl], kt == 0, kt == 7, ["wb_g", "ybT"], ["pb1"])
                    i2 = tb % 2
                    k.tt(m1[i2], pb[0], sga[:, sl], ALU.mult, ["pb0", "sga"], ["m1_%d" % i2])
                    k.tt(m2[i2], pb[1], sgb[:, sl], ALU.mult, ["pb1", "sgb"], ["m2_%d" % i2])
                    k.tt(mrgT[:, c, sl], m1[i2], m2[i2], ALU.add, ["m1_%d" % i2, "m2_%d" % i2], ["mrgT"])

        stage_end(l, "S5a")
        P.barrier()
        B.reset()
        xt = [B.take([128, D], F32) for _ in range(2)]
        osb = B.take([128, D], F32)
        junk = B.take([128, D], F32)
        gg = B.take([128, D], F32)
        k.dma("sp", gg, ggd[0:1, :].partition_broadcast(128), "l_gg", r=["ggd"], w=["gg"])
        vo = w_out[l].rearrange("(kt p) n -> p kt n", p=128)
        for q4 in range(4):
            k.dma("pool", big[:, q4 * 4:(q4 + 1) * 4, :], vo[:, q4 * 4:(q4 + 1) * 4, :], "l_wout", w=["hT", "yaT", "ybT", "wout"])
        for tt_ in range(16):
            xs = xt[tt_ % 2]
            xk = "xt%d" % (tt_ % 2)
            k.dma("sp", xs, xsrc[tt_ * 128:(tt_ + 1) * 128, :], "l_" + xk, w=[xk])
            for n in range(4):
                for kt in range(NKT):
                    k.mm(pb[n], mrgT[:, kt, tt_ * 128:(tt_ + 1) * 128], big[:, kt, n * 512:(n + 1) * 512], kt == 0, kt == NKT - 1,
                         ["mrgT", "wout"], ["pb%d" % n])
                k.cp(osb[:, n * 512:(n + 1) * 512], pb[n], ["pb%d" % n], ["osb"], eng=("act" if n % 2 else "dve"))
            k.act(junk, osb, AF.Square, ["osb"], ["junk", "small"], accum=small[:, tt_:tt_ + 1])
            k.act(small[:, 16 + tt_:17 + tt_], small[:, tt_:tt_ + 1], AF.Sqrt, ["small", "epsr"], ["small"], bias=epsr, scale=1.0 / D)
            k.recip(small[:, 32 + tt_:33 + tt_], small[:, 16 + tt_:17 + tt_], ["small"], ["small"])
            k.stt(osb, osb, small[:, 32 + tt_:33 + tt_], gg, ALU.mult, ALU.mult, ["osb", "small", "gg"], ["osb"])
            k.tt(osb, osb, xs, ALU.add, ["osb", xk], ["osb"])
            k.dma("sp", xdst[tt_ * 128:(tt_ + 1) * 128, :], osb, "s_out", r=["osb"], w=["xmid"])
        k.memset(small[:, 0:16], 0.0, ["small"])
        P.barrier()

    try:
        for l in range(2):
            build_layer(l)
    except StopBuild:
        pass
    P.emit()
    return nc


_NC_CACHE = {}


def _pack_cols(inp, l):
    c = np.zeros((128, NCOLS), np.float32)

    def put(c0, vec):
        v = np.asarray(vec, np.float32).reshape(-1)
        n = v.shape[0] // 128
        c[:, c0:c0 + n] = v.reshape(n, 128).T
    put(CG_PRE, inp["g_pre"][l])
    put(CPSC, inp["pool_scale"][l])
    mu = np.asarray(inp["mu_shift"][l], np.float32)
    put(CMU_R, mu[0:1024])
    put(CMU_K, mu[1024:2048])
    put(CMU_V, mu[2048:3072])
    put(CMU_Z, mu[3072:4096])
    c[0:64, CMU_WA] = mu[4096:4160]
    c[0:64, CMU_A] = mu[4160:4224]
    put(CW0, inp["w0"][l])
    put(CA0, inp["a0"][l])
    if l >= 1:
        put(CMV0, inp["mv0"][l - 1])
        c[0:32, CMU_MV] = np.asarray(inp["mu_mv"][l - 1], np.float32)
    put(CKK, inp["k_k"][l])
    put(CKA, inp["k_a"][l])
    put(CRK, np.asarray(inp["r_k"][l], np.float32).reshape(-1))
    put(CLG, inp["lnx_g"][l])
    put(CLB, inp["lnx_b"][l])
    return c


def kernel(**inputs):
    inp = {k_: np.asarray(v) for k_, v in inputs.items()}
    if "nc" not in _NC_CACHE:
        _NC_CACHE["nc"] = build_program()
    nc = _NC_CACHE["nc"]
    cols = np.stack([_pack_cols(inp, 0), _pack_cols(inp, 1)], axis=0)
    shared = {
        "w_ada": np.ascontiguousarray(inp["w_ada"], np.float32),
        "b_ada": np.ascontiguousarray(inp["b_ada"], np.float32),
        "w_in": np.ascontiguousarray(inp["w_in"], np.float32),
        "w_pool": np.ascontiguousarray(inp["w_pool"], np.float32),
        "w_decay_up": np.ascontiguousarray(inp["w_decay_up"], np.float32),
        "w_aaa_up": np.ascontiguousarray(inp["w_aaa_up"], np.float32),
        "w_mv_down": np.ascontiguousarray(inp["w_mv_down"], np.float32),
        "w_mv_up": np.ascontiguousarray(inp["w_mv_up"], np.float32),
        "w_br_a": np.ascontiguousarray(inp["w_br_a"], np.float32),
        "w_br_b": np.ascontiguousarray(inp["w_br_b"], np.float32),
        "w_out": np.ascontiguousarray(inp["w_out"], np.float32),
        "g_post": np.ascontiguousarray(inp["g_post"], np.float32),
        "cols": cols,
    }
    in_maps = []
    for core in range(8):
        b = core % 4
        m = dict(shared)
        m["x"] = np.ascontiguousarray(inp["x"][b], np.float32)
        m["cT"] = np.ascontiguousarray(np.asarray(inp["c"][b], np.float32).reshape(NKT, 128).T)
        in_maps.append(m)
    res = run_bass_kernel_spmd(nc, in_maps, core_ids=list(range(8)))
    out = np.stack([np.asarray(res.results[b]["y"], np.float32) for b in range(4)], axis=0)
    return out
```

```python
import numpy as np
import concourse.bass as bass
import concourse.mybir as mybir
from concourse.bass_utils import run_bass_kernel_spmd

F32 = mybir.dt.float32
BF16 = mybir.dt.bfloat16
AF = mybir.ActivationFunctionType
ALU = mybir.AluOpType

ENGS = ("pe", "act", "dve", "pool", "sp")
EPOCH = 30000

D = 2048
NT = 2048
NKT = 16
NIN = 10368
NPROJ = 6272
C0 = float(np.exp(-0.5))
RMS_EPS = 1e-6
LNX_EPS = 64e-5

CG_PRE, CPSC, CMU_R, CMU_K, CMU_V, CMU_Z, CMU_WA = 0, 16, 24, 32, 40, 48, 56
CW0, CA0, CMV0, CKK, CKA, CRK, CLG, CLB, CMU_MV, CMU_A = 57, 65, 73, 81, 89, 97, 105, 113, 121, 122
NCOLS = 123


class Prog:
    def __init__(self, nc):
        self.nc = nc
        self.q = {e: [] for e in ENGS}
        self.cnt = {e: 0 for e in ENGS}
        self.seen = {e: {} for e in ENGS}
        self.vc = {}
        self.last_w = {}
        self.readers = {}
        self.sems = {}
        self.dma_cnt = {}
        self.waited = {e: set() for e in ENGS}
        self._stack = []
        self._defer = None

    def _sem(self, key):
        if key not in self.sems:
            cm = self.nc.semaphore("s_%s_%s" % (key[0], key[1]))
            h = cm.__enter__()
            self._stack.append(cm)
            self.sems[key] = h
        return self.sems[key]

    def _note(self, waits):
        for (k_, v) in waits:
            if k_ in self.waited:
                self.waited[k_].add(v)

    def _deps(self, eng, reads, writes, skipkey=None):
        toks = []
        for r in reads:
            t = self.last_w.get(r)
            if t is not None:
                toks.append(t)
        for w in writes:
            t = self.last_w.get(w)
            if t is not None:
                toks.append(t)
            toks.extend(self.readers.get(w, ()))
        need = {}
        for (k, v) in toks:
            if eng == "pe" and k == "pe":
                continue
            if skipkey is not None and k == skipkey:
                continue
            if self.seen[eng].get(k, 0) >= v:
                continue
            if need.get(k, 0) < v:
                need[k] = v
        waits = []
        items = sorted(need.items(), key=lambda kv: -len(self.vc.get((kv[0], kv[1]), ())))
        for k, v in items:
            if self.seen[eng].get(k, 0) >= v:
                continue
            waits.append((k, v))
            self.seen[eng][k] = v
            for k2, v2 in self.vc.get((k, v), {}).items():
                if self.seen[eng].get(k2, 0) < v2:
                    self.seen[eng][k2] = v2
        self._note(waits)
        return waits

    def _finish(self, tok, eng, reads, writes):
        c = dict(self.seen[eng])
        k, v = tok
        c[k] = v
        self.vc[tok] = c
        for r in reads:
            self.readers.setdefault(r, []).append(tok)
        for w in writes:
            self.last_w[w] = tok
            self.readers[w] = []

    def op(self, eng, fn, reads=(), writes=(), cost=0.3):
        if self._defer is not None:
            self._defer.append(("op", eng, fn, tuple(reads), tuple(writes), cost))
            return None
        ps_r = [r for r in reads if r[:2] in ("pb", "pT")]
        if ps_r:
            reads = [r for r in reads if r[:2] not in ("pb", "pT")]
            writes = list(writes) + ps_r
        waits = self._deps(eng, reads, writes)
        self.cnt[eng] += 1
        n = self.cnt[eng]
        tok = (eng, n)
        self.q[eng].append((waits, fn, eng, n))
        self._finish(tok, eng, reads, writes)
        return tok

    def dma(self, eng, out, in_, semname, reads=(), writes=()):
        if self._defer is not None:
            self._defer.append(("dma", eng, (out, in_, semname), tuple(reads), tuple(writes), 2.0))
            return None
        key = ("dma", semname)
        waits = self._deps(eng, reads, writes, skipkey=key)
        self.dma_cnt[key] = self.dma_cnt.get(key, 0) + 16
        tok = (key, self.dma_cnt[key])

        def fn(e, out=out, in_=in_):
            return e.dma_start(out=out, in_=in_)
        self.q[eng].append((waits, fn, key, 16))
        self._finish(tok, eng, reads, writes)
        return tok

    def merge_streams(self, lists, hop=0.05):
        eng_free = {e: 0.0 for e in ENGS}
        kw, kr = {}, {}
        idx = [0] * len(lists)

        def est(rec):
            kind, eng, _, reads, writes, cost = rec
            rd = [r for r in reads if r[:2] not in ("pb", "pT")]
            wr = list(writes) + [r for r in reads if r[:2] in ("pb", "pT")]
            t = eng_free[eng]
            for r in rd:
                t = max(t, kw.get(r, 0.0) + hop)
            for w in wr:
                t = max(t, kw.get(w, 0.0) + hop, kr.get(w, 0.0) + hop)
            return t, rd, wr

        while True:
            best, bt = None, None
            for i, lst in enumerate(lists):
                if idx[i] >= len(lst):
                    continue
                t, _, _ = est(lst[idx[i]])
                if best is None or t < bt - 1e-9:
                    best, bt = i, t
            if best is None:
                break
            rec = lists[best][idx[best]]
            idx[best] += 1
            kind, eng, payload, reads, writes, cost = rec
            t, rd, wr = est(rec)
            fin = t + cost
            eng_free[eng] = t + (0.06 if kind == "dma" else cost)
            for r in rd:
                kr[r] = max(kr.get(r, 0.0), fin)
            for w in wr:
                kw[w] = fin
                kr[w] = 0.0
            if kind == "op":
                self.op(eng, payload, reads, writes)
            else:
                self.dma(eng, payload[0], payload[1], payload[2], reads, writes)
        return max(eng_free.values())

    def dma_like(self, eng, fn, semname, reads=(), writes=()):
        key = ("dma", semname)
        waits = self._deps(eng, reads, writes, skipkey=key)
        self.dma_cnt[key] = self.dma_cnt.get(key, 0) + 16
        tok = (key, self.dma_cnt[key])
        self.q[eng].append((waits, fn, key, 16))
        self._finish(tok, eng, reads, writes)
        return tok

    def barrier(self):
        toks = []
        for e in ENGS:
            if self.cnt[e] > 0:
                toks.append((e, self.cnt[e]))
        for key, val in self.dma_cnt.items():
            toks.append((key, val))
        for e in ENGS:
            waits = []
            for (k_, v) in toks:
                if e == "pe" and k_ == "pe":
                    continue
                if self.seen[e].get(k_, 0) >= v:
                    continue
                waits.append((k_, v))
                self.seen[e][k_] = v
            if waits:
                self._note(waits)
                self.q[e].append((waits, None, None, 0))

    def emit(self):
        rank = {}
        for e in ENGS:
            rank[e] = {idx: i + 1 for i, idx in enumerate(sorted(self.waited[e]))}

        def semval(k_, v):
            if k_ in rank:
                r = rank[k_][v]
                return self._sem((k_, (r - 1) // EPOCH)), (r - 1) % EPOCH + 1
            return self._sem(k_), v
        prog = self
        self.n_inc = {e: len(rank[e]) for e in ENGS}

        def run(engobj, lst):
            for (waits, fn, key, idx) in lst:
                for (k_, v) in waits:
                    sm, val = semval(k_, v)
                    engobj.wait_ge(sm, val)
                if fn is None:
                    continue
                if key in rank:
                    ins = fn(engobj)
                    if idx in rank[key]:
                        sm, _ = semval(key, idx)
                        ins.then_inc(sm, 1)
                else:
                    fn(engobj).then_inc(prog._sem(key), 16)

        with self.nc.Block() as block:
            @block.tensor
            def _(e):
                run(e, prog.q["pe"])

            @block.scalar
            def _(e):
                run(e, prog.q["act"])

            @block.vector
            def _(e):
                run(e, prog.q["dve"])

            @block.gpsimd
            def _(e):
                run(e, prog.q["pool"])

            @block.sync
            def _(e):
                run(e, prog.q["sp"])


DEBUG_OUT = False
STOP = None


class StopBuild(Exception):
    pass


class Arena:
    def __init__(self, ap):
        self.ap = ap
        self.cap = ap.shape[1]
        self.off = 0

    def reset(self):
        self.off = 0

    def take(self, shape, dt):
        n = 1
        for d_ in shape[1:]:
            n *= d_
        units = n * (2 if dt == F32 else 1)
        units = (units + 15) // 16 * 16
        assert self.off + units <= self.cap, ("arena overflow", self.off, units, self.cap)
        v = self.ap[0:shape[0], self.off:self.off + units]
        self.off += units
        if dt == F32:
            v = v.bitcast(F32)
        v = v[:, 0:n]
        if len(shape) == 3:
            v = v.rearrange("p (a b) -> p a b", b=shape[2])
        return v


class K:
    def __init__(self, nc):
        self.nc = nc
        self.P = Prog(nc)

    def sb(self, name, shape, dt):
        return self.nc.alloc_sbuf_tensor(name, list(shape), dt).ap()

    def ps(self, name, shape, dt):
        return self.nc.alloc_psum_tensor(name, list(shape), dt).ap()

    @staticmethod
    def _n(ap):
        n = 1
        for d_ in ap.shape[1:]:
            n *= d_
        return n

    def _cd(self, out, eng="dve", mult=1.0):
        n = self._n(out)
        if eng == "act":
            return (224.0 + n) / 1200.0
        c = (64.0 + n) / 960.0 * mult
        return c * (2.0 if eng == "pool" else 1.0)

    def mm(self, out, lhsT, rhs, start, stop, r, w):
        self.P.op("pe", lambda e: e.matmul(out, lhsT, rhs, start=start, stop=stop), r, w,
                  cost=max(64.0, self._n(out)) / 2400.0 + 0.01)

    def tr(self, out, in_, ident, r, w):
        self.P.op("pe", lambda e: e.transpose(out, in_, ident), r, w, cost=0.07)

    def act(self, out, in_, func, r, w, bias=None, scale=1.0, accum=None):
        def fn(e):
            kw = {}
            if bias is not None:
                kw["bias"] = bias
            if accum is not None:
                kw["accum_out"] = accum
            return e.activation(out=out, in_=in_, func=func, scale=scale, **kw)
        self.P.op("act", fn, r, w, cost=self._cd(out, "act"))

    def tt(self, out, a, b, op, r, w, eng="dve"):
        self.P.op(eng, lambda e: e.tensor_tensor(out=out, in0=a, in1=b, op=op), r, w, cost=self._cd(out, eng))

    def ts1(self, out, a, s, op, r, w, eng="dve"):
        self.P.op(eng, lambda e: e.tensor_single_scalar(out=out, in_=a, scalar=s, op=op), r, w, cost=self._cd(out, eng))

    def ts2(self, out, a, s1, s2, op0, op1, r, w, eng="dve"):
        self.P.op(eng, lambda e: e.tensor_scalar(out=out, in0=a, scalar1=s1, scalar2=s2, op0=op0, op1=op1), r, w,
                  cost=self._cd(out, eng))

    def stt(self, out, a, s, b, op0, op1, r, w, eng="dve"):
        self.P.op(eng, lambda e: e.scalar_tensor_tensor(out=out, in0=a, scalar=s, in1=b, op0=op0, op1=op1), r, w,
                  cost=self._cd(out, eng))

    def cp(self, out, in_, r, w, eng="dve"):
        if eng == "act":
            self.act(out, in_, AF.Copy, r, w)
        else:
            self.P.op(eng, lambda e: e.tensor_copy(out=out, in_=in_), r, w, cost=self._cd(out, eng))

    def recip(self, out, in_, r, w):
        self.P.op("dve", lambda e: e.reciprocal(out=out, in_=in_), r, w, cost=self._cd(out, "dve", 6.0))

    def memset(self, ap, val, w, eng="dve"):
        self.P.op(eng, lambda e: e.memset(ap, val), (), w, cost=self._cd(ap, eng, 0.5))

    def scan(self, out, d0, d1, r, w):
        self.P.op("dve", lambda e: e.tensor_tensor_scan(out=out, data0=d0, data1=d1, initial=0.0,
                                                        op0=ALU.mult, op1=ALU.add), r, w, cost=self._cd(out, "dve", 2.0))

    def asel(self, out, in_, pattern, cmp_op, fill, cm, r, w):
        self.P.op("pool", lambda e: e.affine_select(out=out, in_=in_, pattern=pattern, compare_op=cmp_op,
                                                     fill=fill, base=0, channel_multiplier=cm), r, w)

    def dma(self, eng, out, in_, sem, r=(), w=()):
        self.P.dma(eng, out, in_, sem, r, w)


def build_program():
    nc = bass.Bass("TRN2", target_bir_lowering=False)
    k = K(nc)
    P = k.P

    def din(name, shape, dt=F32):
        return nc.dram_tensor(name, list(shape), dt, kind="ExternalInput").ap()

    x_in = din("x", [NT, D])
    cT_in = din("cT", [128, NKT])
    w_ada = din("w_ada", [2, D, 3 * D])
    b_ada = din("b_ada", [2, 3 * D])
    w_in = din("w_in", [2, D, NIN])
    w_pool = din("w_pool", [2, 4, 256, 256])
    w_du = din("w_decay_up", [2, 64, 1024])
    w_au = din("w_aaa_up", [2, 64, 1024])
    w_mvd = din("w_mv_down", [1, D, 32])
    w_mvu = din("w_mv_up", [1, 32, 1024])
    w_bra = din("w_br_a", [2, 1024, D])
    w_brb = din("w_br_b", [2, 1024, D])
    w_out = din("w_out", [2, D, D])
    g_post = din("g_post", [2, D])
    cols_in = din("cols", [2, 128, NCOLS])
    y_out = nc.dram_tensor("y", [NT, D], F32, kind="ExternalOutput").ap()

    skind = "ExternalOutput" if DEBUG_OUT else "Internal"
    projT = nc.dram_tensor("projT", [NPROJ, NT], F32, kind=skind).ap()
    sgT = nc.dram_tensor("sgT", [2 * D, NT], BF16, kind=skind).ap()
    vfT = nc.dram_tensor("vfT", [1024, NT], F32, kind=skind).ap()
    xmid = nc.dram_tensor("xmid", [NT, D], F32, kind=skind).ap()
    ggd = nc.dram_tensor("ggd", [1, D], F32, kind="Internal").ap()
    lorad = nc.dram_tensor("lorad", [2, 1024, NT], F32, kind="Internal").ap()

    if DEBUG_OUT:
        dbg_big = nc.dram_tensor("dbg_big", [2, 128, 16, NT], BF16, kind="ExternalOutput").ap()
    big = k.sb("big", [128, 16, NT], BF16)
    hT = big
    yaT = big[:, 0:8, :]
    ybT = big[:, 8:16, :]
    arA = k.sb("arenaA", [128, 32768], BF16)
    arB = k.sb("arenaB", [128, 30208], BF16)
    A = Arena(arA)
    B = Arena(arB)
    cols = [k.sb("cols%d" % l, [128, NCOLS], F32) for l in range(2)]
    small = k.sb("small", [128, 64], F32)
    shc = k.sb("shc", [128, 16], F32)
    gsc = k.sb("gsc", [128, 16], F32)
    omk = k.sb("omk", [128, 8], F32)
    cT = k.sb("cTs", [128, NKT], F32)
    condT = k.sb("condT", [128, NKT], BF16)
    onesrow = k.sb("onesrow", [1, 128], F32)
    ident = k.sb("ident", [128, 128], BF16)
    identx = k.sb("identx", [64, 512], F32)
    m_su = k.sb("m_su", [64, 512], BF16)
    m_u = k.sb("m_u", [64, 512], BF16)
    m_sl = k.sb("m_sl", [64, 512], BF16)
    bd1 = k.sb("bd1", [128, 128], BF16)
    bdm = k.sb("bdm", [128, 128], BF16)
    smask = k.sb("smask", [128, 512], F32)
    epsr = k.sb("epsr", [128, 1], F32)
    epsl = k.sb("epsl", [128, 1], F32)
    mvs = k.sb("mvs", [32, NT], BF16)
    negc = k.sb("negc", [128, 24], F32)
    onec = k.sb("onec", [128, 1], F32)
    cnt16 = k.sb("cnt16", [128, 16], F32)
    one16 = k.sb("one16", [128, 16], F32)
    Hf = [k.sb("Hf%d" % p, [128, 128], F32) for p in range(8)]
    Hb = [k.sb("Hb%d" % p, [128, 128], BF16) for p in range(8)]

    pb = [k.ps("pb%d" % i, [128, 512], F32) for i in range(6)]
    pT = [k.ps("pT%d" % i, [128, 1024], BF16) for i in range(2)]

    identf = A.take([128, 128], F32)
    identxf = A.take([64, 512], F32)
    mtmp = A.take([64, 512], F32)
    k.memset(identf, 0.0, ["identf"])
    k.asel(identf, identf, [[-1, 128]], ALU.not_equal, 1.0, 1, ["identf"], ["identf"])
    k.cp(ident, identf, ["identf"], ["ident"])
    ix3 = identxf.rearrange("p (u j) -> p u j", j=64)
    k.memset(identxf, 0.0, ["identxf"])
    k.asel(ix3, ix3, [[0, 8], [-1, 64]], ALU.not_equal, 1.0, 1, ["identxf"], ["identxf"])
    k.cp(identx, identxf, ["identxf"], ["identx"])
    for (m, mk_, cmpop, cm, coef) in ((m_su, "m_su", ALU.is_gt, -1, 1), (m_u, "m_u", ALU.is_ge, -1, 1),
                                      (m_sl, "m_sl", ALU.is_gt, 1, -1)):
        m3 = mtmp.rearrange("p (u j) -> p u j", j=64)
        k.memset(mtmp, 1.0, ["mtmp"])
        k.asel(m3, m3, [[0, 8], [coef, 64]], cmpop, 0.0, cm, ["mtmp"], ["mtmp"])
        k.cp(m, mtmp, ["mtmp"], [mk_])
    k.memset(bd1, 0.0, ["bd1"])
    k.memset(bd1[0:64, 0:64], 1.0, ["bd1"])
    k.memset(bd1[64:128, 64:128], 1.0, ["bd1"])
    k.memset(bdm, 0.0, ["bdm"])
    k.memset(bdm[0:64, 0:64], 1.0 / 64, ["bdm"])
    k.memset(bdm[64:128, 64:128], 1.0 / 64, ["bdm"])
    k.memset(smask, 1.0, ["smask"])
    k.memset(smask.rearrange("p (c j) -> p c j", j=64)[:, :, 0:1], 0.0, ["smask"])
    k.memset(epsr, RMS_EPS, ["epsr"])
    k.memset(epsl, LNX_EPS, ["epsl"])
    k.memset(onesrow, 1.0, ["onesrow"])
    k.memset(small, 0.0, ["small"])
    k.memset(one16, 1.0, ["one16"])
    k.memset(onec, 1.0, ["onec"])
    k.scan(cnt16, one16, one16, ["one16"], ["cnt16"])

    k.dma("sp", cT, cT_in, "l_c", w=["cT"])
    for l in range(2):
        k.dma("sp", cols[l], cols_in[l], "l_cols%d" % l, w=["cols%d" % l])
    k.act(condT, cT, AF.Silu, ["cT"], ["condT"])
    P.barrier()

    evac_rr = [0]
    stage_rr = [0]
    wgrp = [None, None]

    def col(l, c):
        return cols[l][:, c:c + 1]

    def load_wgrp(l, slot, src, c0, wd):
        v = src.rearrange("(kt p) n -> p kt n", p=128)
        for q4 in range(4):
            k.dma("pool", wgrp[slot][:, q4 * 4:(q4 + 1) * 4, 0:wd], v[:, q4 * 4:(q4 + 1) * 4, c0:c0 + wd],
                  "l_wg%d" % slot, w=["wgrp%d" % slot])

    def stage_end(l, tag):
        if STOP == "%d:%s" % (l, tag):
            P.barrier()
            raise StopBuild()

    def build_layer(l):
        xsrc = x_in if l == 0 else xmid
        xdst = xmid if l == 0 else y_out
        ck = "cols%d" % l
        A.reset()
        B.reset()
        modrow = A.take([1, 3 * D], F32)
        gprow = A.take([1, D], F32)
        ggrow = A.take([1, D], F32)
        brow = A.take([1, 512], F32)
        wgrp[0] = B.take([128, 16, 512], BF16)
        wgrp[1] = B.take([128, 16, 512], BF16)
        k.dma("sp", gprow, g_post[l:l + 1, :], "l_gp", w=["gprow"])
        for cg in range(12):
            slot = cg % 2
            load_wgrp(l, slot, w_ada[l], cg * 512, 512)
            k.dma("sp", brow, b_ada[l:l + 1, cg * 512:(cg + 1) * 512], "l_bada", w=["brow"])
            for kt in range(NKT):
                k.mm(pb[0][0:1, :], condT[:, kt:kt + 1], wgrp[slot][:, kt, :], kt == 0, kt == NKT - 1,
                     ["condT", "wgrp%d" % slot], ["pb0"])
            k.tt(modrow[:, cg * 512:(cg + 1) * 512], pb[0][0:1, :], brow, ALU.add,
                 ["pb0", "brow"], ["modrow"])
        for i in range(32):
            k.mm(pb[1][:, i:i + 1], modrow[0:1, i * 128:(i + 1) * 128], onesrow[0:1, 0:1], True, True,
                 ["modrow", "onesrow"], ["pb1"])
        k.cp(shc, pb[1][:, 0:16], ["pb1"], ["shc"])
        k.stt(gsc, pb[1][:, 16:32], 1.0, cols[l][:, CG_PRE:CG_PRE + 16], ALU.add, ALU.mult, ["pb1", ck], ["gsc"])
        k.tt(ggrow, modrow[:, 2 * D:3 * D], gprow, ALU.mult, ["modrow", "gprow"], ["ggrow"])
        k.dma("sp", ggd, ggrow, "s_ggd", r=["ggrow"], w=["ggd"])
        k.ts2(omk, cols[l][:, CKA:CKA + 8], -1.0, 1.0, ALU.mult, ALU.add, [ck], ["omk"])
        P.barrier()
        A.reset()
        B.reset()
        xt = [A.take([128, D], F32) for _ in range(2)]
        junk = A.take([128, D], F32)
        xnb = [A.take([128, D], BF16) for _ in range(4)]
        for tg in range(4):
            for j in range(4):
                tt_ = tg * 4 + j
                xs = xt[tt_ % 2]
                k.dma("sp", xs, xsrc[tt_ * 128:(tt_ + 1) * 128, :], "l_xt%d" % (tt_ % 2), w=["xt%d" % (tt_ % 2)])
                k.act(junk, xs, AF.Square, ["xt%d" % (tt_ % 2)], ["junk", "small"], accum=small[:, tt_:tt_ + 1])
                k.act(small[:, 16 + tt_:17 + tt_], small[:, tt_:tt_ + 1], AF.Sqrt, ["small", "epsr"], ["small"],
                      bias=epsr, scale=1.0 / D)
                k.recip(small[:, 32 + tt_:33 + tt_], small[:, 16 + tt_:17 + tt_], ["small"], ["small"])
                k.ts1(xnb[j], xs, small[:, 32 + tt_:33 + tt_], ALU.mult, ["xt%d" % (tt_ % 2), "small"], ["xnb%d" % j])
            for ft in range(NKT):
                pt = pT[ft % 2]
                for j in range(4):
                    k.tr(pt[:, j * 128:(j + 1) * 128], xnb[j][:, ft * 128:(ft + 1) * 128], ident,
                         ["xnb%d" % j, "ident"], ["pT%d" % (ft % 2)])
                k.act(hT[:, ft, tg * 512:(tg + 1) * 512], pt[:, 0:512], AF.Identity, ["pT%d" % (ft % 2), "gsc", "shc"],
                      ["hT"], bias=shc[:, ft:ft + 1], scale=gsc[:, ft:ft + 1])
        k.memset(small[:, 0:16], 0.0, ["small"])

        stage_end(l, "S1")
        P.barrier()
        A.reset()
        B.reset()
        stage = [A.take([128, 512], F32) for _ in range(4)]
        stageb = [A.take([128, 512], BF16) for _ in range(4)]
        mvraw = A.take([32, 1 + NT], F32)
        f_dm = A.take([32, 512], F32)
        wgrp[0] = B.take([128, 16, 512], BF16)
        wgrp[1] = B.take([128, 16, 512], BF16)
        wmvd = B.take([128, 16, 32], BF16)
        if l == 1:
            k.memset(mvraw[:, 0:1], 0.0, ["mvraw"])
            k.dma("pool", wmvd, w_mvd[0].rearrange("(kt p) n -> p kt n", p=128), "l_wmvd", w=["wmvd"])
            for tb in range(4):
                for kt in range(NKT):
                    k.mm(pb[4][0:32, :], wmvd[:, kt, :], hT[:, kt, tb * 512:(tb + 1) * 512], kt == 0, kt == NKT - 1,
                         ["wmvd", "hT"], ["pb4"])
                k.cp(mvraw[:, 1 + tb * 512:1 + (tb + 1) * 512], pb[4][0:32, :], ["pb4"], ["mvraw"])
            for tb in range(4):
                k.tt(f_dm, mvraw[:, tb * 512:tb * 512 + 512], mvraw[:, 1 + tb * 512:1 + tb * 512 + 512], ALU.subtract,
                     ["mvraw"], ["f_dm"])
                k.stt(mvs[:, tb * 512:(tb + 1) * 512], f_dm, cols[l][0:32, CMU_MV:CMU_MV + 1],
                      mvraw[:, 1 + tb * 512:1 + tb * 512 + 512], ALU.mult, ALU.add, ["f_dm", ck, "mvraw"], ["mvs"])
        groups = [(g * 512, 512) for g in range(20)] + [(10240, 128)]
        for gi, (c0, wd) in enumerate(groups):
            slot = gi % 2
            load_wgrp(l, slot, w_in[l], c0, wd)
            for cb in range(wd // 128):
                cc = c0 + cb * 128
                for tb in range(4):
                    bi = evac_rr[0] % 4
                    evac_rr[0] += 1
                    bank = pb[bi]
                    for kt in range(NKT):
                        k.mm(bank, wgrp[slot][:, kt, cb * 128:(cb + 1) * 128], hT[:, kt, tb * 512:(tb + 1) * 512],
                             kt == 0, kt == NKT - 1, ["wgrp%d" % slot, "hT"], ["pb%d" % bi])
                    si = stage_rr[0] % 4
                    stage_rr[0] += 1
                    if cc >= NPROJ:
                        k.act(stageb[si], bank, AF.Sigmoid, ["pb%d" % bi], ["stageb%d" % si])
                        k.dma("sp", sgT[cc - NPROJ:cc - NPROJ + 128, tb * 512:(tb + 1) * 512], stageb[si],
                              "s_stb%d" % si, r=["stageb%d" % si])
                    else:
                        if 1024 <= cc < 2048:
                            k.act(stage[si], bank, AF.Silu, ["pb%d" % bi], ["stage%d" % si])
                        elif si % 2 == 0:
                            k.cp(stage[si], bank, ["pb%d" % bi], ["stage%d" % si], eng="dve")
                        else:
                            k.cp(stage[si], bank, ["pb%d" % bi], ["stage%d" % si], eng="act")
                        k.dma("sp", projT[cc:cc + 128, tb * 512:(tb + 1) * 512], stage[si],
                              "s_st%d" % si, r=["stage%d" % si], w=["projT"])

        stage_end(l, "S2")
        P.barrier()
        A.reset()
        B.reset()
        ubuf = [A.take([128, 16 + NT], F32) for _ in range(2)]
        sA = A.take([128, 16 + NT], F32)
        sB = A.take([128, 16 + NT], F32)
        invc = A.take([128, NT], F32)
        zab = A.take([128, NT], F32)
        plb = [B.take([128, NT], BF16) for _ in range(2)]
        wpl = B.take([128, 2, 256], BF16)
        for i in range(2):
            k.memset(ubuf[i][:, 0:16], 0.0, ["ubuf%d" % i])
        k.memset(sA[:, 0:16], 0.0, ["sA"])
        k.memset(sB[:, 0:16], 0.0, ["sB"])
        for g in range(4):
            wwin = 2 ** (g + 1)
            k.dma("pool", wpl, w_pool[l, g].rearrange("(ck p) d -> p ck d", p=128), "l_wpl", w=["wpl"])
            k.memset(invc, 1.0 / wwin, ["invc"])
            k.ts1(invc[:, 0:16], cnt16, float(wwin), ALU.min, ["cnt16"], ["invc"])
            k.recip(invc[:, 0:16], invc[:, 0:16], ["invc"], ["invc"])
            for ck_ in range(2):
                ct = 2 * g + ck_
                ub = ubuf[ck_]
                uk = "ubuf%d" % ck_
                k.dma("sp", ub[:, 16:], projT[ct * 128:(ct + 1) * 128, :], "l_ub%d" % ck_, r=["projT"], w=[uk])
                cur, curk = ub, uk
                sh = 1
                bufs = [(sA, "sA"), (sB, "sB")]
                bi2 = 0
                while sh < wwin:
                    nb, nk = bufs[bi2 % 2]
                    bi2 += 1
                    k.tt(nb[:, 16:], cur[:, 16:], cur[:, 16 - sh:16 + NT - sh], ALU.add, [curk], [nk])
                    cur, curk = nb, nk
                    sh *= 2
                k.tt(cur[:, 16:], cur[:, 16:], invc, ALU.mult, [curk, "invc"], [curk])
                k.tt(plb[ck_], cur[:, 16:], ub[:, 16:], ALU.subtract, [curk, uk], ["plb%d" % ck_])
            if g == 0:
                stage_end(l, "S3a")
            for db in range(2):
                dt_ = 2 * g + db
                k.dma("sp", zab, projT[1024 + dt_ * 128:1024 + (dt_ + 1) * 128, :], "l_zab", r=["projT"], w=["zab"])
                for tb in range(4):
                    bi = evac_rr[0] % 4
                    evac_rr[0] += 1
                    for ck_ in range(2):
                        k.mm(pb[bi], wpl[:, ck_, db * 128:(db + 1) * 128], plb[ck_][:, tb * 512:(tb + 1) * 512],
                             ck_ == 0, ck_ == 1, ["wpl", "plb%d" % ck_], ["pb%d" % bi])
                    k.stt(yaT[:, dt_, tb * 512:(tb + 1) * 512], pb[bi], col(l, CPSC + dt_), zab[:, tb * 512:(tb + 1) * 512],
                          ALU.mult, ALU.mult, ["pb%d" % bi, ck, "zab"], ["yaT", "hT"])
            if g == 0:
                stage_end(l, "S3b")

        stage_end(l, "S3")
        P.barrier()
        A.reset()
        B.reset()
        walo = A.take([64, 1 + NT], F32)
        pfd = A.take([64, 512], F32)
        pft = A.take([64, 512], F32)
        pstg = [A.take([128, 512], F32) for _ in range(4)]
        tw_w = B.take([64, NT], BF16)
        tw_a = B.take([64, NT], BF16)
        lw_w = B.take([64, 1024], BF16)
        lw_a = B.take([64, 1024], BF16)
        k.memset(walo[:, 0:1], 0.0, ["walo"])
        k.dma("pool", lw_w, w_du[l], "l_lw", w=["lw"])
        k.dma("pool", lw_a, w_au[l], "l_lw", w=["lw"])
        for which in range(2):
            k.dma("sp", walo[:, 1:], projT[6144 + which * 64:6208 + which * 64, :], "l_walo", r=["projT"], w=["walo"])
            mucol = cols[l][0:64, (CMU_WA if which == 0 else CMU_A):(CMU_WA if which == 0 else CMU_A) + 1]
            for tb in range(4):
                sl = slice(tb * 512, (tb + 1) * 512)
                k.tt(pfd, walo[:, tb * 512:tb * 512 + 512], walo[:, 1 + tb * 512:1 + tb * 512 + 512], ALU.subtract,
                     ["walo"], ["pfd"])
                k.stt(pft, pfd, mucol, walo[:, 1 + tb * 512:1 + tb * 512 + 512], ALU.mult, ALU.add,
                      ["pfd", ck, "walo"], ["pft"])
                if which == 0:
                    k.act(tw_w[:, sl], pft, AF.Tanh, ["pft"], ["tw_w"])
                else:
                    k.cp(tw_a[:, sl], pft, ["pft"], ["tw_a"])
        pi = 0
        for p in range(8):
            for which in range(2):
                lwx, twx, twk = (lw_w, tw_w, "tw_w") if which == 0 else (lw_a, tw_a, "tw_a")
                for tb in range(4):
                    bi = pi % 4
                    pi += 1
                    k.mm(pb[bi], lwx[:, p * 128:(p + 1) * 128], twx[:, tb * 512:(tb + 1) * 512], True, True, ["lw", twk], ["pb%d" % bi])
                    k.cp(pstg[bi], pb[bi], ["pb%d" % bi], ["pstg%d" % bi], eng=("act" if bi % 2 else "dve"))
                    k.dma("sp", lorad[which, p * 128:(p + 1) * 128, tb * 512:(tb + 1) * 512], pstg[bi], "s_pst%d" % bi,
                          r=["pstg%d" % bi], w=["lorad"])
        P.barrier()
        A.reset()
        B.reset()
        W = 256
        NB = NT // W
        NCK = W // 64
        WS = []
        for s_ in range(2):
            ws = {}
            for nm in ("r", "k", "v", "z"):
                ws["raw_" + nm] = A.take([128, W + 1], F32)
            for nm in ("f_r", "f_k", "f_v", "f_z", "f_d", "f_sg", "f_a", "f_t1", "f_kkn", "f_kmod", "f_ka",
                       "f_S", "f_Sp", "f_Se", "f_Wt", "f_Wi", "f_Wp", "f_Wc", "f_bonus"):
                ws[nm] = A.take([128, W], F32)
            ws["f_WC"] = A.take([128, NCK], F32)
            ws["mZ"] = A.take([64, 512], F32)
            ws["mX"] = A.take([64, 512], BF16)
            ws["mIM"] = A.take([64, 512], F32)
            for nm in ("b_x", "b_at", "b_rt", "b_kh", "b_bh", "b_v", "b_atp"):
                ws[nm] = B.take([128, W], BF16)
            ws["b_bt"] = [B.take([128, W], BF16) for _ in range(2)]
            ws["b_kt"] = [B.take([128, W], BF16) for _ in range(2)]
            ws["a_tok"] = B.take([64, NCK, 128], BF16)
            ws["PTb"] = B.take([64, 512], BF16)
            ws["kh_tok"] = B.take([64, NCK, 256], BF16)
            ws["bh_tok"] = B.take([64, NCK, 256], BF16)
            ws["Vpad"] = B.take([64, NCK, 256], BF16)
            ws["Upad"] = B.take([64, 256], BF16)
            ws["mN"] = [B.take([64, 512], F32) for _ in range(2)]
            ws["mM"] = [B.take([64, 512], F32) for _ in range(2)]
            ws["mP"] = [B.take([64, 512], F32) for _ in range(2)]
            for nm in ("mAK", "mRB", "mRK"):
                ws[nm] = B.take([64, 512], BF16)
            WS.append(ws)
        wmvu = A.take([32, 1024], BF16)
        SLOTTED = set(["raw_r", "raw_k", "raw_v", "raw_z", "f_r", "f_k", "f_v", "f_z", "f_d", "f_sg", "f_a", "f_t1",
                       "f_kkn", "f_kmod", "f_ka", "f_S", "f_Sp", "f_Se", "f_Wt", "f_Wi", "f_Wp", "f_Wc", "f_bonus",
                       "f_WC", "b_x", "b_at", "b_rt", "b_kh", "b_bh", "b_v", "b_bt", "b_kt", "a_tok", "kh_tok",
                       "bh_tok", "Vpad", "Upad", "b_atp", "mZ", "mN0", "mN1", "mM0", "mM1", "mP0", "mP1", "mIM",
                       "mAK", "mRB", "mRK", "mX", "vfT", "PTb"])

        class KS:
            def __init__(self, slot):
                self.slot = slot

            def __getattr__(self, name):
                f = getattr(k, name)
                slot = self.slot

                def mapk(a):
                    if isinstance(a, (list, tuple)) and len(a) > 0 and all(isinstance(x_, str) for x_ in a):
                        return [(x_ + "@%d" % slot) if x_ in SLOTTED else x_ for x_ in a]
                    return a

                def wrapped(*args, **kw):
                    return f(*[mapk(a_) for a_ in args], **{kk_: mapk(v_) for kk_, v_ in kw.items()})
                return wrapped

        k.ts1(negc, cols[l][:, CW0:CW0 + 24], -1.0, ALU.mult, [ck], ["negc"])
        for s_ in range(2):
            ks0 = KS(s_)
            ks0.memset(WS[s_]["Upad"], 0.0, ["Upad"])
            ks0.memset(WS[s_]["Vpad"], 0.0, ["Vpad"])
            ks0.memset(WS[s_]["kh_tok"], 0.0, ["kh_tok"])
            ks0.memset(WS[s_]["bh_tok"], 0.0, ["bh_tok"])
            for h_ in range(2):
                ks0.memset(WS[s_]["b_bt"][h_], 0.0, ["b_bt"])
                ks0.memset(WS[s_]["b_kt"][h_], 0.0, ["b_kt"])
        if l == 1:
            k.dma("pool", wmvu, w_mvu[0], "l_wmvu", w=["wmvu"])
        stage_end(l, "S4a")
        for p in range(8):
            k.memset(Hf[p], 0.0, ["Hf%d" % p])
            k.memset(Hb[p], 0.0, ["Hb%d" % p])

        def hr(h):
            return slice(h * 64, (h + 1) * 64)

        def us(u):
            return slice(u * 64, (u + 1) * 64)

        def tk(cc):
            return slice(cc * 64, (cc + 1) * 64)

        def it_gen(p, tb, slot):
            ws = WS[slot]
            kk_ = KS(slot)
            hk, hbk = "Hf%d" % p, "Hb%d" % p
            t0 = tb * W
            raw = {nm: ws["raw_" + nm] for nm in ("r", "k", "v", "z")}
            f_r, f_k, f_v, f_z, f_d, f_sg, f_a, f_t1 = (ws[n_] for n_ in ("f_r", "f_k", "f_v", "f_z", "f_d", "f_sg", "f_a", "f_t1"))
            f_kkn, f_kmod, f_ka, f_S, f_Sp, f_Se = (ws[n_] for n_ in ("f_kkn", "f_kmod", "f_ka", "f_S", "f_Sp", "f_Se"))
            f_Wt, f_Wi, f_Wp, f_Wc, f_bonus, f_WC = (ws[n_] for n_ in ("f_Wt", "f_Wi", "f_Wp", "f_Wc", "f_bonus", "f_WC"))
            f_vf, f_g, f_kk, f_y = f_S, f_Sp, f_Se, f_Wp
            b_x, b_at, b_rt, b_kh, b_bh, b_v = (ws[n_] for n_ in ("b_x", "b_at", "b_rt", "b_kh", "b_bh", "b_v"))
            b_bt, b_kt = ws["b_bt"], ws["b_kt"]
            a_tok, kh_tok, bh_tok, Vpad = ws["a_tok"], ws["kh_tok"], ws["bh_tok"], ws["Vpad"]
            Upad, b_atp, mZ = ws["Upad"], ws["b_atp"], ws["mZ"]
            mN, mM, mP = ws["mN"], ws["mM"], ws["mP"]
            mIM, mAK, mRB, mRK, mX = (ws[n_] for n_ in ("mIM", "mAK", "mRB", "mRK", "mX"))
            PTb = ws["PTb"]
            X0, X1, X2 = pb[3 * slot], pb[3 * slot + 1], pb[3 * slot + 2]
            K0, K1, K2 = "pb%d" % (3 * slot), "pb%d" % (3 * slot + 1), "pb%d" % (3 * slot + 2)
            ptb, ptk = pT[slot], "pT%d" % slot
            fx = {"r": f_r, "k": f_k, "v": f_v, "z": f_z}
            mu0 = {"r": CMU_R, "k": CMU_K, "v": CMU_V, "z": CMU_Z}
            for wi, nm in enumerate(("r", "k", "v", "z")):
                row0 = 2048 + wi * 1024 + p * 128
                rb, rk_ = raw[nm], "raw_" + nm
                sem = "l_raw%s%d" % (nm, slot)
                if tb == 0:
                    kk_.memset(rb[:, 0:1], 0.0, [rk_])
                    kk_.dma("sp", rb[:, 1:W + 1], projT[row0:row0 + 128, 0:W], sem, r=["projT"], w=[rk_])
                else:
                    kk_.dma("sp", rb[:, 0:W + 1], projT[row0:row0 + 128, t0 - 1:t0 + W], sem, r=["projT"], w=[rk_])
            for wi, nm in enumerate(("r", "k", "v", "z")):
                rb, rk_ = raw[nm], "raw_" + nm
                kk_.tt(f_d, rb[:, 0:W], rb[:, 1:W + 1], ALU.subtract, [rk_], ["f_d"])
                kk_.stt(fx[nm], f_d, col(l, mu0[nm] + p), rb[:, 1:W + 1], ALU.mult, ALU.add, ["f_d", ck, rk_], ["f_" + nm])
            yield "prep"
            if l == 0:
                kk_.dma("sp", vfT[p * 128:(p + 1) * 128, t0:t0 + W], f_v, "s_vf%d" % slot, r=["f_v"], w=["vfT"])
            else:
                kk_.dma("sp", f_vf, vfT[p * 128:(p + 1) * 128, t0:t0 + W], "l_vf%d" % slot, r=["vfT"], w=["f_S"])
                kk_.mm(X1[:, 0:W], wmvu[0:32, p * 128:(p + 1) * 128], mvs[0:32, t0:t0 + W], True, True, ["wmvu", "mvs"], [K1])
                kk_.act(f_g, X1[:, 0:W], AF.Exp, [K1, "negc"], ["f_Sp"], bias=negc[:, 16 + p:17 + p], scale=-1.0)
                kk_.act(f_g, f_g, AF.Ln, ["f_Sp", "onec"], ["f_Sp"], bias=onec)
                kk_.act(f_g, f_g, AF.Exp, ["f_Sp"], ["f_Sp"], scale=-1.0)
                kk_.tt(f_d, f_vf, f_v, ALU.subtract, ["f_S", "f_v"], ["f_d"])
                kk_.tt(f_d, f_d, f_g, ALU.mult, ["f_d", "f_Sp"], ["f_d"])
                kk_.tt(f_v, f_v, f_d, ALU.add, ["f_v", "f_d"], ["f_v"])
            kk_.dma("sp", f_sg, lorad[0, p * 128:(p + 1) * 128, t0:t0 + W], "l_sg%d" % slot, r=["lorad"], w=["f_sg"])
            kk_.dma("sp", f_a, lorad[1, p * 128:(p + 1) * 128, t0:t0 + W], "l_fa%d" % slot, r=["lorad"], w=["f_a"])
            kk_.act(f_sg, f_sg, AF.Exp, ["f_sg", "negc"], ["f_sg"], bias=negc[:, p:p + 1], scale=-1.0)
            kk_.act(f_a, f_a, AF.Exp, ["f_a", "negc"], ["f_a"], bias=negc[:, 8 + p:9 + p], scale=-1.0)
            kk_.act(f_sg, f_sg, AF.Ln, ["f_sg", "onec"], ["f_sg"], bias=onec)
            kk_.act(f_a, f_a, AF.Ln, ["f_a", "onec"], ["f_a"], bias=onec)
            kk_.act(f_sg, f_sg, AF.Exp, ["f_sg"], ["f_sg"], scale=-1.0)
            kk_.act(f_a, f_a, AF.Exp, ["f_a"], ["f_a"], scale=-1.0)
            yield "prep"
            kk_.ts1(f_kk, f_k, col(l, CKK + p), ALU.mult, ["f_k", ck], ["f_Se"])
            kk_.tt(b_x, f_kk, f_kk, ALU.mult, ["f_Se"], ["b_x"])
            kk_.mm(X1[:, 0:W], bd1, b_x, True, True, ["bd1", "b_x"], [K1])
            kk_.ts1(f_t1, X1[:, 0:W], 1e-24, ALU.max, [K1], ["f_t1"])
            kk_.act(f_t1, f_t1, AF.Ln, ["f_t1"], ["f_t1"])
            kk_.act(f_t1, f_t1, AF.Exp, ["f_t1"], ["f_t1"], scale=-0.5)
            kk_.tt(f_kkn, f_kk, f_t1, ALU.mult, ["f_Se", "f_t1"], ["f_kkn"])
            kk_.ts2(f_t1, f_a, col(l, CKA + p), omk[:, p:p + 1], ALU.mult, ALU.add, ["f_a", ck, "omk"], ["f_t1"])
            kk_.tt(f_kmod, f_t1, f_k, ALU.mult, ["f_t1", "f_k"], ["f_kmod"])
            kk_.tt(f_ka, f_kkn, f_a, ALU.mult, ["f_kkn", "f_a"], ["f_ka"])
            yield "prep"
            kk_.tt(f_t1, f_r, f_kmod, ALU.mult, ["f_r", "f_kmod"], ["f_t1"])
            kk_.ts1(b_x, f_t1, col(l, CRK + p), ALU.mult, ["f_t1", ck], ["b_x"])
            kk_.mm(X2[:, 0:W], bd1, b_x, True, True, ["bd1", "b_x"], [K2])
            kk_.tt(f_bonus, X2[:, 0:W], f_v, ALU.mult, [K2, "f_v"], ["f_bonus"])
            kk_.scan(f_S, smask[:, 0:W], f_sg, ["smask", "f_sg"], ["f_S"])
            kk_.tt(f_Sp, f_S, f_sg, ALU.subtract, ["f_S", "f_sg"], ["f_Sp"])
            S3 = f_S.rearrange("p (c j) -> p c j", j=64)
            kk_.tt(f_Se.rearrange("p (c j) -> p c j", j=64), S3[:, :, 63:64].to_broadcast([128, NCK, 64]), S3, ALU.subtract,
                   ["f_S"], ["f_Se"])
            kk_.act(f_Wt, f_S, AF.Exp, ["f_S"], ["f_Wt"], scale=-C0)
            kk_.act(f_Wi, f_S, AF.Exp, ["f_S"], ["f_Wi"], scale=C0)
            kk_.act(f_Wp, f_Sp, AF.Exp, ["f_Sp"], ["f_Wp"], scale=-C0)
            kk_.act(f_Wc, f_Se, AF.Exp, ["f_Se"], ["f_Wc"], scale=-C0)
            kk_.act(f_WC, S3[:, :, 63], AF.Exp, ["f_S"], ["f_WC"], scale=-C0)
            yield "prep"
            kk_.stt(b_at, f_kkn, -1.0, f_Wp, ALU.mult, ALU.mult, ["f_kkn", "f_Wp"], ["b_at"])
            for h_ in range(2):
                hs_ = slice(h_ * 64, (h_ + 1) * 64)
                kk_.tt(b_bt[h_][hs_, :], f_ka[hs_, :], f_Wi[hs_, :], ALU.mult, ["f_ka", "f_Wi"], ["b_bt"])
                kk_.tt(b_kt[h_][hs_, :], f_kmod[hs_, :], f_Wi[hs_, :], ALU.mult, ["f_kmod", "f_Wi"], ["b_kt"])
            kk_.tt(b_rt, f_r, f_Wt, ALU.mult, ["f_r", "f_Wt"], ["b_rt"])
            kk_.tt(b_kh, f_kmod, f_Wc, ALU.mult, ["f_kmod", "f_Wc"], ["b_kh"])
            kk_.tt(b_bh, f_ka, f_Wc, ALU.mult, ["f_ka", "f_Wc"], ["b_bh"])
            kk_.cp(b_v, f_v, ["f_v"], ["b_v"], eng="act")
            yield "prep"
            for ti, (src, sk, dst, dk) in enumerate(((b_at, "b_at", a_tok, "a_tok"), (b_kh, "b_kh", kh_tok, "kh_tok"),
                                                     (b_bh, "b_bh", bh_tok, "bh_tok"), (b_v, "b_v", None, "Vpad"))):
                pt = ptb[:, (ti % 2) * 512:(ti % 2) * 512 + 512]
                pk = ptk
                for c in range(NCK):
                    kk_.tr(pt[0:64, c * 128:(c + 1) * 128], src[:, c * 64:(c + 1) * 64], ident, [sk, "ident"], [pk])
                if ti == 0:
                    kk_.cp(dst.rearrange("p c n -> p (c n)"), pt[0:64, 0:NCK * 128], [pk], [dk], eng="dve")
                else:
                    dpad = Vpad if dst is None else dst
                    o4 = dpad.rearrange("p c (h x) -> p c h x", x=128)[:, :, :, 0:64]
                    i4 = pt[0:64, 0:NCK * 128].rearrange("p (c h k) -> p c h k", h=2, k=64)
                    kk_.cp(o4, i4, [pk], [dk], eng=("act" if ti % 2 else "dve"))
            yield "prep_done"
            def vb(buf):
                return buf.bitcast(BF16)[:, 0:512]

            units = [(h, cc) for h in range(2) for cc in range(4)]
            for u, (h, cc) in enumerate(units):
                kk_.mm(X0[0:64, us(u)], b_bt[h][:, tk(cc)], b_at[:, tk(cc)], True, True, ["b_bt", "b_at"], [K0])
                kk_.mm(X1[0:64, us(u)], b_at[:, tk(cc)], b_bt[h][:, tk(cc)], True, True, ["b_bt", "b_at"], [K1])
            for u, (h, cc) in enumerate(units):
                kk_.mm(X2[0:64, us(u)], b_kt[h][:, tk(cc)], b_at[:, tk(cc)], True, True, ["b_kt", "b_at"], [K2])
            kk_.tt(vb(mN[0]), X0[0:64, :], m_su, ALU.mult, [K0, "m_su"], ["mN0"])
            kk_.tt(vb(mM[0]), X1[0:64, :], m_sl, ALU.mult, [K1, "m_sl"], ["mM0"])
            yield "ph"
            for u, (h, cc) in enumerate(units):
                kk_.mm(X0[0:64, us(u)], b_bt[h][:, tk(cc)], b_rt[:, tk(cc)], True, True, ["b_bt", "b_rt"], [K0])
                kk_.mm(X1[0:64, us(u)], b_kt[h][:, tk(cc)], b_rt[:, tk(cc)], True, True, ["b_kt", "b_rt"], [K1])
            kk_.tt(mAK, X2[0:64, :], m_su, ALU.mult, [K2, "m_su"], ["mAK"])
            kk_.tt(mP[0], vb(mN[0]), identx, ALU.add, ["mN0", "identx"], ["mP0"])
            kk_.tt(mRB, X0[0:64, :], m_u, ALU.mult, [K0, "m_u"], ["mRB"])
            kk_.tt(mRK, X1[0:64, :], m_u, ALU.mult, [K1, "m_u"], ["mRK"])
            yield "ph"
            def p_step(i):
                a_, b_ = (i - 1) % 2, i % 2
                im = vb(mIM) if i >= 3 else mIM
                pin = vb(mP[a_]) if i >= 3 else mP[a_]
                for u in range(8):
                    kk_.mm(X2[0:64, us(u)], im[:, us(u)], pin[:, us(u)], True, True, ["mIM", "mP%d" % a_], [K2])
                if i == 5:
                    kk_.cp(PTb, X2[0:64, :], [K2], ["PTb"], eng="act")
                elif i in (2, 3, 4):
                    kk_.cp(vb(mP[b_]), X2[0:64, :], [K2], ["mP%d" % b_], eng="act")
                else:
                    kk_.cp(mP[b_], X2[0:64, :], [K2], ["mP%d" % b_], eng="act")

            for i in range(1, 6):
                a_, b_ = (i - 1) % 2, i % 2
                nin = vb(mN[a_]) if i in (1, 3, 4, 5) else mN[a_]
                min_ = vb(mM[a_]) if i in (1, 3, 4, 5) else mM[a_]
                for u in range(8):
                    kk_.mm(X0[0:64, us(u)], nin[:, us(u)], min_[:, us(u)], True, True,
                           ["mN%d" % a_, "mM%d" % a_], [K0])
                if i < 5:
                    for u in range(8):
                        kk_.mm(X1[0:64, us(u)], min_[:, us(u)], nin[:, us(u)], True, True,
                               ["mN%d" % a_, "mM%d" % a_], [K1])
                if i > 1:
                    p_step(i - 1)
                if i < 5:
                    mo = vb(mM[b_]) if i in (2, 3, 4) else mM[b_]
                    no = vb(mN[b_]) if i in (2, 3, 4) else mN[b_]
                    kk_.cp(mo, X0[0:64, :], [K0], ["mM%d" % b_], eng="act")
                    kk_.cp(no, X1[0:64, :], [K1], ["mN%d" % b_], eng="dve")
                kk_.tt(vb(mIM) if i >= 3 else mIM, X0[0:64, :], identx, ALU.add, [K0, "identx"], ["mIM"])
                yield "ph"
            p_step(5)
            PT, PTk = PTb, "PTb"
            for u, (h, cc) in enumerate(units):
                kk_.mm(X0[0:64, us(u)], mAK[:, us(u)], Vpad[:, cc, h * 128:h * 128 + 64], True, True, ["mAK", "Vpad"], [K0])
            kk_.cp(mX, X0[0:64, :], [K0], ["mX"], eng="dve")
            for u, (h, cc) in enumerate(units):
                kk_.mm(X2[:, us(u)], a_tok[:, cc, :], PT[:, us(u)], True, True, [PTk, "a_tok"], [K2])
            for u, (h, cc) in enumerate(units):
                kk_.mm(X1[0:64, us(u)], PT[:, us(u)], mX[:, us(u)], True, True, [PTk, "mX"], [K1])
            kk_.cp(b_atp[0:64, :], X2[0:64, 0:256], [K2], ["b_atp"], eng="dve")
            kk_.cp(b_atp[64:128, :], X2[64:128, 256:512], [K2], ["b_atp"], eng="act")
            kk_.cp(mZ, X1[0:64, :], [K1], ["mZ"], eng="act")
            yield "ph"
            U3 = Upad.rearrange("p (h x) -> p h x", x=128)[:, :, 0:64]
            for cc in range(4):
                kk_.mm(X0[0:64, 0:128], b_atp[:, tk(cc)], Hb[p], True, True, ["b_atp", hbk], [K0])
                z3 = mZ.rearrange("p (h c v) -> p h c v", h=2, v=64)[:, :, cc, :]
                kk_.tt(U3, X0[0:64, 0:128].rearrange("p (h v) -> p h v", v=64), z3, ALU.add, [K0, "mZ"], ["Upad"])
                kk_.mm(X1[:, 0:64], Hb[p], b_rt[:, tk(cc)], True, False, [hbk, "b_rt"], [K1])
                for h in range(2):
                    u = h * 4 + cc
                    kk_.mm(X1[:, 0:64], Upad[:, h * 64:h * 64 + 128], mRB[:, us(u)], False, False, ["Upad", "mRB"], [K1])
                    kk_.mm(X1[:, 0:64], Vpad[:, cc, h * 64:h * 64 + 128], mRK[:, us(u)], False, h == 1,
                           ["Vpad", "mRK"], [K1])
                for h in range(2):
                    kk_.mm(X2[:, h * 64:(h + 1) * 64], bh_tok[:, cc, h * 64:h * 64 + 128], Upad[:, h * 128:h * 128 + 64], True, False,
                           ["bh_tok", "Upad"], [K2])
                    kk_.mm(X2[:, h * 64:(h + 1) * 64], kh_tok[:, cc, h * 64:h * 64 + 128], Vpad[:, cc, h * 128:h * 128 + 64], False, True,
                           ["kh_tok", "Vpad"], [K2])
                kk_.stt(Hb[p], Hf[p], f_WC[:, cc:cc + 1], X2[:, 0:128], ALU.mult, ALU.add, [hk, "f_WC", K2], [hbk])
                kk_.stt(Hf[p], Hf[p], f_WC[:, cc:cc + 1], X2[:, 0:128], ALU.mult, ALU.add, [hk, "f_WC", K2], [hk])
                kk_.cp(f_y[:, tk(cc)], X1[:, 0:64], [K1], ["f_Wp"], eng="act")
                yield "ph"
            kk_.cp(b_x, f_y, ["f_Wp"], ["b_x"], eng="dve")
            kk_.mm(X0[:, 0:W], bdm, b_x, True, True, ["bdm", "b_x"], [K0])
            kk_.tt(f_d, f_y, X0[:, 0:W], ALU.subtract, ["f_Wp", K0], ["f_d"])
            kk_.tt(b_x, f_d, f_d, ALU.mult, ["f_d"], ["b_x"])
            kk_.mm(X0[:, W:2 * W], bdm, b_x, True, True, ["bdm", "b_x"], [K0])
            kk_.act(f_t1, X0[:, W:2 * W], AF.Ln, [K0, "epsl"], ["f_t1"], bias=epsl)
            kk_.act(f_t1, f_t1, AF.Exp, ["f_t1"], ["f_t1"], scale=-0.5)
            yield "ph"
            kk_.tt(f_d, f_d, f_t1, ALU.mult, ["f_d", "f_t1"], ["f_d"])
            kk_.ts2(f_d, f_d, col(l, CLG + p), col(l, CLB + p), ALU.mult, ALU.add, ["f_d", ck], ["f_d"])
            kk_.tt(f_d, f_d, f_bonus, ALU.add, ["f_d", "f_bonus"], ["f_d"])
            kk_.act(f_t1, f_z, AF.Exp, ["f_z"], ["f_t1"], scale=-1.0)
            kk_.act(f_t1, f_t1, AF.Ln, ["f_t1", "onec"], ["f_t1"], bias=onec)
            kk_.act(f_t1, f_t1, AF.Exp, ["f_t1"], ["f_t1"], scale=-1.0)
            kk_.tt(f_d, f_d, f_z, ALU.mult, ["f_d", "f_z"], ["f_d"])
            kk_.tt(ybT[:, p, t0:t0 + W], f_d, f_t1, ALU.mult, ["f_d", "f_t1"], ["ybT%d" % p])

        def chain(st):
            for p in range(st, 8, 2):
                for tb in range(NB):
                    for tag in it_gen(p, tb, st):
                        yield tag

        lists = []
        for st in range(2):
            P._defer = []
            for _ in chain(st):
                pass
            lists.append(P._defer)
            P._defer = None
        P.merge_streams(lists)

        if DEBUG_OUT:
            P.barrier()
            k.dma("sp", dbg_big[l], big, "s_dbg", r=["yaT", "ybT"])
        stage_end(l, "S4")
        P.barrier()
        A.reset()
        B.reset()
        mrgT = A.take([128, 16, NT], BF16)
        sga = B.take([128, NT], BF16)
        sgb = B.take([128, NT], BF16)
        wa_g = B.take([128, 8, 512], BF16)
        wb_g = B.take([128, 8, 512], BF16)
        m1 = [B.take([128, 512], F32) for _ in range(2)]
        m2 = [B.take([128, 512], F32) for _ in range(2)]
        for cgp in range(4):
            va = w_bra[l].rearrange("(kt p) n -> p kt n", p=128)
            vb = w_brb[l].rearrange("(kt p) n -> p kt n", p=128)
            for q2 in range(2):
                k.dma("pool", wa_g[:, q2 * 4:(q2 + 1) * 4, :], va[:, q2 * 4:(q2 + 1) * 4, cgp * 512:(cgp + 1) * 512], "l_wag", w=["wa_g"])
                k.dma("pool", wb_g[:, q2 * 4:(q2 + 1) * 4, :], vb[:, q2 * 4:(q2 + 1) * 4, cgp * 512:(cgp + 1) * 512], "l_wbg", w=["wb_g"])
            for cb in range(4):
                c = cgp * 4 + cb
                k.dma("sp", sga, sgT[c * 128:(c + 1) * 128, :], "l_sga", r=["sgT"], w=["sga"])
                k.dma("sp", sgb, sgT[D + c * 128:D + (c + 1) * 128, :], "l_sgb", r=["sgT"], w=["sgb"])
                for tb in range(4):
                    sl = slice(tb * 512, (tb + 1) * 512)
                    for kt in range(8):
                        k.mm(pb[0], wa_g[:, kt, cb * 128:(cb + 1) * 128], yaT[:, kt, sl], kt == 0, kt == 7, ["wa_g", "yaT"], ["pb0"])
                    for kt in range(8):
                        k.mm(pb[1], wb_g[:, kt, cb * 128:(cb + 1) * 128], ybT[:, kt, sl], kt == 0, kt == 7, ["wb_g", "ybT"], ["pb1"])
                    i2 = tb % 2
                    k.tt(m1[i2], pb[0], sga[:, sl], ALU.mult, ["pb0", "sga"], ["m1_%d" % i2])
                    k.tt(m2[i2], pb[1], sgb[:, sl], ALU.mult, ["pb1", "sgb"], ["m2_%d" % i2])
                    k.tt(mrgT[:, c, sl], m1[i2], m2[i2], ALU.add, ["m1_%d" % i2, "m2_%d" % i2], ["mrgT"])

        stage_end(l, "S5a")
        P.barrier()
        B.reset()
        xt = [B.take([128, D], F32) for _ in range(2)]
        osb = B.take([128, D], F32)
        junk = B.take([128, D], F32)
        gg = B.take([128, D], F32)
        k.dma("sp", gg, ggd[0:1, :].partition_broadcast(128), "l_gg", r=["ggd"], w=["gg"])
        vo = w_out[l].rearrange("(kt p) n -> p kt n", p=128)
        for q4 in range(4):
            k.dma("pool", big[:, q4 * 4:(q4 + 1) * 4, :], vo[:, q4 * 4:(q4 + 1) * 4, :], "l_wout", w=["hT", "yaT", "ybT", "wout"])
        for tt_ in range(16):
            xs = xt[tt_ % 2]
            xk = "xt%d" % (tt_ % 2)
            k.dma("sp", xs, xsrc[tt_ * 128:(tt_ + 1) * 128, :], "l_" + xk, w=[xk])
            for n in range(4):
                for kt in range(NKT):
                    k.mm(pb[n], mrgT[:, kt, tt_ * 128:(tt_ + 1) * 128], big[:, kt, n * 512:(n + 1) * 512], kt == 0, kt == NKT - 1,
                         ["mrgT", "wout"], ["pb%d" % n])
                k.cp(osb[:, n * 512:(n + 1) * 512], pb[n], ["pb%d" % n], ["osb"], eng=("act" if n % 2 else "dve"))
            k.act(junk, osb, AF.Square, ["osb"], ["junk", "small"], accum=small[:, tt_:tt_ + 1])
            k.act(small[:, 16 + tt_:17 + tt_], small[:, tt_:tt_ + 1], AF.Sqrt, ["small", "epsr"], ["small"], bias=epsr, scale=1.0 / D)
            k.recip(small[:, 32 + tt_:33 + tt_], small[:, 16 + tt_:17 + tt_], ["small"], ["small"])
            k.stt(osb, osb, small[:, 32 + tt_:33 + tt_], gg, ALU.mult, ALU.mult, ["osb", "small", "gg"], ["osb"])
            k.tt(osb, osb, xs, ALU.add, ["osb", xk], ["osb"])
            k.dma("sp", xdst[tt_ * 128:(tt_ + 1) * 128, :], osb, "s_out", r=["osb"], w=["xmid"])
        k.memset(small[:, 0:16], 0.0, ["small"])
        P.barrier()

    try:
        for l in range(2):
            build_layer(l)
    except StopBuild:
        pass
    P.emit()
    return nc


_NC_CACHE = {}


def _pack_cols(inp, l):
    c = np.zeros((128, NCOLS), np.float32)

    def put(c0, vec):
        v = np.asarray(vec, np.float32).reshape(-1)
        n = v.shape[0] // 128
        c[:, c0:c0 + n] = v.reshape(n, 128).T
    put(CG_PRE, inp["g_pre"][l])
    put(CPSC, inp["pool_scale"][l])
    mu = np.asarray(inp["mu_shift"][l], np.float32)
    put(CMU_R, mu[0:1024])
    put(CMU_K, mu[1024:2048])
    put(CMU_V, mu[2048:3072])
    put(CMU_Z, mu[3072:4096])
    c[0:64, CMU_WA] = mu[4096:4160]
    c[0:64, CMU_A] = mu[4160:4224]
    put(CW0, inp["w0"][l])
    put(CA0, inp["a0"][l])
    if l >= 1:
        put(CMV0, inp["mv0"][l - 1])
        c[0:32, CMU_MV] = np.asarray(inp["mu_mv"][l - 1], np.float32)
    put(CKK, inp["k_k"][l])
    put(CKA, inp["k_a"][l])
    put(CRK, np.asarray(inp["r_k"][l], np.float32).reshape(-1))
    put(CLG, inp["lnx_g"][l])
    put(CLB, inp["lnx_b"][l])
    return c


def kernel(**inputs):
    inp = {k_: np.asarray(v) for k_, v in inputs.items()}
    if "nc" not in _NC_CACHE:
        _NC_CACHE["nc"] = build_program()
    nc = _NC_CACHE["nc"]
    cols = np.stack([_pack_cols(inp, 0), _pack_cols(inp, 1)], axis=0)
    shared = {
        "w_ada": np.ascontiguousarray(inp["w_ada"], np.float32),
        "b_ada": np.ascontiguousarray(inp["b_ada"], np.float32),
        "w_in": np.ascontiguousarray(inp["w_in"], np.float32),
        "w_pool": np.ascontiguousarray(inp["w_pool"], np.float32),
        "w_decay_up": np.ascontiguousarray(inp["w_decay_up"], np.float32),
        "w_aaa_up": np.ascontiguousarray(inp["w_aaa_up"], np.float32),
        "w_mv_down": np.ascontiguousarray(inp["w_mv_down"], np.float32),
        "w_mv_up": np.ascontiguousarray(inp["w_mv_up"], np.float32),
        "w_br_a": np.ascontiguousarray(inp["w_br_a"], np.float32),
        "w_br_b": np.ascontiguousarray(inp["w_br_b"], np.float32),
        "w_out": np.ascontiguousarray(inp["w_out"], np.float32),
        "g_post": np.ascontiguousarray(inp["g_post"], np.float32),
        "cols": cols,
    }
    in_maps = []
    for core in range(8):
        b = core % 4
        m = dict(shared)
        m["x"] = np.ascontiguousarray(inp["x"][b], np.float32)
        m["cT"] = np.ascontiguousarray(np.asarray(inp["c"][b], np.float32).reshape(NKT, 128).T)
        in_maps.append(m)
    res = run_bass_kernel_spmd(nc, in_maps, core_ids=list(range(8)))
    out = np.stack([np.asarray(res.results[b]["y"], np.float32) for b in range(4)], axis=0)
    return out
```

```python
import numpy as np
import concourse.bass as bass
import concourse.mybir as mybir
from concourse.bass_utils import run_bass_kernel_spmd

F32 = mybir.dt.float32
BF16 = mybir.dt.bfloat16
AF = mybir.ActivationFunctionType
ALU = mybir.AluOpType

ENGS = ("pe", "act", "dve", "pool", "sp")
EPOCH = 30000

D = 2048
NT = 2048
NKT = 16
NIN = 10368
NPROJ = 6272
C0 = float(np.exp(-0.5))
RMS_EPS = 1e-6
LNX_EPS = 64e-5

CG_PRE, CPSC, CMU_R, CMU_K, CMU_V, CMU_Z, CMU_WA = 0, 16, 24, 32, 40, 48, 56
CW0, CA0, CMV0, CKK, CKA, CRK, CLG, CLB, CMU_MV, CMU_A = 57, 65, 73, 81, 89, 97, 105, 113, 121, 122
NCOLS = 123


class Prog:
    def __init__(self, nc):
        self.nc = nc
        self.q = {e: [] for e in ENGS}
        self.cnt = {e: 0 for e in ENGS}
        self.seen = {e: {} for e in ENGS}
        self.vc = {}
        self.last_w = {}
        self.readers = {}
        self.sems = {}
        self.dma_cnt = {}
        self.waited = {e: set() for e in ENGS}
        self._stack = []
        self._defer = None

    def _sem(self, key):
        if key not in self.sems:
            cm = self.nc.semaphore("s_%s_%s" % (key[0], key[1]))
            h = cm.__enter__()
            self._stack.append(cm)
            self.sems[key] = h
        return self.sems[key]

    def _note(self, waits):
        for (k_, v) in waits:
            if k_ in self.waited:
                self.waited[k_].add(v)

    def _deps(self, eng, reads, writes, skipkey=None):
        toks = []
        for r in reads:
            t = self.last_w.get(r)
            if t is not None:
                toks.append(t)
        for w in writes:
            t = self.last_w.get(w)
            if t is not None:
                toks.append(t)
            toks.extend(self.readers.get(w, ()))
        need = {}
        for (k, v) in toks:
            if eng == "pe" and k == "pe":
                continue
            if skipkey is not None and k == skipkey:
                continue
            if self.seen[eng].get(k, 0) >= v:
                continue
            if need.get(k, 0) < v:
                need[k] = v
        waits = []
        items = sorted(need.items(), key=lambda kv: -len(self.vc.get((kv[0], kv[1]), ())))
        for k, v in items:
            if self.seen[eng].get(k, 0) >= v:
                continue
            waits.append((k, v))
            self.seen[eng][k] = v
            for k2, v2 in self.vc.get((k, v), {}).items():
                if self.seen[eng].get(k2, 0) < v2:
                    self.seen[eng][k2] = v2
        self._note(waits)
        return waits

    def _finish(self, tok, eng, reads, writes):
        c = dict(self.seen[eng])
        k, v = tok
        c[k] = v
        self.vc[tok] = c
        for r in reads:
            self.readers.setdefault(r, []).append(tok)
        for w in writes:
            self.last_w[w] = tok
            self.readers[w] = []

    def op(self, eng, fn, reads=(), writes=(), cost=0.3):
        if self._defer is not None:
            self._defer.append(("op", eng, fn, tuple(reads), tuple(writes), cost))
            return None
        ps_r = [r for r in reads if r[:2] in ("pb", "pT")]
        if ps_r:
            reads = [r for r in reads if r[:2] not in ("pb", "pT")]
            writes = list(writes) + ps_r
        waits = self._deps(eng, reads, writes)
        self.cnt[eng] += 1
        n = self.cnt[eng]
        tok = (eng, n)
        self.q[eng].append((waits, fn, eng, n))
        self._finish(tok, eng, reads, writes)
        return tok

    def dma(self, eng, out, in_, semname, reads=(), writes=()):
        if self._defer is not None:
            self._defer.append(("dma", eng, (out, in_, semname), tuple(reads), tuple(writes), 2.0))
            return None
        key = ("dma", semname)
        waits = self._deps(eng, reads, writes, skipkey=key)
        self.dma_cnt[key] = self.dma_cnt.get(key, 0) + 16
        tok = (key, self.dma_cnt[key])

        def fn(e, out=out, in_=in_):
            return e.dma_start(out=out, in_=in_)
        self.q[eng].append((waits, fn, key, 16))
        self._finish(tok, eng, reads, writes)
        return tok

    def merge_streams(self, lists, hop=0.05):
        eng_free = {e: 0.0 for e in ENGS}
        kw, kr = {}, {}
        idx = [0] * len(lists)

        def est(rec):
            kind, eng, _, reads, writes, cost = rec
            rd = [r for r in reads if r[:2] not in ("pb", "pT")]
            wr = list(writes) + [r for r in reads if r[:2] in ("pb", "pT")]
            t = eng_free[eng]
            for r in rd:
                t = max(t, kw.get(r, 0.0) + hop)
            for w in wr:
                t = max(t, kw.get(w, 0.0) + hop, kr.get(w, 0.0) + hop)
            return t, rd, wr

        while True:
            best, bt = None, None
            for i, lst in enumerate(lists):
                if idx[i] >= len(lst):
                    continue
                t, _, _ = est(lst[idx[i]])
                if best is None or t < bt - 1e-9:
                    best, bt = i, t
            if best is None:
                break
            rec = lists[best][idx[best]]
            idx[best] += 1
            kind, eng, payload, reads, writes, cost = rec
            t, rd, wr = est(rec)
            fin = t + cost
            eng_free[eng] = t + (0.06 if kind == "dma" else cost)
            for r in rd:
                kr[r] = max(kr.get(r, 0.0), fin)
            for w in wr:
                kw[w] = fin
                kr[w] = 0.0
            if kind == "op":
                self.op(eng, payload, reads, writes)
            else:
                self.dma(eng, payload[0], payload[1], payload[2], reads, writes)
        return max(eng_free.values())

    def dma_like(self, eng, fn, semname, reads=(), writes=()):
        key = ("dma", semname)
        waits = self._deps(eng, reads, writes, skipkey=key)
        self.dma_cnt[key] = self.dma_cnt.get(key, 0) + 16
        tok = (key, self.dma_cnt[key])
        self.q[eng].append((waits, fn, key, 16))
        self._finish(tok, eng, reads, writes)
        return tok

    def barrier(self):
        toks = []
        for e in ENGS:
            if self.cnt[e] > 0:
                toks.append((e, self.cnt[e]))
        for key, val in self.dma_cnt.items():
            toks.append((key, val))
        for e in ENGS:
            waits = []
            for (k_, v) in toks:
                if e == "pe" and k_ == "pe":
                    continue
                if self.seen[e].get(k_, 0) >= v:
                    continue
                waits.append((k_, v))
                self.seen[e][k_] = v
            if waits:
                self._note(waits)
                self.q[e].append((waits, None, None, 0))

    def emit(self):
        rank = {}
        for e in ENGS:
            rank[e] = {idx: i + 1 for i, idx in enumerate(sorted(self.waited[e]))}

        def semval(k_, v):
            if k_ in rank:
                r = rank[k_][v]
                return self._sem((k_, (r - 1) // EPOCH)), (r - 1) % EPOCH + 1
            return self._sem(k_), v
        prog = self
        self.n_inc = {e: len(rank[e]) for e in ENGS}

        def run(engobj, lst):
            for (waits, fn, key, idx) in lst:
                for (k_, v) in waits:
                    sm, val = semval(k_, v)
                    engobj.wait_ge(sm, val)
                if fn is None:
                    continue
                if key in rank:
                    ins = fn(engobj)
                    if idx in rank[key]:
                        sm, _ = semval(key, idx)
                        ins.then_inc(sm, 1)
                else:
                    fn(engobj).then_inc(prog._sem(key), 16)

        with self.nc.Block() as block:
            @block.tensor
            def _(e):
                run(e, prog.q["pe"])

            @block.scalar
            def _(e):
                run(e, prog.q["act"])

            @block.vector
            def _(e):
                run(e, prog.q["dve"])

            @block.gpsimd
            def _(e):
                run(e, prog.q["pool"])

            @block.sync
            def _(e):
                run(e, prog.q["sp"])


DEBUG_OUT = False
STOP = None


class StopBuild(Exception):
    pass


class Arena:
    def __init__(self, ap):
        self.ap = ap
        self.cap = ap.shape[1]
        self.off = 0

    def reset(self):
        self.off = 0

    def take(self, shape, dt):
        n = 1
        for d_ in shape[1:]:
            n *= d_
        units = n * (2 if dt == F32 else 1)
        units = (units + 15) // 16 * 16
        assert self.off + units <= self.cap, ("arena overflow", self.off, units, self.cap)
        v = self.ap[0:shape[0], self.off:self.off + units]
        self.off += units
        if dt == F32:
            v = v.bitcast(F32)
        v = v[:, 0:n]
        if len(shape) == 3:
            v = v.rearrange("p (a b) -> p a b", b=shape[2])
        return v


class K:
    def __init__(self, nc):
        self.nc = nc
        self.P = Prog(nc)

    def sb(self, name, shape, dt):
        return self.nc.alloc_sbuf_tensor(name, list(shape), dt).ap()

    def ps(self, name, shape, dt):
        return self.nc.alloc_psum_tensor(name, list(shape), dt).ap()

    @staticmethod
    def _n(ap):
        n = 1
        for d_ in ap.shape[1:]:
            n *= d_
        return n

    def _cd(self, out, eng="dve", mult=1.0):
        n = self._n(out)
        if eng == "act":
            return (224.0 + n) / 1200.0
        c = (64.0 + n) / 960.0 * mult
        return c * (2.0 if eng == "pool" else 1.0)

    def mm(self, out, lhsT, rhs, start, stop, r, w):
        self.P.op("pe", lambda e: e.matmul(out, lhsT, rhs, start=start, stop=stop), r, w,
                  cost=max(64.0, self._n(out)) / 2400.0 + 0.01)

    def tr(self, out, in_, ident, r, w):
        self.P.op("pe", lambda e: e.transpose(out, in_, ident), r, w, cost=0.07)

    def act(self, out, in_, func, r, w, bias=None, scale=1.0, accum=None):
        def fn(e):
            kw = {}
            if bias is not None:
                kw["bias"] = bias
            if accum is not None:
                kw["accum_out"] = accum
            return e.activation(out=out, in_=in_, func=func, scale=scale, **kw)
        self.P.op("act", fn, r, w, cost=self._cd(out, "act"))

    def tt(self, out, a, b, op, r, w, eng="dve"):
        self.P.op(eng, lambda e: e.tensor_tensor(out=out, in0=a, in1=b, op=op), r, w, cost=self._cd(out, eng))

    def ts1(self, out, a, s, op, r, w, eng="dve"):
        self.P.op(eng, lambda e: e.tensor_single_scalar(out=out, in_=a, scalar=s, op=op), r, w, cost=self._cd(out, eng))

    def ts2(self, out, a, s1, s2, op0, op1, r, w, eng="dve"):
        self.P.op(eng, lambda e: e.tensor_scalar(out=out, in0=a, scalar1=s1, scalar2=s2, op0=op0, op1=op1), r, w,
                  cost=self._cd(out, eng))

    def stt(self, out, a, s, b, op0, op1, r, w, eng="dve"):
        self.P.op(eng, lambda e: e.scalar_tensor_tensor(out=out, in0=a, scalar=s, in1=b, op0=op0, op1=op1), r, w,
                  cost=self._cd(out, eng))

    def cp(self, out, in_, r, w, eng="dve"):
        if eng == "act":
            self.act(out, in_, AF.Copy, r, w)
        else:
            self.P.op(eng, lambda e: e.tensor_copy(out=out, in_=in_), r, w, cost=self._cd(out, eng))

    def recip(self, out, in_, r, w):
        self.P.op("dve", lambda e: e.reciprocal(out=out, in_=in_), r, w, cost=self._cd(out, "dve", 6.0))

    def memset(self, ap, val, w, eng="dve"):
        self.P.op(eng, lambda e: e.memset(ap, val), (), w, cost=self._cd(ap, eng, 0.5))

    def scan(self, out, d0, d1, r, w):
        self.P.op("dve", lambda e: e.tensor_tensor_scan(out=out, data0=d0, data1=d1, initial=0.0,
                                                        op0=ALU.mult, op1=ALU.add), r, w, cost=self._cd(out, "dve", 2.0))

    def asel(self, out, in_, pattern, cmp_op, fill, cm, r, w):
        self.P.op("pool", lambda e: e.affine_select(out=out, in_=in_, pattern=pattern, compare_op=cmp_op,
                                                     fill=fill, base=0, channel_multiplier=cm), r, w)

    def dma(self, eng, out, in_, sem, r=(), w=()):
        self.P.dma(eng, out, in_, sem, r, w)


def build_program():
    nc = bass.Bass("TRN2", target_bir_lowering=False)
    k = K(nc)
    P = k.P

    def din(name, shape, dt=F32):
        return nc.dram_tensor(name, list(shape), dt, kind="ExternalInput").ap()

    x_in = din("x", [NT, D])
    cT_in = din("cT", [128, NKT])
    w_ada = din("w_ada", [2, D, 3 * D])
    b_ada = din("b_ada", [2, 3 * D])
    w_in = din("w_in", [2, D, NIN])
    w_pool = din("w_pool", [2, 4, 256, 256])
    w_du = din("w_decay_up", [2, 64, 1024])
    w_au = din("w_aaa_up", [2, 64, 1024])
    w_mvd = din("w_mv_down", [1, D, 32])
    w_mvu = din("w_mv_up", [1, 32, 1024])
    w_bra = din("w_br_a", [2, 1024, D])
    w_brb = din("w_br_b", [2, 1024, D])
    w_out = din("w_out", [2, D, D])
    g_post = din("g_post", [2, D])
    cols_in = din("cols", [2, 128, NCOLS])
    y_out = nc.dram_tensor("y", [NT, D], F32, kind="ExternalOutput").ap()

    skind = "ExternalOutput" if DEBUG_OUT else "Internal"
    projT = nc.dram_tensor("projT", [NPROJ, NT], F32, kind=skind).ap()
    sgT = nc.dram_tensor("sgT", [2 * D, NT], BF16, kind=skind).ap()
    vfT = nc.dram_tensor("vfT", [1024, NT], F32, kind=skind).ap()
    xmid = nc.dram_tensor("xmid", [NT, D], F32, kind=skind).ap()
    ggd = nc.dram_tensor("ggd", [1, D], F32, kind="Internal").ap()
    lorad = nc.dram_tensor("lorad", [2, 1024, NT], F32, kind="Internal").ap()

    if DEBUG_OUT:
        dbg_big = nc.dram_tensor("dbg_big", [2, 128, 16, NT], BF16, kind="ExternalOutput").ap()
    big = k.sb("big", [128, 16, NT], BF16)
    hT = big
    yaT = big[:, 0:8, :]
    ybT = big[:, 8:16, :]
    arA = k.sb("arenaA", [128, 32768], BF16)
    arB = k.sb("arenaB", [128, 30208], BF16)
    A = Arena(arA)
    B = Arena(arB)
    cols = [k.sb("cols%d" % l, [128, NCOLS], F32) for l in range(2)]
    small = k.sb("small", [128, 64], F32)
    shc = k.sb("shc", [128, 16], F32)
    gsc = k.sb("gsc", [128, 16], F32)
    omk = k.sb("omk", [128, 8], F32)
    cT = k.sb("cTs", [128, NKT], F32)
    condT = k.sb("condT", [128, NKT], BF16)
    onesrow = k.sb("onesrow", [1, 128], F32)
    ident = k.sb("ident", [128, 128], BF16)
    identx = k.sb("identx", [64, 512], F32)
    m_su = k.sb("m_su", [64, 512], BF16)
    m_u = k.sb("m_u", [64, 512], BF16)
    m_sl = k.sb("m_sl", [64, 512], BF16)
    bd1 = k.sb("bd1", [128, 128], BF16)
    bdm = k.sb("bdm", [128, 128], BF16)
    smask = k.sb("smask", [128, 512], F32)
    epsr = k.sb("epsr", [128, 1], F32)
    epsl = k.sb("epsl", [128, 1], F32)
    mvs = k.sb("mvs", [32, NT], BF16)
    negc = k.sb("negc", [128, 24], F32)
    onec = k.sb("onec", [128, 1], F32)
    cnt16 = k.sb("cnt16", [128, 16], F32)
    one16 = k.sb("one16", [128, 16], F32)
    Hf = [k.sb("Hf%d" % p, [128, 128], F32) for p in range(8)]
    Hb = [k.sb("Hb%d" % p, [128, 128], BF16) for p in range(8)]

    pb = [k.ps("pb%d" % i, [128, 512], F32) for i in range(6)]
    pT = [k.ps("pT%d" % i, [128, 1024], BF16) for i in range(2)]

    identf = A.take([128, 128], F32)
    identxf = A.take([64, 512], F32)
    mtmp = A.take([64, 512], F32)
    k.memset(identf, 0.0, ["identf"])
    k.asel(identf, identf, [[-1, 128]], ALU.not_equal, 1.0, 1, ["identf"], ["identf"])
    k.cp(ident, identf, ["identf"], ["ident"])
    ix3 = identxf.rearrange("p (u j) -> p u j", j=64)
    k.memset(identxf, 0.0, ["identxf"])
    k.asel(ix3, ix3, [[0, 8], [-1, 64]], ALU.not_equal, 1.0, 1, ["identxf"], ["identxf"])
    k.cp(identx, identxf, ["identxf"], ["identx"])
    for (m, mk_, cmpop, cm, coef) in ((m_su, "m_su", ALU.is_gt, -1, 1), (m_u, "m_u", ALU.is_ge, -1, 1),
                                      (m_sl, "m_sl", ALU.is_gt, 1, -1)):
        m3 = mtmp.rearrange("p (u j) -> p u j", j=64)
        k.memset(mtmp, 1.0, ["mtmp"])
        k.asel(m3, m3, [[0, 8], [coef, 64]], cmpop, 0.0, cm, ["mtmp"], ["mtmp"])
        k.cp(m, mtmp, ["mtmp"], [mk_])
    k.memset(bd1, 0.0, ["bd1"])
    k.memset(bd1[0:64, 0:64], 1.0, ["bd1"])
    k.memset(bd1[64:128, 64:128], 1.0, ["bd1"])
    k.memset(bdm, 0.0, ["bdm"])
    k.memset(bdm[0:64, 0:64], 1.0 / 64, ["bdm"])
    k.memset(bdm[64:128, 64:128], 1.0 / 64, ["bdm"])
    k.memset(smask, 1.0, ["smask"])
    k.memset(smask.rearrange("p (c j) -> p c j", j=64)[:, :, 0:1], 0.0, ["smask"])
    k.memset(epsr, RMS_EPS, ["epsr"])
    k.memset(epsl, LNX_EPS, ["epsl"])
    k.memset(onesrow, 1.0, ["onesrow"])
    k.memset(small, 0.0, ["small"])
    k.memset(one16, 1.0, ["one16"])
    k.memset(onec, 1.0, ["onec"])
    k.scan(cnt16, one16, one16, ["one16"], ["cnt16"])

    k.dma("sp", cT, cT_in, "l_c", w=["cT"])
    for l in range(2):
        k.dma("sp", cols[l], cols_in[l], "l_cols%d" % l, w=["cols%d" % l])
    k.act(condT, cT, AF.Silu, ["cT"], ["condT"])
    P.barrier()

    evac_rr = [0]
    stage_rr = [0]
    wgrp = [None, None]

    def col(l, c):
        return cols[l][:, c:c + 1]

    def load_wgrp(l, slot, src, c0, wd):
        v = src.rearrange("(kt p) n -> p kt n", p=128)
        for q4 in range(4):
            k.dma("pool", wgrp[slot][:, q4 * 4:(q4 + 1) * 4, 0:wd], v[:, q4 * 4:(q4 + 1) * 4, c0:c0 + wd],
                  "l_wg%d" % slot, w=["wgrp%d" % slot])

    def stage_end(l, tag):
        if STOP == "%d:%s" % (l, tag):
            P.barrier()
            raise StopBuild()

    def build_layer(l):
        xsrc = x_in if l == 0 else xmid
        xdst = xmid if l == 0 else y_out
        ck = "cols%d" % l
        A.reset()
        B.reset()
        modrow = A.take([1, 3 * D], F32)
        gprow = A.take([1, D], F32)
        ggrow = A.take([1, D], F32)
        brow = A.take([1, 512], F32)
        wgrp[0] = B.take([128, 16, 512], BF16)
        wgrp[1] = B.take([128, 16, 512], BF16)
        k.dma("sp", gprow, g_post[l:l + 1, :], "l_gp", w=["gprow"])
        for cg in range(12):
            slot = cg % 2
            load_wgrp(l, slot, w_ada[l], cg * 512, 512)
            k.dma("sp", brow, b_ada[l:l + 1, cg * 512:(cg + 1) * 512], "l_bada", w=["brow"])
            for kt in range(NKT):
                k.mm(pb[0][0:1, :], condT[:, kt:kt + 1], wgrp[slot][:, kt, :], kt == 0, kt == NKT - 1,
                     ["condT", "wgrp%d" % slot], ["pb0"])
            k.tt(modrow[:, cg * 512:(cg + 1) * 512], pb[0][0:1, :], brow, ALU.add,
                 ["pb0", "brow"], ["modrow"])
        for i in range(32):
            k.mm(pb[1][:, i:i + 1], modrow[0:1, i * 128:(i + 1) * 128], onesrow[0:1, 0:1], True, True,
                 ["modrow", "onesrow"], ["pb1"])
        k.cp(shc, pb[1][:, 0:16], ["pb1"], ["shc"])
        k.stt(gsc, pb[1][:, 16:32], 1.0, cols[l][:, CG_PRE:CG_PRE + 16], ALU.add, ALU.mult, ["pb1", ck], ["gsc"])
        k.tt(ggrow, modrow[:, 2 * D:3 * D], gprow, ALU.mult, ["modrow", "gprow"], ["ggrow"])
        k.dma("sp", ggd, ggrow, "s_ggd", r=["ggrow"], w=["ggd"])
        k.ts2(omk, cols[l][:, CKA:CKA + 8], -1.0, 1.0, ALU.mult, ALU.add, [ck], ["omk"])
        P.barrier()
        A.reset()
        B.reset()
        xt = [A.take([128, D], F32) for _ in range(2)]
        junk = A.take([128, D], F32)
        xnb = [A.take([128, D], BF16) for _ in range(4)]
        for tg in range(4):
            for j in range(4):
                tt_ = tg * 4 + j
                xs = xt[tt_ % 2]
                k.dma("sp", xs, xsrc[tt_ * 128:(tt_ + 1) * 128, :], "l_xt%d" % (tt_ % 2), w=["xt%d" % (tt_ % 2)])
                k.act(junk, xs, AF.Square, ["xt%d" % (tt_ % 2)], ["junk", "small"], accum=small[:, tt_:tt_ + 1])
                k.act(small[:, 16 + tt_:17 + tt_], small[:, tt_:tt_ + 1], AF.Sqrt, ["small", "epsr"], ["small"],
                      bias=epsr, scale=1.0 / D)
                k.recip(small[:, 32 + tt_:33 + tt_], small[:, 16 + tt_:17 + tt_], ["small"], ["small"])
                k.ts1(xnb[j], xs, small[:, 32 + tt_:33 + tt_], ALU.mult, ["xt%d" % (tt_ % 2), "small"], ["xnb%d" % j])
            for ft in range(NKT):
                pt = pT[ft % 2]
                for j in range(4):
                    k.tr(pt[:, j * 128:(j + 1) * 128], xnb[j][:, ft * 128:(ft + 1) * 128], ident,
                         ["xnb%d" % j, "ident"], ["pT%d" % (ft % 2)])
                k.act(hT[:, ft, tg * 512:(tg + 1) * 512], pt[:, 0:512], AF.Identity, ["pT%d" % (ft % 2), "gsc", "shc"],
                      ["hT"], bias=shc[:, ft:ft + 1], scale=gsc[:, ft:ft + 1])
        k.memset(small[:, 0:16], 0.0, ["small"])

        stage_end(l, "S1")
        P.barrier()
        A.reset()
        B.reset()
        stage = [A.take([128, 512], F32) for _ in range(4)]
        stageb = [A.take([128, 512], BF16) for _ in range(4)]
        mvraw = A.take([32, 1 + NT], F32)
        f_dm = A.take([32, 512], F32)
        wgrp[0] = B.take([128, 16, 512], BF16)
        wgrp[1] = B.take([128, 16, 512], BF16)
        wmvd = B.take([128, 16, 32], BF16)
        if l == 1:
            k.memset(mvraw[:, 0:1], 0.0, ["mvraw"])
            k.dma("pool", wmvd, w_mvd[0].rearrange("(kt p) n -> p kt n", p=128), "l_wmvd", w=["wmvd"])
            for tb in range(4):
                for kt in range(NKT):
                    k.mm(pb[4][0:32, :], wmvd[:, kt, :], hT[:, kt, tb * 512:(tb + 1) * 512], kt == 0, kt == NKT - 1,
                         ["wmvd", "hT"], ["pb4"])
                k.cp(mvraw[:, 1 + tb * 512:1 + (tb + 1) * 512], pb[4][0:32, :], ["pb4"], ["mvraw"])
            for tb in range(4):
                k.tt(f_dm, mvraw[:, tb * 512:tb * 512 + 512], mvraw[:, 1 + tb * 512:1 + tb * 512 + 512], ALU.subtract,
                     ["mvraw"], ["f_dm"])
                k.stt(mvs[:, tb * 512:(tb + 1) * 512], f_dm, cols[l][0:32, CMU_MV:CMU_MV + 1],
                      mvraw[:, 1 + tb * 512:1 + tb * 512 + 512], ALU.mult, ALU.add, ["f_dm", ck, "mvraw"], ["mvs"])
        groups = [(g * 512, 512) for g in range(20)] + [(10240, 128)]
        for gi, (c0, wd) in enumerate(groups):
            slot = gi % 2
            load_wgrp(l, slot, w_in[l], c0, wd)
            for cb in range(wd // 128):
                cc = c0 + cb * 128
                for tb in range(4):
                    bi = evac_rr[0] % 4
                    evac_rr[0] += 1
                    bank = pb[bi]
                    for kt in range(NKT):
                        k.mm(bank, wgrp[slot][:, kt, cb * 128:(cb + 1) * 128], hT[:, kt, tb * 512:(tb + 1) * 512],
                             kt == 0, kt == NKT - 1, ["wgrp%d" % slot, "hT"], ["pb%d" % bi])
                    si = stage_rr[0] % 4
                    stage_rr[0] += 1
                    if cc >= NPROJ:
                        k.act(stageb[si], bank, AF.Sigmoid, ["pb%d" % bi], ["stageb%d" % si])
                        k.dma("sp", sgT[cc - NPROJ:cc - NPROJ + 128, tb * 512:(tb + 1) * 512], stageb[si],
                              "s_stb%d" % si, r=["stageb%d" % si])
                    else:
                        if 1024 <= cc < 2048:
                            k.act(stage[si], bank, AF.Silu, ["pb%d" % bi], ["stage%d" % si])
                        elif si % 2 == 0:
                            k.cp(stage[si], bank, ["pb%d" % bi], ["stage%d" % si], eng="dve")
                        else:
                            k.cp(stage[si], bank, ["pb%d" % bi], ["stage%d" % si], eng="act")
                        k.dma("sp", projT[cc:cc + 128, tb * 512:(tb + 1) * 512], stage[si],
                              "s_st%d" % si, r=["stage%d" % si], w=["projT"])

        stage_end(l, "S2")
        P.barrier()
        A.reset()
        B.reset()
        ubuf = [A.take([128, 16 + NT], F32) for _ in range(2)]
        sA = A.take([128, 16 + NT], F32)
        sB = A.take([128, 16 + NT], F32)
        invc = A.take([128, NT], F32)
        zab = A.take([128, NT], F32)
        plb = [B.take([128, NT], BF16) for _ in range(2)]
        wpl = B.take([128, 2, 256], BF16)
        for i in range(2):
            k.memset(ubuf[i][:, 0:16], 0.0, ["ubuf%d" % i])
        k.memset(sA[:, 0:16], 0.0, ["sA"])
        k.memset(sB[:, 0:16], 0.0, ["sB"])
        for g in range(4):
            wwin = 2 ** (g + 1)
            k.dma("pool", wpl, w_pool[l, g].rearrange("(ck p) d -> p ck d", p=128), "l_wpl", w=["wpl"])
            k.memset(invc, 1.0 / wwin, ["invc"])
            k.ts1(invc[:, 0:16], cnt16, float(wwin), ALU.min, ["cnt16"], ["invc"])
            k.recip(invc[:, 0:16], invc[:, 0:16], ["invc"], ["invc"])
            for ck_ in range(2):
                ct = 2 * g + ck_
                ub = ubuf[ck_]
                uk = "ubuf%d" % ck_
                k.dma("sp", ub[:, 16:], projT[ct * 128:(ct + 1) * 128, :], "l_ub%d" % ck_, r=["projT"], w=[uk])
                cur, curk = ub, uk
                sh = 1
                bufs = [(sA, "sA"), (sB, "sB")]
                bi2 = 0
                while sh < wwin:
                    nb, nk = bufs[bi2 % 2]
                    bi2 += 1
                    k.tt(nb[:, 16:], cur[:, 16:], cur[:, 16 - sh:16 + NT - sh], ALU.add, [curk], [nk])
                    cur, curk = nb, nk
                    sh *= 2
                k.tt(cur[:, 16:], cur[:, 16:], invc, ALU.mult, [curk, "invc"], [curk])
                k.tt(plb[ck_], cur[:, 16:], ub[:, 16:], ALU.subtract, [curk, uk], ["plb%d" % ck_])
            if g == 0:
                stage_end(l, "S3a")
            for db in range(2):
                dt_ = 2 * g + db
                k.dma("sp", zab, projT[1024 + dt_ * 128:1024 + (dt_ + 1) * 128, :], "l_zab", r=["projT"], w=["zab"])
                for tb in range(4):
                    bi = evac_rr[0] % 4
                    evac_rr[0] += 1
                    for ck_ in range(2):
                        k.mm(pb[bi], wpl[:, ck_, db * 128:(db + 1) * 128], plb[ck_][:, tb * 512:(tb + 1) * 512],
                             ck_ == 0, ck_ == 1, ["wpl", "plb%d" % ck_], ["pb%d" % bi])
                    k.stt(yaT[:, dt_, tb * 512:(tb + 1) * 512], pb[bi], col(l, CPSC + dt_), zab[:, tb * 512:(tb + 1) * 512],
                          ALU.mult, ALU.mult, ["pb%d" % bi, ck, "zab"], ["yaT", "hT"])
            if g == 0:
                stage_end(l, "S3b")

        stage_end(l, "S3")
        P.barrier()
        A.reset()
        B.reset()
        walo = A.take([64, 1 + NT], F32)
        pfd = A.take([64, 512], F32)
        pft = A.take([64, 512], F32)
        pstg = [A.take([128, 512], F32) for _ in range(4)]
        tw_w = B.take([64, NT], BF16)
        tw_a = B.take([64, NT], BF16)
        lw_w = B.take([64, 1024], BF16)
        lw_a = B.take([64, 1024], BF16)
        k.memset(walo[:, 0:1], 0.0, ["walo"])
        k.dma("pool", lw_w, w_du[l], "l_lw", w=["lw"])
        k.dma("pool", lw_a, w_au[l], "l_lw", w=["lw"])
        for which in range(2):
            k.dma("sp", walo[:, 1:], projT[6144 + which * 64:6208 + which * 64, :], "l_walo", r=["projT"], w=["walo"])
            mucol = cols[l][0:64, (CMU_WA if which == 0 else CMU_A):(CMU_WA if which == 0 else CMU_A) + 1]
            for tb in range(4):
                sl = slice(tb * 512, (tb + 1) * 512)
                k.tt(pfd, walo[:, tb * 512:tb * 512 + 512], walo[:, 1 + tb * 512:1 + tb * 512 + 512], ALU.subtract,
                     ["walo"], ["pfd"])
                k.stt(pft, pfd, mucol, walo[:, 1 + tb * 512:1 + tb * 512 + 512], ALU.mult, ALU.add,
                      ["pfd", ck, "walo"], ["pft"])
                if which == 0:
                    k.act(tw_w[:, sl], pft, AF.Tanh, ["pft"], ["tw_w"])
                else:
                    k.cp(tw_a[:, sl], pft, ["pft"], ["tw_a"])
        pi = 0
        for p in range(8):
            for which in range(2):
                lwx, twx, twk = (lw_w, tw_w, "tw_w") if which == 0 else (lw_a, tw_a, "tw_a")
                for tb in range(4):
                    bi = pi % 4
                    pi += 1
                    k.mm(pb[bi], lwx[:, p * 128:(p + 1) * 128], twx[:, tb * 512:(tb + 1) * 512], True, True, ["lw", twk], ["pb%d" % bi])
                    k.cp(pstg[bi], pb[bi], ["pb%d" % bi], ["pstg%d" % bi], eng=("act" if bi % 2 else "dve"))
                    k.dma("sp", lorad[which, p * 128:(p + 1) * 128, tb * 512:(tb + 1) * 512], pstg[bi], "s_pst%d" % bi,
                          r=["pstg%d" % bi], w=["lorad"])
        P.barrier()
        A.reset()
        B.reset()
        W = 256
        NB = NT // W
        NCK = W // 64
        WS = []
        for s_ in range(2):
            ws = {}
            for nm in ("r", "k", "v", "z"):
                ws["raw_" + nm] = A.take([128, W + 1], F32)
            for nm in ("f_r", "f_k", "f_v", "f_z", "f_d", "f_sg", "f_a", "f_t1", "f_kkn", "f_kmod", "f_ka",
                       "f_S", "f_Sp", "f_Se", "f_Wt", "f_Wi", "f_Wp", "f_Wc", "f_bonus"):
                ws[nm] = A.take([128, W], F32)
            ws["f_WC"] = A.take([128, NCK], F32)
            ws["mZ"] = A.take([64, 512], F32)
            ws["mX"] = A.take([64, 512], BF16)
            ws["mIM"] = A.take([64, 512], F32)
            for nm in ("b_x", "b_at", "b_rt", "b_kh", "b_bh", "b_v", "b_atp"):
                ws[nm] = B.take([128, W], BF16)
            ws["b_bt"] = [B.take([128, W], BF16) for _ in range(2)]
            ws["b_kt"] = [B.take([128, W], BF16) for _ in range(2)]
            ws["a_tok"] = B.take([64, NCK, 128], BF16)
            ws["PTb"] = B.take([64, 512], BF16)
            ws["kh_tok"] = B.take([64, NCK, 256], BF16)
            ws["bh_tok"] = B.take([64, NCK, 256], BF16)
            ws["Vpad"] = B.take([64, NCK, 256], BF16)
            ws["Upad"] = B.take([64, 256], BF16)
            ws["mN"] = [B.take([64, 512], F32) for _ in range(2)]
            ws["mM"] = [B.take([64, 512], F32) for _ in range(2)]
            ws["mP"] = [B.take([64, 512], F32) for _ in range(2)]
            for nm in ("mAK", "mRB", "mRK"):
                ws[nm] = B.take([64, 512], BF16)
            WS.append(ws)
        wmvu = A.take([32, 1024], BF16)
        SLOTTED = set(["raw_r", "raw_k", "raw_v", "raw_z", "f_r", "f_k", "f_v", "f_z", "f_d", "f_sg", "f_a", "f_t1",
                       "f_kkn", "f_kmod", "f_ka", "f_S", "f_Sp", "f_Se", "f_Wt", "f_Wi", "f_Wp", "f_Wc", "f_bonus",
                       "f_WC", "b_x", "b_at", "b_rt", "b_kh", "b_bh", "b_v", "b_bt", "b_kt", "a_tok", "kh_tok",
                       "bh_tok", "Vpad", "Upad", "b_atp", "mZ", "mN0", "mN1", "mM0", "mM1", "mP0", "mP1", "mIM",
                       "mAK", "mRB", "mRK", "mX", "vfT", "PTb"])

        class KS:
            def __init__(self, slot):
                self.slot = slot

            def __getattr__(self, name):
                f = getattr(k, name)
                slot = self.slot

                def mapk(a):
                    if isinstance(a, (list, tuple)) and len(a) > 0 and all(isinstance(x_, str) for x_ in a):
                        return [(x_ + "@%d" % slot) if x_ in SLOTTED else x_ for x_ in a]
                    return a

                def wrapped(*args, **kw):
                    return f(*[mapk(a_) for a_ in args], **{kk_: mapk(v_) for kk_, v_ in kw.items()})
                return wrapped

        k.ts1(negc, cols[l][:, CW0:CW0 + 24], -1.0, ALU.mult, [ck], ["negc"])
        for s_ in range(2):
            ks0 = KS(s_)
            ks0.memset(WS[s_]["Upad"], 0.0, ["Upad"])
            ks0.memset(WS[s_]["Vpad"], 0.0, ["Vpad"])
            ks0.memset(WS[s_]["kh_tok"], 0.0, ["kh_tok"])
            ks0.memset(WS[s_]["bh_tok"], 0.0, ["bh_tok"])
            for h_ in range(2):
                ks0.memset(WS[s_]["b_bt"][h_], 0.0, ["b_bt"])
                ks0.memset(WS[s_]["b_kt"][h_], 0.0, ["b_kt"])
        if l == 1:
            k.dma("pool", wmvu, w_mvu[0], "l_wmvu", w=["wmvu"])
        stage_end(l, "S4a")
        for p in range(8):
            k.memset(Hf[p], 0.0, ["Hf%d" % p])
            k.memset(Hb[p], 0.0, ["Hb%d" % p])

        def hr(h):
            return slice(h * 64, (h + 1) * 64)

        def us(u):
            return slice(u * 64, (u + 1) * 64)

        def tk(cc):
            return slice(cc * 64, (cc + 1) * 64)

        def it_gen(p, tb, slot):
            ws = WS[slot]
            kk_ = KS(slot)
            hk, hbk = "Hf%d" % p, "Hb%d" % p
            t0 = tb * W
            raw = {nm: ws["raw_" + nm] for nm in ("r", "k", "v", "z")}
            f_r, f_k, f_v, f_z, f_d, f_sg, f_a, f_t1 = (ws[n_] for n_ in ("f_r", "f_k", "f_v", "f_z", "f_d", "f_sg", "f_a", "f_t1"))
            f_kkn, f_kmod, f_ka, f_S, f_Sp, f_Se = (ws[n_] for n_ in ("f_kkn", "f_kmod", "f_ka", "f_S", "f_Sp", "f_Se"))
            f_Wt, f_Wi, f_Wp, f_Wc, f_bonus, f_WC = (ws[n_] for n_ in ("f_Wt", "f_Wi", "f_Wp", "f_Wc", "f_bonus", "f_WC"))
            f_vf, f_g, f_kk, f_y = f_S, f_Sp, f_Se, f_Wp
            b_x, b_at, b_rt, b_kh, b_bh, b_v = (ws[n_] for n_ in ("b_x", "b_at", "b_rt", "b_kh", "b_bh", "b_v"))
            b_bt, b_kt = ws["b_bt"], ws["b_kt"]
            a_tok, kh_tok, bh_tok, Vpad = ws["a_tok"], ws["kh_tok"], ws["bh_tok"], ws["Vpad"]
            Upad, b_atp, mZ = ws["Upad"], ws["b_atp"], ws["mZ"]
            mN, mM, mP = ws["mN"], ws["mM"], ws["mP"]
            mIM, mAK, mRB, mRK, mX = (ws[n_] for n_ in ("mIM", "mAK", "mRB", "mRK", "mX"))
            PTb = ws["PTb"]
            X0, X1, X2 = pb[3 * slot], pb[3 * slot + 1], pb[3 * slot + 2]
            K0, K1, K2 = "pb%d" % (3 * slot), "pb%d" % (3 * slot + 1), "pb%d" % (3 * slot + 2)
            ptb, ptk = pT[slot], "pT%d" % slot
            fx = {"r": f_r, "k": f_k, "v": f_v, "z": f_z}
            mu0 = {"r": CMU_R, "k": CMU_K, "v": CMU_V, "z": CMU_Z}
            for wi, nm in enumerate(("r", "k", "v", "z")):
                row0 = 2048 + wi * 1024 + p * 128
                rb, rk_ = raw[nm], "raw_" + nm
                sem = "l_raw%s%d" % (nm, slot)
                if tb == 0:
                    kk_.memset(rb[:, 0:1], 0.0, [rk_])
                    kk_.dma("sp", rb[:, 1:W + 1], projT[row0:row0 + 128, 0:W], sem, r=["projT"], w=[rk_])
                else:
                    kk_.dma("sp", rb[:, 0:W + 1], projT[row0:row0 + 128, t0 - 1:t0 + W], sem, r=["projT"], w=[rk_])
            for wi, nm in enumerate(("r", "k", "v", "z")):
                rb, rk_ = raw[nm], "raw_" + nm
                kk_.tt(f_d, rb[:, 0:W], rb[:, 1:W + 1], ALU.subtract, [rk_], ["f_d"])
                kk_.stt(fx[nm], f_d, col(l, mu0[nm] + p), rb[:, 1:W + 1], ALU.mult, ALU.add, ["f_d", ck, rk_], ["f_" + nm])
            yield "prep"
            if l == 0:
                kk_.dma("sp", vfT[p * 128:(p + 1) * 128, t0:t0 + W], f_v, "s_vf%d" % slot, r=["f_v"], w=["vfT"])
            else:
                kk_.dma("sp", f_vf, vfT[p * 128:(p + 1) * 128, t0:t0 + W], "l_vf%d" % slot, r=["vfT"], w=["f_S"])
                kk_.mm(X1[:, 0:W], wmvu[0:32, p * 128:(p + 1) * 128], mvs[0:32, t0:t0 + W], True, True, ["wmvu", "mvs"], [K1])
                kk_.act(f_g, X1[:, 0:W], AF.Exp, [K1, "negc"], ["f_Sp"], bias=negc[:, 16 + p:17 + p], scale=-1.0)
                kk_.act(f_g, f_g, AF.Ln, ["f_Sp", "onec"], ["f_Sp"], bias=onec)
                kk_.act(f_g, f_g, AF.Exp, ["f_Sp"], ["f_Sp"], scale=-1.0)
                kk_.tt(f_d, f_vf, f_v, ALU.subtract, ["f_S", "f_v"], ["f_d"])
                kk_.tt(f_d, f_d, f_g, ALU.mult, ["f_d", "f_Sp"], ["f_d"])
                kk_.tt(f_v, f_v, f_d, ALU.add, ["f_v", "f_d"], ["f_v"])
            kk_.dma("sp", f_sg, lorad[0, p * 128:(p + 1) * 128, t0:t0 + W], "l_sg%d" % slot, r=["lorad"], w=["f_sg"])
            kk_.dma("sp", f_a, lorad[1, p * 128:(p + 1) * 128, t0:t0 + W], "l_fa%d" % slot, r=["lorad"], w=["f_a"])
            kk_.act(f_sg, f_sg, AF.Exp, ["f_sg", "negc"], ["f_sg"], bias=negc[:, p:p + 1], scale=-1.0)
            kk_.act(f_a, f_a, AF.Exp, ["f_a", "negc"], ["f_a"], bias=negc[:, 8 + p:9 + p], scale=-1.0)
            kk_.act(f_sg, f_sg, AF.Ln, ["f_sg", "onec"], ["f_sg"], bias=onec)
            kk_.act(f_a, f_a, AF.Ln, ["f_a", "onec"], ["f_a"], bias=onec)
            kk_.act(f_sg, f_sg, AF.Exp, ["f_sg"], ["f_sg"], scale=-1.0)
            kk_.act(f_a, f_a, AF.Exp, ["f_a"], ["f_a"], scale=-1.0)
            yield "prep"
            kk_.ts1(f_kk, f_k, col(l, CKK + p), ALU.mult, ["f_k", ck], ["f_Se"])
            kk_.tt(b_x, f_kk, f_kk, ALU.mult, ["f_Se"], ["b_x"])
            kk_.mm(X1[:, 0:W], bd1, b_x, True, True, ["bd1", "b_x"], [K1])
            kk_.ts1(f_t1, X1[:, 0:W], 1e-24, ALU.max, [K1], ["f_t1"])
            kk_.act(f_t1, f_t1, AF.Ln, ["f_t1"], ["f_t1"])
            kk_.act(f_t1, f_t1, AF.Exp, ["f_t1"], ["f_t1"], scale=-0.5)
            kk_.tt(f_kkn, f_kk, f_t1, ALU.mult, ["f_Se", "f_t1"], ["f_kkn"])
            kk_.ts2(f_t1, f_a, col(l, CKA + p), omk[:, p:p + 1], ALU.mult, ALU.add, ["f_a", ck, "omk"], ["f_t1"])
            kk_.tt(f_kmod, f_t1, f_k, ALU.mult, ["f_t1", "f_k"], ["f_kmod"])
            kk_.tt(f_ka, f_kkn, f_a, ALU.mult, ["f_kkn", "f_a"], ["f_ka"])
            yield "prep"
            kk_.tt(f_t1, f_r, f_kmod, ALU.mult, ["f_r", "f_kmod"], ["f_t1"])
            kk_.ts1(b_x, f_t1, col(l, CRK + p), ALU.mult, ["f_t1", ck], ["b_x"])
            kk_.mm(X2[:, 0:W], bd1, b_x, True, True, ["bd1", "b_x"], [K2])
            kk_.tt(f_bonus, X2[:, 0:W], f_v, ALU.mult, [K2, "f_v"], ["f_bonus"])
            kk_.scan(f_S, smask[:, 0:W], f_sg, ["smask", "f_sg"], ["f_S"])
            kk_.tt(f_Sp, f_S, f_sg, ALU.subtract, ["f_S", "f_sg"], ["f_Sp"])
            S3 = f_S.rearrange("p (c j) -> p c j", j=64)
            kk_.tt(f_Se.rearrange("p (c j) -> p c j", j=64), S3[:, :, 63:64].to_broadcast([128, NCK, 64]), S3, ALU.subtract,
                   ["f_S"], ["f_Se"])
            kk_.act(f_Wt, f_S, AF.Exp, ["f_S"], ["f_Wt"], scale=-C0)
            kk_.act(f_Wi, f_S, AF.Exp, ["f_S"], ["f_Wi"], scale=C0)
            kk_.act(f_Wp, f_Sp, AF.Exp, ["f_Sp"], ["f_Wp"], scale=-C0)
            kk_.act(f_Wc, f_Se, AF.Exp, ["f_Se"], ["f_Wc"], scale=-C0)
            kk_.act(f_WC, S3[:, :, 63], AF.Exp, ["f_S"], ["f_WC"], scale=-C0)
            yield "prep"
            kk_.stt(b_at, f_kkn, -1.0, f_Wp, ALU.mult, ALU.mult, ["f_kkn", "f_Wp"], ["b_at"])
            for h_ in range(2):
                hs_ = slice(h_ * 64, (h_ + 1) * 64)
                kk_.tt(b_bt[h_][hs_, :], f_ka[hs_, :], f_Wi[hs_, :], ALU.mult, ["f_ka", "f_Wi"], ["b_bt"])
                kk_.tt(b_kt[h_][hs_, :], f_kmod[hs_, :], f_Wi[hs_, :], ALU.mult, ["f_kmod", "f_Wi"], ["b_kt"])
            kk_.tt(b_rt, f_r, f_Wt, ALU.mult, ["f_r", "f_Wt"], ["b_rt"])
            kk_.tt(b_kh, f_kmod, f_Wc, ALU.mult, ["f_kmod", "f_Wc"], ["b_kh"])
            kk_.tt(b_bh, f_ka, f_Wc, ALU.mult, ["f_ka", "f_Wc"], ["b_bh"])
            kk_.cp(b_v, f_v, ["f_v"], ["b_v"], eng="act")
            yield "prep"
            for ti, (src, sk, dst, dk) in enumerate(((b_at, "b_at", a_tok, "a_tok"), (b_kh, "b_kh", kh_tok, "kh_tok"),
                                                     (b_bh, "b_bh", bh_tok, "bh_tok"), (b_v, "b_v", None, "Vpad"))):
                pt = ptb[:, (ti % 2) * 512:(ti % 2) * 512 + 512]
                pk = ptk
                for c in range(NCK):
                    kk_.tr(pt[0:64, c * 128:(c + 1) * 128], src[:, c * 64:(c + 1) * 64], ident, [sk, "ident"], [pk])
                if ti == 0:
                    kk_.cp(dst.rearrange("p c n -> p (c n)"), pt[0:64, 0:NCK * 128], [pk], [dk], eng="dve")
                else:
                    dpad = Vpad if dst is None else dst
                    o4 = dpad.rearrange("p c (h x) -> p c h x", x=128)[:, :, :, 0:64]
                    i4 = pt[0:64, 0:NCK * 128].rearrange("p (c h k) -> p c h k", h=2, k=64)
                    kk_.cp(o4, i4, [pk], [dk], eng=("act" if ti % 2 else "dve"))
            yield "prep_done"
            def vb(buf):
                return buf.bitcast(BF16)[:, 0:512]

            units = [(h, cc) for h in range(2) for cc in range(4)]
            for u, (h, cc) in enumerate(units):
                kk_.mm(X0[0:64, us(u)], b_bt[h][:, tk(cc)], b_at[:, tk(cc)], True, True, ["b_bt", "b_at"], [K0])
                kk_.mm(X1[0:64, us(u)], b_at[:, tk(cc)], b_bt[h][:, tk(cc)], True, True, ["b_bt", "b_at"], [K1])
            for u, (h, cc) in enumerate(units):
                kk_.mm(X2[0:64, us(u)], b_kt[h][:, tk(cc)], b_at[:, tk(cc)], True, True, ["b_kt", "b_at"], [K2])
            kk_.tt(vb(mN[0]), X0[0:64, :], m_su, ALU.mult, [K0, "m_su"], ["mN0"])
            kk_.tt(vb(mM[0]), X1[0:64, :], m_sl, ALU.mult, [K1, "m_sl"], ["mM0"])
            yield "ph"
            for u, (h, cc) in enumerate(units):
                kk_.mm(X0[0:64, us(u)], b_bt[h][:, tk(cc)], b_rt[:, tk(cc)], True, True, ["b_bt", "b_rt"], [K0])
                kk_.mm(X1[0:64, us(u)], b_kt[h][:, tk(cc)], b_rt[:, tk(cc)], True, True, ["b_kt", "b_rt"], [K1])
            kk_.tt(mAK, X2[0:64, :], m_su, ALU.mult, [K2, "m_su"], ["mAK"])
            kk_.tt(vb(mP[0]), vb(mN[0]), identx, ALU.add, ["mN0", "identx"], ["mP0"])
            kk_.tt(mRB, X0[0:64, :], m_u, ALU.mult, [K0, "m_u"], ["mRB"])
            kk_.tt(mRK, X1[0:64, :], m_u, ALU.mult, [K1, "m_u"], ["mRK"])
            yield "ph"
            def p_step(i):
                a_, b_ = (i - 1) % 2, i % 2
                im = vb(mIM)
                pin = vb(mP[a_])
                for u in range(8):
                    kk_.mm(X2[0:64, us(u)], im[:, us(u)], pin[:, us(u)], True, True, ["mIM", "mP%d" % a_], [K2])
                if i == 5:
                    kk_.cp(PTb, X2[0:64, :], [K2], ["PTb"], eng="act")
                elif i in (1, 2, 3, 4):
                    kk_.cp(vb(mP[b_]), X2[0:64, :], [K2], ["mP%d" % b_], eng="act")
                else:
                    kk_.cp(mP[b_], X2[0:64, :], [K2], ["mP%d" % b_], eng="act")

            for i in range(1, 6):
                a_, b_ = (i - 1) % 2, i % 2
                nin = vb(mN[a_]) if i in (1, 3, 4, 5) else mN[a_]
                min_ = vb(mM[a_]) if i in (1, 3, 4, 5) else mM[a_]
                for u in range(8):
                    kk_.mm(X0[0:64, us(u)], nin[:, us(u)], min_[:, us(u)], True, True,
                           ["mN%d" % a_, "mM%d" % a_], [K0])
                if i < 5:
                    for u in range(8):
                        kk_.mm(X1[0:64, us(u)], min_[:, us(u)], nin[:, us(u)], True, True,
                               ["mN%d" % a_, "mM%d" % a_], [K1])
                if i > 1:
                    p_step(i - 1)
                if i < 5:
                    mo = vb(mM[b_]) if i in (2, 3, 4) else mM[b_]
                    no = vb(mN[b_]) if i in (2, 3, 4) else mN[b_]
                    kk_.cp(mo, X0[0:64, :], [K0], ["mM%d" % b_], eng="act")
                    kk_.cp(no, X1[0:64, :], [K1], ["mN%d" % b_], eng="dve")
                kk_.tt(vb(mIM), X0[0:64, :], identx, ALU.add, [K0, "identx"], ["mIM"])
                yield "ph"
            p_step(5)
            PT, PTk = PTb, "PTb"
            for u, (h, cc) in enumerate(units):
                kk_.mm(X0[0:64, us(u)], mAK[:, us(u)], Vpad[:, cc, h * 128:h * 128 + 64], True, True, ["mAK", "Vpad"], [K0])
            kk_.cp(mX, X0[0:64, :], [K0], ["mX"], eng="dve")
            for u, (h, cc) in enumerate(units):
                kk_.mm(X2[:, us(u)], a_tok[:, cc, :], PT[:, us(u)], True, True, [PTk, "a_tok"], [K2])
            for u, (h, cc) in enumerate(units):
                kk_.mm(X1[0:64, us(u)], PT[:, us(u)], mX[:, us(u)], True, True, [PTk, "mX"], [K1])
            kk_.cp(b_atp[0:64, :], X2[0:64, 0:256], [K2], ["b_atp"], eng="dve")
            kk_.cp(b_atp[64:128, :], X2[64:128, 256:512], [K2], ["b_atp"], eng="act")
            kk_.cp(mZ, X1[0:64, :], [K1], ["mZ"], eng="act")
            yield "ph"
            U3 = Upad.rearrange("p (h x) -> p h x", x=128)[:, :, 0:64]
            for cc in range(4):
                kk_.mm(X0[0:64, 0:128], b_atp[:, tk(cc)], Hb[p], True, True, ["b_atp", hbk], [K0])
                z3 = mZ.rearrange("p (h c v) -> p h c v", h=2, v=64)[:, :, cc, :]
                kk_.tt(U3, X0[0:64, 0:128].rearrange("p (h v) -> p h v", v=64), z3, ALU.add, [K0, "mZ"], ["Upad"])
                kk_.mm(X1[:, 0:64], Hb[p], b_rt[:, tk(cc)], True, False, [hbk, "b_rt"], [K1])
                for h in range(2):
                    u = h * 4 + cc
                    kk_.mm(X1[:, 0:64], Upad[:, h * 64:h * 64 + 128], mRB[:, us(u)], False, False, ["Upad", "mRB"], [K1])
                    kk_.mm(X1[:, 0:64], Vpad[:, cc, h * 64:h * 64 + 128], mRK[:, us(u)], False, h == 1,
                           ["Vpad", "mRK"], [K1])
                for h in range(2):
                    kk_.mm(X2[:, h * 64:(h + 1) * 64], bh_tok[:, cc, h * 64:h * 64 + 128], Upad[:, h * 128:h * 128 + 64], True, False,
                           ["bh_tok", "Upad"], [K2])
                    kk_.mm(X2[:, h * 64:(h + 1) * 64], kh_tok[:, cc, h * 64:h * 64 + 128], Vpad[:, cc, h * 128:h * 128 + 64], False, True,
                           ["kh_tok", "Vpad"], [K2])
                kk_.stt(Hb[p], Hf[p], f_WC[:, cc:cc + 1], X2[:, 0:128], ALU.mult, ALU.add, [hk, "f_WC", K2], [hbk])
                kk_.stt(Hf[p], Hf[p], f_WC[:, cc:cc + 1], X2[:, 0:128], ALU.mult, ALU.add, [hk, "f_WC", K2], [hk])
                kk_.cp(f_y[:, tk(cc)], X1[:, 0:64], [K1], ["f_Wp"], eng="act")
                yield "ph"
            kk_.cp(b_x, f_y, ["f_Wp"], ["b_x"], eng="dve")
            kk_.mm(X0[:, 0:W], bdm, b_x, True, True, ["bdm", "b_x"], [K0])
            kk_.tt(f_d, f_y, X0[:, 0:W], ALU.subtract, ["f_Wp", K0], ["f_d"])
            kk_.tt(b_x, f_d, f_d, ALU.mult, ["f_d"], ["b_x"])
            kk_.mm(X0[:, W:2 * W], bdm, b_x, True, True, ["bdm", "b_x"], [K0])
            kk_.act(f_t1, X0[:, W:2 * W], AF.Ln, [K0, "epsl"], ["f_t1"], bias=epsl)
            kk_.act(f_t1, f_t1, AF.Exp, ["f_t1"], ["f_t1"], scale=-0.5)
            yield "ph"
            kk_.tt(f_d, f_d, f_t1, ALU.mult, ["f_d", "f_t1"], ["f_d"])
            kk_.ts2(f_d, f_d, col(l, CLG + p), col(l, CLB + p), ALU.mult, ALU.add, ["f_d", ck], ["f_d"])
            kk_.tt(f_d, f_d, f_bonus, ALU.add, ["f_d", "f_bonus"], ["f_d"])
            kk_.act(f_t1, f_z, AF.Exp, ["f_z"], ["f_t1"], scale=-1.0)
            kk_.act(f_t1, f_t1, AF.Ln, ["f_t1", "onec"], ["f_t1"], bias=onec)
            kk_.act(f_t1, f_t1, AF.Exp, ["f_t1"], ["f_t1"], scale=-1.0)
            kk_.tt(f_d, f_d, f_z, ALU.mult, ["f_d", "f_z"], ["f_d"])
            kk_.tt(ybT[:, p, t0:t0 + W], f_d, f_t1, ALU.mult, ["f_d", "f_t1"], ["ybT%d" % p])

        def chain(st):
            for p in range(st, 8, 2):
                for tb in range(NB):
                    for tag in it_gen(p, tb, st):
                        yield tag

        lists = []
        for st in range(2):
            P._defer = []
            for _ in chain(st):
                pass
            lists.append(P._defer)
            P._defer = None
        P.merge_streams(lists)

        if DEBUG_OUT:
            P.barrier()
            k.dma("sp", dbg_big[l], big, "s_dbg", r=["yaT", "ybT"])
        stage_end(l, "S4")
        P.barrier()
        A.reset()
        B.reset()
        mrgT = A.take([128, 16, NT], BF16)
        sga = B.take([128, NT], BF16)
        sgb = B.take([128, NT], BF16)
        wa_g = B.take([128, 8, 512], BF16)
        wb_g = B.take([128, 8, 512], BF16)
        m1 = [B.take([128, 512], F32) for _ in range(2)]
        m2 = [B.take([128, 512], F32) for _ in range(2)]
        for cgp in range(4):
            va = w_bra[l].rearrange("(kt p) n -> p kt n", p=128)
            vb = w_brb[l].rearrange("(kt p) n -> p kt n", p=128)
            for q2 in range(2):
                k.dma("pool", wa_g[:, q2 * 4:(q2 + 1) * 4, :], va[:, q2 * 4:(q2 + 1) * 4, cgp * 512:(cgp + 1) * 512], "l_wag", w=["wa_g"])
                k.dma("pool", wb_g[:, q2 * 4:(q2 + 1) * 4, :], vb[:, q2 * 4:(q2 + 1) * 4, cgp * 512:(cgp + 1) * 512], "l_wbg", w=["wb_g"])
            for cb in range(4):
                c = cgp * 4 + cb
                k.dma("sp", sga, sgT[c * 128:(c + 1) * 128, :], "l_sga", r=["sgT"], w=["sga"])
                k.dma("sp", sgb, sgT[D + c * 128:D + (c + 1) * 128, :], "l_sgb", r=["sgT"], w=["sgb"])
                for tb in range(4):
                    sl = slice(tb * 512, (tb + 1) * 512)
                    for kt in range(8):
                        k.mm(pb[0], wa_g[:, kt, cb * 128:(cb + 1) * 128], yaT[:, kt, sl], kt == 0, kt == 7, ["wa_g", "yaT"], ["pb0"])
                    for kt in range(8):
                        k.mm(pb[1], wb_g[:, kt, cb * 128:(cb + 1) * 128], ybT[:, kt, sl], kt == 0, kt == 7, ["wb_g", "ybT"], ["pb1"])
                    i2 = tb % 2
                    k.tt(m1[i2], pb[0], sga[:, sl], ALU.mult, ["pb0", "sga"], ["m1_%d" % i2])
                    k.tt(m2[i2], pb[1], sgb[:, sl], ALU.mult, ["pb1", "sgb"], ["m2_%d" % i2])
                    k.tt(mrgT[:, c, sl], m1[i2], m2[i2], ALU.add, ["m1_%d" % i2, "m2_%d" % i2], ["mrgT"])

        stage_end(l, "S5a")
        P.barrier()
        B.reset()
        xt = [B.take([128, D], F32) for _ in range(2)]
        osb = B.take([128, D], F32)
        junk = B.take([128, D], F32)
        gg = B.take([128, D], F32)
        k.dma("sp", gg, ggd[0:1, :].partition_broadcast(128), "l_gg", r=["ggd"], w=["gg"])
        vo = w_out[l].rearrange("(kt p) n -> p kt n", p=128)
        for q4 in range(4):
            k.dma("pool", big[:, q4 * 4:(q4 + 1) * 4, :], vo[:, q4 * 4:(q4 + 1) * 4, :], "l_wout", w=["hT", "yaT", "ybT", "wout"])
        for tt_ in range(16):
            xs = xt[tt_ % 2]
            xk = "xt%d" % (tt_ % 2)
            k.dma("sp", xs, xsrc[tt_ * 128:(tt_ + 1) * 128, :], "l_" + xk, w=[xk])
            for n in range(4):
                for kt in range(NKT):
                    k.mm(pb[n], mrgT[:, kt, tt_ * 128:(tt_ + 1) * 128], big[:, kt, n * 512:(n + 1) * 512], kt == 0, kt == NKT - 1,
                         ["mrgT", "wout"], ["pb%d" % n])
                k.cp(osb[:, n * 512:(n + 1) * 512], pb[n], ["pb%d" % n], ["osb"], eng=("act" if n % 2 else "dve"))
            k.act(junk, osb, AF.Square, ["osb"], ["junk", "small"], accum=small[:, tt_:tt_ + 1])
            k.act(small[:, 16 + tt_:17 + tt_], small[:, tt_:tt_ + 1], AF.Sqrt, ["small", "epsr"], ["small"], bias=epsr, scale=1.0 / D)
            k.recip(small[:, 32 + tt_:33 + tt_], small[:, 16 + tt_:17 + tt_], ["small"], ["small"])
            k.stt(osb, osb, small[:, 32 + tt_:33 + tt_], gg, ALU.mult, ALU.mult, ["osb", "small", "gg"], ["osb"])
            k.tt(osb, osb, xs, ALU.add, ["osb", xk], ["osb"])
            k.dma("sp", xdst[tt_ * 128:(tt_ + 1) * 128, :], osb, "s_out", r=["osb"], w=["xmid"])
        k.memset(small[:, 0:16], 0.0, ["small"])
        P.barrier()

    try:
        for l in range(2):
            build_layer(l)
    except StopBuild:
        pass
    P.emit()
    return nc


_NC_CACHE = {}


def _pack_cols(inp, l):
    c = np.zeros((128, NCOLS), np.float32)

    def put(c0, vec):
        v = np.asarray(vec, np.float32).reshape(-1)
        n = v.shape[0] // 128
        c[:, c0:c0 + n] = v.reshape(n, 128).T
    put(CG_PRE, inp["g_pre"][l])
    put(CPSC, inp["pool_scale"][l])
    mu = np.asarray(inp["mu_shift"][l], np.float32)
    put(CMU_R, mu[0:1024])
    put(CMU_K, mu[1024:2048])
    put(CMU_V, mu[2048:3072])
    put(CMU_Z, mu[3072:4096])
    c[0:64, CMU_WA] = mu[4096:4160]
    c[0:64, CMU_A] = mu[4160:4224]
    put(CW0, inp["w0"][l])
    put(CA0, inp["a0"][l])
    if l >= 1:
        put(CMV0, inp["mv0"][l - 1])
        c[0:32, CMU_MV] = np.asarray(inp["mu_mv"][l - 1], np.float32)
    put(CKK, inp["k_k"][l])
    put(CKA, inp["k_a"][l])
    put(CRK, np.asarray(inp["r_k"][l], np.float32).reshape(-1))
    put(CLG, inp["lnx_g"][l])
    put(CLB, inp["lnx_b"][l])
    return c


def kernel(**inputs):
    inp = {k_: np.asarray(v) for k_, v in inputs.items()}
    if "nc" not in _NC_CACHE:
        _NC_CACHE["nc"] = build_program()
    nc = _NC_CACHE["nc"]
    cols = np.stack([_pack_cols(inp, 0), _pack_cols(inp, 1)], axis=0)
    shared = {
        "w_ada": np.ascontiguousarray(inp["w_ada"], np.float32),
        "b_ada": np.ascontiguousarray(inp["b_ada"], np.float32),
        "w_in": np.ascontiguousarray(inp["w_in"], np.float32),
        "w_pool": np.ascontiguousarray(inp["w_pool"], np.float32),
        "w_decay_up": np.ascontiguousarray(inp["w_decay_up"], np.float32),
        "w_aaa_up": np.ascontiguousarray(inp["w_aaa_up"], np.float32),
        "w_mv_down": np.ascontiguousarray(inp["w_mv_down"], np.float32),
        "w_mv_up": np.ascontiguousarray(inp["w_mv_up"], np.float32),
        "w_br_a": np.ascontiguousarray(inp["w_br_a"], np.float32),
        "w_br_b": np.ascontiguousarray(inp["w_br_b"], np.float32),
        "w_out": np.ascontiguousarray(inp["w_out"], np.float32),
        "g_post": np.ascontiguousarray(inp["g_post"], np.float32),
        "cols": cols,
    }
    in_maps = []
    for core in range(8):
        b = core % 4
        m = dict(shared)
        m["x"] = np.ascontiguousarray(inp["x"][b], np.float32)
        m["cT"] = np.ascontiguousarray(np.asarray(inp["c"][b], np.float32).reshape(NKT, 128).T)
        in_maps.append(m)
    res = run_bass_kernel_spmd(nc, in_maps, core_ids=list(range(8)))
    out = np.stack([np.asarray(res.results[b]["y"], np.float32) for b in range(4)], axis=0)
    return out
```
